# Optimizing a Trainium2 kernel written in Bass

```python
import math
import jax, jax.numpy as jnp
from jax import lax
import numpy as np

D_MODEL = 1024
BATCH = 16
SEQ = 4096
DEPTH = 1

CHUNK = 64
Q_BLOCK = 128
HEAD_DIM = 64
SB_HEADS = 8
SB_WIDTH = SB_HEADS * HEAD_DIM
DA_HEADS = 4
DA_V_DIM = 2 * HEAD_DIM
DA_QK_WIDTH = DA_HEADS * 2 * HEAD_DIM
DA_V_WIDTH = DA_HEADS * DA_V_DIM
N_BRANCH = 2
IN_SPLITS = (SB_WIDTH, SB_WIDTH, SB_WIDTH, DA_QK_WIDTH, DA_QK_WIDTH, DA_V_WIDTH, N_BRANCH * D_MODEL)
IN_WIDTH = sum(IN_SPLITS)
D_FF = -(-8 * D_MODEL // (3 * 256)) * 256
ROPE_THETA = 10000.0
NORM_EPS = 1e-6
SUBLN_EPS = 1e-5

kernel_name = "hybrid_stickbreak_diffattn_gated_block"


def rms_norm(x, g, eps=NORM_EPS):
    xf = x.astype(jnp.float32)
    y = xf * lax.rsqrt(jnp.mean(xf * xf, axis=-1, keepdims=True) + eps)
    return (y * g.astype(jnp.float32)).astype(x.dtype)


def rope_tables(seq_len):
    inv_freq = 1.0 / (ROPE_THETA ** (jnp.arange(0, HEAD_DIM, 2, dtype=jnp.float32) / HEAD_DIM))
    ang = jnp.arange(seq_len, dtype=jnp.float32)[:, None] * inv_freq[None, :]
    ang = jnp.concatenate([ang, ang], axis=-1)
    return jnp.cos(ang), jnp.sin(ang)


def apply_rope(x, cos, sin):
    half = HEAD_DIM // 2
    rot = jnp.concatenate([-x[..., half:], x[..., :half]], axis=-1)
    c = cos[None, :, None, :].astype(x.dtype)
    s = sin[None, :, None, :].astype(x.dtype)
    return x * c + rot * s


def stick_breaking_attention(q, k, v):
    seq_len = q.shape[2]
    scale = HEAD_DIM ** -0.5
    outs = []
    for i in range(seq_len // Q_BLOCK):
        start, end = i * Q_BLOCK, (i + 1) * Q_BLOCK
        z = jnp.einsum('bhqd,bhkd->bhqk', q[:, :, start:end], k[:, :, :end]).astype(jnp.float32) * scale
        qpos = jnp.arange(start, end)[:, None]
        kpos = jnp.arange(end)[None, :]
        mask = kpos < qpos
        log_keep = jnp.where(mask, jax.nn.log_sigmoid(-z), 0.0)
        log_between = lax.cumsum(log_keep, axis=3, reverse=True) - log_keep
        w = jnp.where(mask, jnp.exp(jax.nn.log_sigmoid(z) + log_between), 0.0)
        outs.append(jnp.einsum('bhqk,bhkd->bhqd', w.astype(v.dtype), v[:, :, :end]))
    return jnp.concatenate(outs, axis=2)


def differential_attention(q, k, v, lam):
    seq_len = q.shape[3]
    scale = HEAD_DIM ** -0.5
    outs = []
    for i in range(seq_len // Q_BLOCK):
        start, end = i * Q_BLOCK, (i + 1) * Q_BLOCK
        s = jnp.einsum('bhcqd,bhckd->bhcqk', q[:, :, :, start:end], k[:, :, :, :end]).astype(jnp.float32) * scale
        qchunk = jnp.arange(start, end)[:, None] // CHUNK
        kchunk = jnp.arange(end)[None, :] // CHUNK
        s = jnp.where(kchunk <= qchunk, s, -jnp.inf)
        p = jax.nn.softmax(s, axis=-1)
        a = p[:, :, 0] - lam * p[:, :, 1]
        outs.append(jnp.einsum('bhqk,bhkv->bhqv', a.astype(v.dtype), v[:, :, :end]))
    return jnp.concatenate(outs, axis=2)


def setup_inputs(seed: int = 0) -> dict:
    key = jax.random.key(seed)
    ks = jax.random.split(key, 16)
    nrm = jax.random.normal
    f32 = jnp.float32
    return {
        "x": nrm(ks[0], (BATCH, SEQ, D_MODEL), f32),
        "g_mix": 1.0 + 0.02 * nrm(ks[1], (DEPTH, D_MODEL), f32),
        "w_in": nrm(ks[2], (DEPTH, D_MODEL, IN_WIDTH), f32) * D_MODEL ** -0.5,
        "lambda_q1": 0.1 * nrm(ks[3], (DEPTH, HEAD_DIM), f32),
        "lambda_k1": 0.1 * nrm(ks[4], (DEPTH, HEAD_DIM), f32),
        "lambda_q2": 0.1 * nrm(ks[5], (DEPTH, HEAD_DIM), f32),
        "lambda_k2": 0.1 * nrm(ks[6], (DEPTH, HEAD_DIM), f32),
        "g_subln": 1.0 + 0.02 * nrm(ks[7], (DEPTH, DA_V_DIM), f32),
        "w_branch_sb": nrm(ks[8], (DEPTH, SB_WIDTH, D_MODEL), f32) * SB_WIDTH ** -0.5,
        "w_branch_da": nrm(ks[9], (DEPTH, DA_V_WIDTH, D_MODEL), f32) * DA_V_WIDTH ** -0.5,
        "w_out": nrm(ks[10], (DEPTH, D_MODEL, D_MODEL), f32) * D_MODEL ** -0.5,
        "g_ffn": 1.0 + 0.02 * nrm(ks[11], (DEPTH, D_MODEL), f32),
        "w_ffn_gate": nrm(ks[12], (DEPTH, D_MODEL, D_FF), f32) * D_MODEL ** -0.5,
        "w_ffn_up": nrm(ks[13], (DEPTH, D_MODEL, D_FF), f32) * D_MODEL ** -0.5,
        "w_ffn_down": nrm(ks[14], (DEPTH, D_FF, D_MODEL), f32) * D_FF ** -0.5,
        "g_final": 1.0 + 0.02 * nrm(ks[15], (D_MODEL,), f32),
    }


def reference(x, g_mix, w_in, lambda_q1, lambda_k1, lambda_q2, lambda_k2, g_subln,
              w_branch_sb, w_branch_da, w_out, g_ffn, w_ffn_gate, w_ffn_up, w_ffn_down, g_final):
    B, S, _ = x.shape
    cos, sin = rope_tables(S)
    offsets = [int(o) for o in np.cumsum(IN_SPLITS)[:-1]]
    for l in range(DEPTH):
        lambda_init = 0.8 - 0.6 * math.exp(-0.3 * l)
        h = rms_norm(x, g_mix[l])
        proj = h @ w_in[l]
        qa, ka, va, qd, kd, vd, gates = jnp.split(proj, offsets, axis=-1)
        qa = qa.reshape(B, S, SB_HEADS, HEAD_DIM).transpose(0, 2, 1, 3)
        ka = ka.reshape(B, S, SB_HEADS, HEAD_DIM).transpose(0, 2, 1, 3)
        va = va.reshape(B, S, SB_HEADS, HEAD_DIM).transpose(0, 2, 1, 3)
        o_sb = stick_breaking_attention(qa, ka, va).transpose(0, 2, 1, 3).reshape(B, S, SB_WIDTH)
        qd = apply_rope(qd.reshape(B, S, 2 * DA_HEADS, HEAD_DIM), cos, sin)
        kd = apply_rope(kd.reshape(B, S, 2 * DA_HEADS, HEAD_DIM), cos, sin)
        qd = qd.reshape(B, S, DA_HEADS, 2, HEAD_DIM).transpose(0, 2, 3, 1, 4)
        kd = kd.reshape(B, S, DA_HEADS, 2, HEAD_DIM).transpose(0, 2, 3, 1, 4)
        vd = vd.reshape(B, S, DA_HEADS, DA_V_DIM).transpose(0, 2, 1, 3)
        lam = (jnp.exp(jnp.sum(lambda_q1[l].astype(jnp.float32) * lambda_k1[l].astype(jnp.float32)))
               - jnp.exp(jnp.sum(lambda_q2[l].astype(jnp.float32) * lambda_k2[l].astype(jnp.float32)))
               + lambda_init)
        o_da = differential_attention(qd, kd, vd, lam)
        o_da = rms_norm(o_da, g_subln[l], SUBLN_EPS) * (1.0 - lambda_init)
        o_da = o_da.transpose(0, 2, 1, 3).reshape(B, S, DA_V_WIDTH)
        gate_sb, gate_da = jnp.split(gates, N_BRANCH, axis=-1)
        mixed = (jax.nn.sigmoid(gate_sb) * (o_sb @ w_branch_sb[l])
                 + jax.nn.sigmoid(gate_da) * (o_da @ w_branch_da[l]))
        x = x + mixed @ w_out[l]
        h = rms_norm(x, g_ffn[l])
        x = x + (jax.nn.silu(h @ w_ffn_gate[l]) * (h @ w_ffn_up[l])) @ w_ffn_down[l]
    return rms_norm(x, g_final)
```

```python
import math
from contextlib import ExitStack

import numpy as np
import concourse.bass as bass
import concourse.mybir as mybir
from concourse.bass_utils import run_bass_kernel_spmd

F32 = mybir.dt.float32
BF = mybir.dt.bfloat16
AF = mybir.ActivationFunctionType
ALU = mybir.AluOpType
AX = mybir.AxisListType

D = 1024
DC = 8
DFF = 2816
NFC = 22
INW = 5120
NCORES = 8
SEQ = 4096
BATCH = 16
NORM_EPS = 1e-6
SUBLN_EPS = 1e-5
LAMBDA_INIT = 0.8 - 0.6 * math.exp(-0.3 * 0)

SEM_CAP = 30000
STRICT_SAME_ENGINE = True
ENGS = ("pe", "act", "dve", "pool", "sp")


class Buf:
    __slots__ = ("name", "lw", "rd")

    def __init__(self, name):
        self.name = name
        self.lw = None
        self.rd = []


class Op:
    __slots__ = ("eng", "fn", "pos", "waits", "marked", "sem", "val", "dma_key", "dma_val", "dma_sem")

    def __init__(self, eng, fn):
        self.eng = eng
        self.fn = fn
        self.pos = -1
        self.waits = []
        self.marked = False
        self.sem = None
        self.val = 0
        self.dma_key = None
        self.dma_val = 0
        self.dma_sem = None


class Sched:
    def __init__(self, nc):
        self.nc = nc
        self.streams = {e: [] for e in ENGS}
        self.seen = {e: {p: -1 for p in ENGS} for e in ENGS}
        self.seen_dma = {e: {} for e in ENGS}
        self.dma_cnt = {}
        self.final_dma = []

    def add(self, eng, fn, reads=(), writes=(), dma_key=None):
        op = Op(eng, fn)
        op.pos = len(self.streams[eng])
        is_dma = dma_key is not None
        if is_dma:
            ep, cnt = self.dma_cnt.get(dma_key, (0, 0))
            if cnt + 16 > SEM_CAP:
                ep, cnt = ep + 1, 0
            cnt += 16
            self.dma_cnt[dma_key] = (ep, cnt)
            op.dma_key = (dma_key, ep)
            op.dma_val = cnt
        deps = {}
        for b in reads:
            if b.lw is not None:
                deps[id(b.lw)] = (b.lw, True)
        for b in writes:
            if b.lw is not None and id(b.lw) not in deps:
                deps[id(b.lw)] = (b.lw, False)
            for r in b.rd:
                if id(r) not in deps:
                    deps[id(r)] = (r, False)
        best = {}
        for p, raw in deps.values():
            if p is op:
                continue
            if p.dma_key is not None:
                k = p.dma_key
                if self.seen_dma[eng].get(k, 0) >= p.dma_val:
                    continue
                self.seen_dma[eng][k] = p.dma_val
                op.waits.append(p)
                continue
            if p.eng == eng and not is_dma:
                if eng == "pe" or (not raw and not STRICT_SAME_ENGINE):
                    continue
            if p.eng not in best or best[p.eng].pos < p.pos:
                best[p.eng] = p
        for pe_, p in best.items():
            if self.seen[eng][pe_] >= p.pos:
                continue
            self.seen[eng][pe_] = p.pos
            p.marked = True
            op.waits.append(p)
        for b in reads:
            b.rd.append(op)
        for b in writes:
            b.lw = op
            b.rd = []
        self.streams[eng].append(op)
        return op

    def barrier(self):
        lasts = {}
        for e in ENGS:
            if e == "sp":
                continue
            for o in reversed(self.streams[e]):
                if o.dma_key is None and o.fn is not None:
                    lasts[e] = o
                    break
        dlast = {}
        for e in ENGS:
            for o in self.streams[e]:
                if o.dma_key is not None:
                    dlast[o.dma_key] = o
        for e in ENGS:
            op = Op(e, None)
            op.pos = len(self.streams[e])
            for pe_, p in lasts.items():
                if pe_ == e or self.seen[e][pe_] >= p.pos:
                    continue
                self.seen[e][pe_] = p.pos
                p.marked = True
                op.waits.append(p)
            for k, p in dlast.items():
                if self.seen_dma[e].get(k, 0) >= p.dma_val:
                    continue
                self.seen_dma[e][k] = p.dma_val
                op.waits.append(p)
            if op.waits:
                self.streams[e].append(op)

    def emit(self, es):
        nc = self.nc
        for e in ENGS:
            cnt = 0
            sem = None
            n = 0
            for o in self.streams[e]:
                if o.marked:
                    if sem is None or cnt >= SEM_CAP:
                        sem = es.enter_context(nc.semaphore(f"s_{e}_{n}"))
                        n += 1
                        cnt = 0
                    cnt += 1
                    o.sem = sem
                    o.val = cnt
        dsem = {}
        for e in ENGS:
            for o in self.streams[e]:
                if o.dma_key is not None:
                    if o.dma_key not in dsem:
                        dsem[o.dma_key] = es.enter_context(nc.semaphore(f"d_{len(dsem)}"))
                    o.dma_sem = dsem[o.dma_key]
        streams = self.streams
        final_dma = self.final_dma

        def run(eng_name, eng):
            for o in streams[eng_name]:
                for p in o.waits:
                    if p.dma_key is not None:
                        eng.wait_ge(p.dma_sem, p.dma_val)
                    else:
                        eng.wait_ge(p.sem, p.val)
                if o.fn is None:
                    continue
                ins = o.fn(eng)
                if o.dma_key is not None:
                    ins.then_inc(o.dma_sem, 16)
                elif o.marked:
                    ins.then_inc(o.sem, 1)
            if eng_name == "sp":
                for o in final_dma:
                    eng.wait_ge(o.dma_sem, o.dma_val)

        block = es.enter_context(nc.Block())

        @block.tensor
        def _(eng):
            run("pe", eng)

        @block.scalar
        def _(eng):
            run("act", eng)

        @block.vector
        def _(eng):
            run("dve", eng)

        @block.gpsimd
        def _(eng):
            run("pool", eng)

        @block.sync
        def _(eng):
            run("sp", eng)


class Arena:
    def __init__(self, ap, nbytes):
        self.ap = ap
        self.nbytes = nbytes
        self.off = 0
        self.peak = 0

    def _take(self, nb):
        nb = (nb + 63) // 64 * 64
        o = self.off
        self.off += nb
        self.peak = max(self.peak, self.off)
        assert self.off <= self.nbytes, f"arena overflow {self.off} > {self.nbytes}"
        return o

    def bf(self, n):
        o = self._take(2 * n)
        return self.ap[:, o // 2:o // 2 + n]

    def f32(self, n):
        o = self._take(4 * n)
        return self.ap[:, o // 2:o // 2 + 2 * n].bitcast(F32)


class KB:
    def __init__(self, S, NB, arena_bytes=212480):
        self.S = S
        self.NB = NB
        self.NT = S // 512
        self.arena_bytes = arena_bytes
        self.nbuf = 0

    def buf(self, name):
        self.nbuf += 1
        return Buf(f"{name}#{self.nbuf}")

    def mm(self, out, lhsT, rhs, start, stop, reads, writes, skip=False):
        self.s.add("pe", lambda e: e.matmul(out, lhsT=lhsT, rhs=rhs, start=start, stop=stop,
                                            skip_group_check=skip), reads, writes)

    def tr(self, out, in_, reads, writes):
        ident = self.IDENT
        self.s.add("pe", lambda e: e.transpose(out, in_, ident), list(reads) + [self.bCM], writes)

    def act(self, out, in_, func, reads, writes, bias=None, scale=1.0, accum=None):
        def fn(e):
            kw = {}
            if bias is not None:
                kw["bias"] = bias
            if accum is not None:
                kw["accum_out"] = accum
            return e.activation(out=out, in_=in_, func=func, scale=scale, **kw)
        self.s.add("act", fn, reads, writes)

    def dve(self, fn, reads, writes):
        self.s.add("dve", fn, reads, writes)

    def pool(self, fn, reads, writes):
        self.s.add("pool", fn, reads, writes)

    def dma(self, q, out, in_, reads, writes, key):
        return self.s.add(q, lambda e: e.dma_start(out=out, in_=in_), reads, writes, dma_key=key)

    def bank(self, i):
        return self.PS[i // 2][:, (i % 2) * 512:(i % 2 + 1) * 512]

    def build(self):
        S, NB = self.S, self.NB
        nc = bass.Bass("TRN2", target_bir_lowering=False)
        self.nc = nc
        dt = nc.dram_tensor
        I = "ExternalInput"
        self.x = dt("x", [NB, S, D], F32, kind=I).ap()
        self.w_in = dt("w_in", [D, INW], F32, kind=I).ap()
        self.wbs = dt("wbs", [512, D], F32, kind=I).ap()
        self.wbd = dt("wbd", [512, D], F32, kind=I).ap()
        self.wo = dt("wo", [D, D], F32, kind=I).ap()
        self.wg = dt("wg", [D, DFF], F32, kind=I).ap()
        self.wu = dt("wu", [D, DFF], F32, kind=I).ap()
        self.wd = dt("wd", [DFF, D], F32, kind=I).ap()
        self.gmix = dt("gmix", [1, D], F32, kind=I).ap()
        self.gffn = dt("gffn", [1, D], F32, kind=I).ap()
        self.gfin = dt("gfin", [1, D], F32, kind=I).ap()
        self.gsub = dt("gsub", [1, 128], F32, kind=I).ap()
        self.lamv = dt("lamv", [4, 64], F32, kind=I).ap()
        self.cmat = dt("cmat", [128, 9 * 128], F32, kind=I).ap()
        self.cosT = dt("cosT", [128, S], F32, kind=I).ap()
        self.sinT = dt("sinT", [128, S], F32, kind=I).ap()
        self.out = dt("out", [NB, S, D], F32, kind="ExternalOutput").ap()
        self.WA = dt("WA", [8, 4, 128, 1024], BF, kind="Internal").ap()
        self.WG = dt("WG", [8, 128, 3072], BF, kind="Internal").ap()
        self.WO = dt("WO", [2, 128, 4096], BF, kind="Internal").ap()
        self.WF = dt("WF", [NFC, 2, 128, 1024], BF, kind="Internal").ap()
        self.WD = dt("WD", [2, 128, NFC * 512], BF, kind="Internal").ap()

        with ExitStack() as es:
            AR = es.enter_context(nc.sbuf_tensor("AR", [128, self.arena_bytes // 2], BF))
            self.A = Arena(AR, self.arena_bytes)
            self.PS = [es.enter_context(nc.psum_tensor(f"PS{i}", [128, 1024], F32)) for i in range(4)]
            self.pb = [self.buf(f"pb{i}") for i in range(8)]
            self.s = Sched(nc)
            self.setup()
            self.s.barrier()
            self.phase_w()
            for b in range(NB):
                self.pass_a(b)
                self.pass_b(b)
                self.pass_c(b)
            self.s.emit(es)
        return nc

    def setup(self):
        A, S = self.A, self.S
        self.CM = A.bf(9 * 128)
        self.bCM = self.buf("CM")
        CM = self.CM
        self.IDENT = CM[:, 0:128]
        self.NEGTRI8 = CM[:, 128:256]
        self.NEGONES8 = CM[:, 256:384]
        self.ONES = CM[:, 384:512]
        self.ONESDIV = CM[:, 512:640]
        self.SBMASK = CM[:, 640:768]
        self.DAMASK = CM[:, 768:896]
        self.PX = CM[:, 896:1024]
        self.PY = CM[:, 1024:1152]
        self.dma("pool", CM, self.cmat, [], [self.bCM], "CM")
        self.SM = A.f32(32)
        self.bSM = self.buf("SM")
        SM = self.SM
        self.M05 = SM[:, 0:1]
        self.GSUB = SM[:, 1:2]
        self.GS = SM[:, 2:3]
        self.NEGLAM = SM[:, 3:4]
        self.bLAMV = self.buf("LAMV")
        self.ST = A.f32(32)
        self.bST = [self.buf(f"ST{i}") for i in range(8)]
        self.stn = 0
        self.OT = A.bf(8 * S).rearrange("p (c s) -> p c s", c=8)
        self.bOT = [self.buf(f"OT{t}") for t in range(self.NT)]
        self.persist_mark = A.off
        self.LAMV = A.f32(4 * 64)
        PR = A.f32(128)
        self.pool(lambda e: e.memset(self.M05, -0.5), [], [self.bSM])
        self.dma("sp", self.GSUB, self.gsub.rearrange("o p -> p o"), [], [self.bSM], "SMg")
        self.dma("sp", self.LAMV, self.lamv.rearrange("a b -> (a b)").partition_broadcast(128),
                 [], [self.bLAMV], "LAMV")
        LV = self.LAMV.rearrange("p (a b) -> p a b", a=4)
        bPR = self.buf("PR")
        LS = SM[:, 8:10]
        EL = SM[:, 10:12]
        self.dve(lambda e: e.tensor_tensor(out=PR[:, 0:64], in0=LV[:, 0, :], in1=LV[:, 1, :], op=ALU.mult),
                 [self.bLAMV], [bPR])
        self.dve(lambda e: e.tensor_tensor(out=PR[:, 64:128], in0=LV[:, 2, :], in1=LV[:, 3, :], op=ALU.mult),
                 [self.bLAMV], [bPR])
        self.dve(lambda e: e.reduce_sum(out=LS, in_=PR.rearrange("p (a b) -> p a b", a=2), axis=AX.X),
                 [bPR], [self.bSM])
        self.act(EL, LS, AF.Exp, [self.bSM], [self.bSM])
        self.dve(lambda e: e.tensor_tensor(out=SM[:, 12:13], in0=EL[:, 1:2], in1=EL[:, 0:1], op=ALU.subtract),
                 [self.bSM], [self.bSM])
        self.dve(lambda e: e.tensor_scalar(out=self.NEGLAM, in0=SM[:, 12:13], scalar1=-LAMBDA_INIT, scalar2=None,
                                           op0=ALU.add), [self.bSM], [self.bSM])
        self.dve(lambda e: e.tensor_scalar(out=self.GS, in0=self.GSUB, scalar1=1.0 - LAMBDA_INIT, scalar2=None,
                                           op0=ALU.mult), [self.bSM], [self.bSM])
        self.LAMCOL = SM[:, 4:5]
        self.pool(lambda e: e.memset(SM[0:64, 4:5], 1.0), [], [self.bSM])
        self.dve(lambda e: e.tensor_copy(out=SM[64:128, 4:5], in_=SM[64:128, 3:4]), [self.bSM], [self.bSM])

    def phase_w(self):
        A = self.A
        A.off = self.persist_mark
        Fs = [A.f32(4096) for _ in range(3)]
        bF = [self.buf(f"F{i}") for i in range(3)]
        Hs = [A.bf(4096) for _ in range(3)]
        bH = [self.buf(f"H{i}") for i in range(3)]
        self.wconv_n = 0
        self.wconv_h = 0
        self.bW = {}

        def nextF():
            i = self.wconv_n % 3
            self.wconv_n += 1
            return Fs[i], bF[i]

        def nextH():
            i = self.wconv_h % 3
            self.wconv_h += 1
            return Hs[i], bH[i]

        def wbuf(key):
            b = self.buf(f"W{key}")
            self.bW[key] = b
            return b

        w_in_v = self.w_in.rearrange("(c p) n -> p c n", p=128)
        for g in range(6):
            F, bf_ = nextF()
            self.dma("sp", F.rearrange("p (c n) -> p c n", c=8), w_in_v[:, :, g * 512:(g + 1) * 512], [], [bf_],
                     ("ld", bf_.name))
            H, bh = nextH()
            self.dve(lambda e, H=H, F=F: e.tensor_copy(out=H.rearrange("p (j c n) -> p j c n", j=4, c=8),
                                                      in_=F.rearrange("p (c j n) -> p j c n", c=8, j=4)),
                     [bf_], [bh])
            self.dma("act", self.WA[g].rearrange("j p m -> p j m"), H.rearrange("p (j m) -> p j m", j=4),
                     [bh], [wbuf(("WA", g))], ("wst", "WA", g))
            if g in (3, 4):
                H2, bh2 = nextH()
                for j in range(4):
                    src = F.rearrange("p (c n) -> p c n", c=8)[:, :, j * 128:(j + 1) * 128] \
                        .rearrange("p c (m h r) -> p c m h r", m=2, h=2)
                    dst = H2[:, j * 1024:(j + 1) * 1024].rearrange("p (c m h r) -> p c m h r", c=8, m=2, h=2)
                    self.dve(lambda e, dst=dst, src=src: e.tensor_scalar(out=dst[:, :, :, 0, :], in0=src[:, :, :, 1, :],
                                                                        scalar1=-1.0, scalar2=None, op0=ALU.mult),
                             [bf_], [bh2])
                    self.dve(lambda e, dst=dst, src=src: e.tensor_copy(out=dst[:, :, :, 1, :], in_=src[:, :, :, 0, :]),
                             [bf_], [bh2])
                gg = 6 if g == 3 else 7
                self.dma("act", self.WA[gg].rearrange("j p m -> p j m"), H2.rearrange("p (j m) -> p j m", j=4),
                         [bh2], [wbuf(("WA", gg))], ("wst", "WA", gg))
        self.wd_groups = [(0, 8), (8, 16), (16, 22)]
        self.bWC = self.buf("WC")
        self.s.barrier()

    def alloc_common(self, nslots, with_xb=True, extra_g=False):
        A = self.A
        if with_xb:
            self.XB = [A.f32(D) for _ in range(2)]
            self.bXB = [self.buf(f"XB{i}") for i in range(2)]
        self.HB = [A.bf(D) for _ in range(2)]
        self.bHB = [self.buf(f"HB{i}") for i in range(2)]
        self.HT = A.bf(8 * 512).rearrange("p (c s) -> p c s", c=8)
        self.bHT = self.buf("HT")
        self.WS = [A.bf(4096) for _ in range(nslots)]
        self.bWS = [self.buf(f"WS{i}") for i in range(nslots)]
        self.wsn = 0
        self.nhb = 0
        self.bG = self.buf("G")
        self.GMIX = A.f32(D)
        gl = [(self.GMIX, self.gmix)]
        if extra_g:
            self.GFFN = A.f32(D)
            self.GFIN = A.f32(D)
            gl += [(self.GFFN, self.gffn), (self.GFIN, self.gfin)]
        for gt, gs in gl:
            self.dma("sp", gt, gs.rearrange("o d -> (o d)").partition_broadcast(128), [], [self.bG], "G")

    def wslot(self):
        i = self.wsn % len(self.WS)
        self.wsn += 1
        return self.WS[i], self.bWS[i]

    def rstd(self, x_ap, xbufs, junk, bjunk):
        g = self.stn % 8
        self.stn += 1
        st = self.ST[:, g * 4:g * 4 + 4]
        bst = self.bST[g]
        self.act(junk, x_ap, AF.Square, xbufs, [bjunk, bst], accum=st[:, 0:1])
        self.pool(lambda e: e.tensor_scalar(out=st[:, 1:2], in0=st[:, 0:1], scalar1=1.0 / D, scalar2=NORM_EPS,
                                            op0=ALU.mult, op1=ALU.add), [bst], [bst])
        self.pool(lambda e: e.tensor_tensor(out=st[:, 2:3], in0=st[:, 1:2], in1=self.M05, op=ALU.pow),
                  [bst, self.bSM], [bst])
        return st[:, 2:3], bst

    def next_hb(self):
        i = self.nhb % 2
        self.nhb += 1
        return self.HB[i], self.bHB[i]

    def norm_to_hT(self, x_ap, xbufs, gtile, tb, tpbank):
        hb, bhb = self.next_hb()
        r, bst = self.rstd(x_ap, xbufs, hb, bhb)
        self.dve(lambda e: e.scalar_tensor_tensor(out=hb, in0=x_ap, scalar=r, in1=gtile, op0=ALU.mult, op1=ALU.mult),
                 list(xbufs) + [bst, self.bG], [bhb])
        tp = self.PS[tpbank // 2][:, (tpbank % 2) * 512:(tpbank % 2 + 1) * 512].bitcast(BF)
        for c in range(8):
            self.tr(tp[:, c * 128:(c + 1) * 128], hb[:, c * 128:(c + 1) * 128], [bhb], [self.pb[tpbank]])
        ht = self.HT
        self.dve(lambda e: e.tensor_copy(out=ht[:, :, tb * 128:(tb + 1) * 128],
                                         in_=tp.rearrange("p (c n) -> p c n", c=8)),
                 [self.pb[tpbank]], [self.bHT])

    def load_x_block(self, b, blk):
        i = blk % 2
        self.dma("pool", self.XB[i], self.x[b, blk * 128:(blk + 1) * 128, :], [], [self.bXB[i]], ("ld", self.bXB[i].name))
        return self.XB[i], self.bXB[i]

    def tile_norm_T(self, b, t, gtile):
        for tb in range(4):
            xb, bxb = self.load_x_block(b, t * 4 + tb)
            self.norm_to_hT(xb, [bxb], gtile, tb, 6 + (tb % 2))

    def load_wa(self, g):
        ws, bws = self.wslot()
        self.dma("sp", ws.rearrange("p (j m) -> p j m", j=4), self.WA[g].rearrange("j p m -> p j m"),
                 [self.bW[("WA", g)]], [bws], ("ld", bws.name))
        return ws.rearrange("p (j m) -> p j m", j=4), bws

    def proj_fm(self, wsv, bws, c, bank_i):
        bk = self.bank(bank_i)
        for dc in range(8):
            self.mm(bk, wsv[:, c, dc * 128:(dc + 1) * 128], self.HT[:, dc, :], dc == 0, dc == 7,
                    [bws, self.bHT], [self.pb[bank_i]])
        return bk

    def proj_v(self, g, t, Vt, bV):
        wsv, bws = self.load_wa(g)
        for tb in range(4):
            bi = 4 + tb
            bk = self.bank(bi)
            for j in range(4):
                for dc in range(8):
                    self.mm(bk[:, j * 128:(j + 1) * 128], self.HT[:, dc, tb * 128:(tb + 1) * 128],
                            wsv[:, j, dc * 128:(dc + 1) * 128], dc == 0, dc == 7, [bws, self.bHT], [self.pb[bi]])
            self.dve(lambda e, bk=bk, tb=tb: e.tensor_copy(out=Vt[:, t * 4 + tb, :], in_=bk), [self.pb[bi]], [bV])

    def prep_items(self, b, t, specs, bankfn, vspec=None, pre=()):
        items = []
        st = {}

        def LD(tb):
            i = (t * 4 + tb) % 2
            self.dma("sp", self.XB[i], self.x[b, (t * 4 + tb) * 128:(t * 4 + tb + 1) * 128, :], [], [self.bXB[i]],
                     ("ld", self.bXB[i].name))

        def SQ(tb):
            i = (t * 4 + tb) % 2
            hb, bhb = self.next_hb()
            st[("hb", tb)] = (hb, bhb)
            st[("r", tb)] = self.rstd(self.XB[i], [self.bXB[i]], hb, bhb)

        def NRM(tb):
            i = (t * 4 + tb) % 2
            hb, bhb = st[("hb", tb)]
            r, bst = st[("r", tb)]
            xb = self.XB[i]
            gm = self.GMIX
            self.dve(lambda e: e.scalar_tensor_tensor(out=hb, in0=xb, scalar=r, in1=gm, op0=ALU.mult,
                                                      op1=ALU.mult), [self.bXB[i], bst, self.bG], [bhb])

        def TR(tb):
            hb, bhb = st[("hb", tb)]
            bi = bankfn()
            tp = self.bank(bi).bitcast(BF)
            for c in range(8):
                self.tr(tp[:, c * 128:(c + 1) * 128], hb[:, c * 128:(c + 1) * 128], [bhb], [self.pb[bi]])
            ht = self.HT
            self.dve(lambda e: e.tensor_copy(out=ht[:, :, tb * 128:(tb + 1) * 128],
                                             in_=tp.rearrange("p (c n) -> p c n", c=8)), [self.pb[bi]], [self.bHT])

        items += list(pre)
        items += [lambda: LD(0), lambda: LD(1), lambda: SQ(0), lambda: SQ(1), lambda: NRM(0), lambda: LD(2),
                  lambda: TR(0), lambda: NRM(1), lambda: LD(3), lambda: SQ(2), lambda: TR(1), lambda: NRM(2),
                  lambda: SQ(3), lambda: TR(2), lambda: NRM(3), lambda: TR(3)]
        for (groups, evac_fn) in specs:
            def LW(groups=groups):
                st[("w", tuple(groups))] = [self.load_wa(g) for g in groups]
            items.append(LW)
            for c in range(4):
                for gi in range(len(groups)):
                    def PJ(c=c, gi=gi, groups=groups, evac_fn=evac_fn):
                        wsv, bws = st[("w", tuple(groups))][gi]
                        bi = bankfn()
                        self.proj_fm(wsv, bws, c, bi)
                        evac_fn(c, gi, bi)
                    items.append(PJ)
        if vspec is not None:
            g, Vt, bV = vspec

            def LWV():
                st["wv"] = self.load_wa(g)
            items.append(LWV)
            for tb in range(4):
                def PV(tb=tb):
                    wsv, bws = st["wv"]
                    bi = bankfn()
                    bk = self.bank(bi)
                    for j in range(4):
                        for dc in range(8):
                            self.mm(bk[:, j * 128:(j + 1) * 128], self.HT[:, dc, tb * 128:(tb + 1) * 128],
                                    wsv[:, j, dc * 128:(dc + 1) * 128], dc == 0, dc == 7, [bws, self.bHT],
                                    [self.pb[bi]])
                    vdst = Vt[:, t * 4 + tb, :]
                    self.dve(lambda e: e.tensor_copy(out=vdst, in_=bk), [self.pb[bi]], [bV])
                items.append(PV)
        return items

    def wc_pieces(self):
        A = self.A
        F = A.f32(1024)
        H = A.bf(1024)
        bF, bH = self.buf("WCF"), self.buf("WCH")
        out = []

        def piece(src_ap, fview, conv, dst_ap, hview):
            def ld():
                self.dma("sp", fview(F), src_ap, [], [bF], ("ld", "WCF"))

            def cv():
                self.dve(conv, [bF], [bH])

            def st():
                self.dma("sp", dst_ap, hview(H), [bH], [self.bWC], ("wstc",))
            out.extend([ld, None, cv, st])

        straight = lambda e: e.tensor_copy(out=H, in_=F)
        w_in_v = self.w_in.rearrange("(c p) n -> p c n", p=128)
        v8 = lambda T: T.rearrange("p (c n) -> p c n", c=8)
        ident_h = lambda T: T
        for j in range(8):
            for k in range(2):
                col0 = 3072 + 1024 * k + 128 * j
                piece(w_in_v[:, :, col0:col0 + 128], v8, straight, self.WG[j][:, 1024 * k:1024 * (k + 1)], ident_h)
        for k, wsrc in enumerate((self.wbs, self.wbd)):
            wv = wsrc.rearrange("(c p) n -> p c n", p=128)
            for q in range(4):
                conv = lambda e: e.tensor_copy(out=H.rearrange("p (j c n) -> p j c n", j=2, c=4),
                                               in_=F.rearrange("p (c j n) -> p j c n", c=4, j=2))
                off = 2048 + 512 * k
                piece(wv[:, :, 256 * q:256 * (q + 1)], lambda T: T.rearrange("p (c n) -> p c n", c=4), conv,
                      self.WG[2 * q:2 * q + 2, :, off:off + 512].rearrange("j p m -> p j m"),
                      lambda T: T.rearrange("p (j m) -> p j m", j=2))
        wo_v = self.wo.rearrange("(c p) n -> p c n", p=128)
        v2 = lambda T: T.rearrange("p (c n) -> p c n", c=2)
        for ch in range(2):
            for q in range(4):
                piece(wo_v[:, 2 * q:2 * q + 2, ch * 512:(ch + 1) * 512], v2, straight,
                      self.WO[ch][:, 2 * q * 512:(2 * q + 2) * 512], ident_h)
        for f in range(NFC):
            for k, wsrc in enumerate((self.wg, self.wu)):
                wv = wsrc.rearrange("(c p) n -> p c n", p=128)
                piece(wv[:, :, 128 * f:128 * (f + 1)], v8, straight, self.WF[f, k], ident_h)
        wd_v = self.wd.rearrange("(f p) n -> p f n", p=128)
        for ch in range(2):
            for f in range(0, NFC, 2):
                piece(wd_v[:, f:f + 2, ch * 512:(ch + 1) * 512], v2, straight, self.WD[ch][:, f * 512:(f + 2) * 512],
                      ident_h)
        return out

    def pass_a(self, b):
        A, S, NT = self.A, self.S, self.NT
        A.off = self.persist_mark
        KT = A.bf(4 * S).rearrange("p (c s) -> p c s", c=4)
        Vt = A.bf(S * 4).rearrange("p (k n) -> p k n", n=512)
        bKT = [self.buf(f"KT{t}") for t in range(NT)]
        bV = [self.buf(f"V{t}") for t in range(NT)]
        self.alloc_common(2)
        QTs = [A.bf(4 * 512).rearrange("p (c s) -> p c s", c=4) for _ in range(2)]
        bQTs = [self.buf(f"QT{i}") for i in range(2)]
        Es = [A.f32(1024).rearrange("p (h s) -> p h s", h=2) for _ in range(2)]
        bEs = [self.buf(f"E{i}") for i in range(2)]
        SP = [A.bf(1024).rearrange("p (h s) -> p h s", h=2) for _ in range(2)]
        bSP = [self.buf(f"SP{i}") for i in range(2)]
        WT = [A.bf(1024).rearrange("p (h s) -> p h s", h=2) for _ in range(2)]
        bWT = [self.buf(f"WT{i}") for i in range(2)]
        R = [A.bf(1024).rearrange("p (h s) -> p h s", h=2) for _ in range(2)]
        bR = [self.buf(f"R{i}") for i in range(2)]
        wcp = self.wc_pieces() if b == 0 else []
        wcp_pos = 0
        fg = {"n": 0}

        def fg_bank():
            fg["n"] += 1
            return 4 + fg["n"] % 4

        def mk_items(t, bankfn):
            qt, bqt = QTs[t % 2], bQTs[t % 2]

            def evq(c, gi, bi):
                bk = self.bank(bi)
                self.dve(lambda e: e.tensor_copy(out=qt[:, c, :], in_=bk), [self.pb[bi]], [bqt])

            def evk(c, gi, bi):
                bk = self.bank(bi)
                dst = KT[:, c, t * 512:(t + 1) * 512]
                self.dve(lambda e: e.tensor_copy(out=dst, in_=bk), [self.pb[bi]], [bKT[t]])

            return self.prep_items(b, t, [([0], evq), ([1], evk)], bankfn, (2, Vt, bV[t]))

        for it in mk_items(0, fg_bank):
            it()
        for t in range(NT):
            QT, bQT = QTs[t % 2], bQTs[t % 2]
            bg = mk_items(t + 1, lambda: 7) if t + 1 < NT else []
            bg_pos = 0
            nkb = 4 * t + 4
            steps = [(c, kb) for c in range(4) for kb in range(nkb - 1, -1, -1)]
            n = len(steps)

            def info(i):
                c, kb = steps[i]
                j = kb - 4 * t
                c0 = j * 128 if j >= 0 else 0
                return c, kb, j >= 0, c0, kb == nkb - 1, kb == 0

            def zzv(i):
                return self.PS[i % 3].rearrange("p (h s) -> p h s", h=2)

            def zzb(i):
                return [self.pb[2 * (i % 3)], self.pb[2 * (i % 3) + 1]]

            def S_(i):
                c, kb, dg, c0, first, last = info(i)
                zz = zzv(i)
                for h in range(2):
                    self.mm(zz[:, h, c0:512], KT[64 * h:64 * h + 64, c, kb * 128:(kb + 1) * 128],
                            QT[64 * h:64 * h + 64, c, c0:512], True, True, [bKT[kb // 4], bQT],
                            [self.pb[2 * (i % 3) + h]])
                if dg:
                    for h in range(2):
                        self.mm(zz[:, h, c0:c0 + 128], self.IDENT, self.SBMASK, False, True, [self.bCM],
                                [self.pb[2 * (i % 3) + h]], skip=True)

            def zero_left(buf_ap, bbuf, c0):
                if c0 > 0:
                    self.pool(lambda e: e.memset(buf_ap[:, :, 0:c0], 0.0), [], [bbuf])

            S_(0)
            for i in range(n + 2):
                if i < n:
                    c, kb, dg, c0, first, last = info(i)
                    zz = zzv(i)
                    E, bE = Es[i % 2], bEs[i % 2]
                    self.act(E[:, :, c0:512], zz[:, :, c0:512], AF.Exp, zzb(i), [bE], scale=0.125)
                if 2 <= i:
                    pc, pkb, pdg, pc0, pfirst, plast = info(i - 2)
                    pzz = zzv(i - 2)
                    wt, bwt = WT[(i - 2) % 2], bWT[(i - 2) % 2]
                    if pdg:
                        zero_left(wt, bwt, pc0)
                    self.act(wt[:, :, pc0:512], pzz[:, :, pc0:512], AF.Exp, zzb(i - 2), [bwt], scale=0.125)
                if i < n:
                    sp, bsp = SP[i % 2], bSP[i % 2]
                    if dg:
                        zero_left(sp, bsp, c0)
                    self.act(sp[:, :, c0:512], E[:, :, c0:512], AF.Ln, [bE], [bsp], bias=1.0)
                if i + 1 < n:
                    S_(i + 1)
                if 2 <= i:
                    ob_i = 6
                    ob = self.bank(ob_i)
                    for h in range(2):
                        self.mm(ob[64 * h:64 * h + 64, :], Vt[:, pkb, pc * 128 + 64 * h:pc * 128 + 64 * h + 64],
                                wt[:, h, :], pfirst, plast, [bV[pkb // 4], bwt], [self.pb[ob_i]])
                    if plast:
                        odst = self.OT[:, pc, t * 512:(t + 1) * 512]
                        self.dve(lambda e, ob=ob, odst=odst: e.tensor_copy(out=odst, in_=ob),
                                 [self.pb[ob_i]], [self.bOT[t]])
                if i < n:
                    r, br = R[c % 2], bR[c % 2]
                    for h in range(2):
                        self.mm(zz[:, h, c0:512], self.NEGTRI8, sp[:, h, c0:512], False, True,
                                [bsp, self.bCM], [self.pb[2 * (i % 3) + h]], skip=True)
                        if not first:
                            self.mm(zz[:, h, c0:512], self.NEGONES8, r[:, h, c0:512], False, True,
                                    [br, self.bCM], [self.pb[2 * (i % 3) + h]], skip=True)
                    if not last:
                        if first:
                            self.dve(lambda e, r=r, sp=sp: e.tensor_copy(out=r, in_=sp), [bsp], [br])
                        else:
                            self.dve(lambda e, r=r, sp=sp: e.tensor_tensor(out=r, in0=r, in1=sp, op=ALU.add),
                                     [bsp, br], [br])
                if wcp_pos < len(wcp):
                    if wcp[wcp_pos] is not None:
                        wcp[wcp_pos]()
                    wcp_pos += 1
                if bg and i >= 1:
                    per = -(-len(bg) // max(1, n - 3))
                    for _ in range(per):
                        if bg_pos < len(bg):
                            bg[bg_pos]()
                            bg_pos += 1
            while bg_pos < len(bg):
                bg[bg_pos]()
                bg_pos += 1
        while wcp_pos < len(wcp):
            if wcp[wcp_pos] is not None:
                wcp[wcp_pos]()
            wcp_pos += 1
        self.s.barrier()

    def pass_b(self, b):
        A, S, NT = self.A, self.S, self.NT
        A.off = self.persist_mark
        KT = A.bf(4 * S).rearrange("p (c s) -> p c s", c=4)
        Vt = A.bf(S * 4).rearrange("p (k n) -> p k n", n=512)
        bKT = [self.buf(f"KTd{t}") for t in range(NT)]
        bV = [self.buf(f"Vd{t}") for t in range(NT)]
        self.alloc_common(3)
        QTs = [A.bf(4 * 512).rearrange("p (c s) -> p c s", c=4) for _ in range(2)]
        bQTs = [self.buf(f"QTd{i}") for i in range(2)]
        COS = A.f32(512)
        SIN = A.f32(512)
        bCOS = self.buf("COS")
        bSIN = self.buf("SIN")
        ET = [A.bf(1024).rearrange("p (h s) -> p h s", h=2) for _ in range(2)]
        bET = [self.buf(f"ET{i}") for i in range(2)]
        RD, XS, YS = [A.f32(512) for _ in range(3)]
        bN = self.buf("NRM")
        bRP = self.buf("ROPE")
        RT = A.f32(512)
        SQ = YS.bitcast(BF)[:, 0:512]
        XL, YL = A.bf(512), A.bf(512)

        def hi(ap):
            return ap.bitcast(BF).rearrange("p (n two) -> p n two", two=2)[:, :, 1]
        fg = {"n": 0}

        def fg_bank():
            fg["n"] += 1
            return 4 + fg["n"] % 4

        def mk_items(t, bankfn):
            qt, bqt = QTs[t % 2], bQTs[t % 2]

            def ev(dst, dbuf):
                def f(c, gi, bi):
                    bk = self.bank(bi)
                    if gi == 0:
                        self.dve(lambda e: e.tensor_tensor(out=RT, in0=bk, in1=COS, op=ALU.mult),
                                 [self.pb[bi], bCOS], [bRP])
                    else:
                        d = dst(c)
                        self.dve(lambda e: e.tensor_tensor(out=bk, in0=bk, in1=SIN, op=ALU.mult),
                                 [self.pb[bi], bSIN], [self.pb[bi]])
                        self.dve(lambda e: e.tensor_tensor(out=d, in0=bk, in1=RT, op=ALU.add),
                                 [self.pb[bi], bRP], [dbuf])
                return f

            def ldcs():
                self.dma("sp", COS, self.cosT[:, t * 512:(t + 1) * 512], [], [bCOS], ("ld", "COS"))
                self.dma("sp", SIN, self.sinT[:, t * 512:(t + 1) * 512], [], [bSIN], ("ld", "SIN"))

            return self.prep_items(b, t, [([3, 6], ev(lambda c: qt[:, c, :], bqt)),
                                         ([4, 7], ev(lambda c: KT[:, c, t * 512:(t + 1) * 512], bKT[t]))],
                                   bankfn, (5, Vt, bV[t]), pre=[ldcs])

        for it in mk_items(0, fg_bank):
            it()
        for t in range(NT):
            QT, bQT = QTs[t % 2], bQTs[t % 2]
            bg = mk_items(t + 1, lambda: 7) if t + 1 < NT else []
            bg_pos = 0
            nkb = 4 * t + 4
            steps = [(hd, kb) for hd in range(4) for kb in range(nkb)]
            n = len(steps)

            def info(i):
                hd, kb = steps[i]
                j = kb - 4 * t
                c0 = j * 128 if j >= 0 else 0
                return hd, kb, j >= 0, c0, kb == 0, kb == nkb - 1

            def S_(i):
                hd, kb, dg, c0, first, last = info(i)
                zz = self.PS[i % 2].rearrange("p (h s) -> p h s", h=2)
                for h in range(2):
                    self.mm(zz[:, h, c0:512], KT[64 * h:64 * h + 64, hd, kb * 128:(kb + 1) * 128],
                            QT[64 * h:64 * h + 64, hd, c0:512], True, True, [bKT[kb // 4], bQT],
                            [self.pb[2 * (i % 2) + h]])
                if dg:
                    for h in range(2):
                        self.mm(zz[:, h, c0:c0 + 128], self.IDENT, self.DAMASK, False, True, [self.bCM],
                                [self.pb[2 * (i % 2) + h]], skip=True)

            deferred = []
            S_(0)
            for i in range(n):
                hd, kb, dg, c0, first, last = info(i)
                zz = self.PS[i % 2].rearrange("p (h s) -> p h s", h=2)
                et, bet = ET[i % 2], bET[i % 2]
                self.act(et[:, :, c0:512], zz[:, :, c0:512], AF.Exp, [self.pb[2 * (i % 2)], self.pb[2 * (i % 2) + 1]],
                         [bet], scale=0.125)
                if i + 1 < n:
                    S_(i + 1)
                for half, bi in ((0, 4), (1, 5)):
                    vv = Vt[:, kb, hd * 128 + 64 * half:hd * 128 + 64 * half + 64]
                    for h in range(2):
                        self.mm(self.bank(bi)[64 * h:64 * h + 64, c0:512], vv, et[:, h, c0:512], first, dg,
                                [bV[kb // 4], bet], [self.pb[bi]], skip=(c0 > 0))
                for h in range(2):
                    self.mm(self.bank(6)[64 * h:64 * h + 64, c0:512], self.ONES[:, 0:64], et[:, h, c0:512], first, dg,
                            [self.bCM, bet], [self.pb[6]], skip=(c0 > 0))
                if last:
                    self.act(RD, self.bank(6), AF.Ln, [self.pb[6]], [bN])
                    self.dve(lambda e: e.tensor_copy(out=XS, in_=self.bank(4)), [self.pb[4]], [bN])
                    self.dve(lambda e: e.tensor_copy(out=YS, in_=self.bank(5)), [self.pb[5]], [bN])

                    def stage1():
                        self.act(RD, RD, AF.Exp, [bN], [bN], scale=-1.0)
                        for T_, L_ in ((XS, XL), (YS, YL)):
                            self.dve(lambda e, T_=T_: e.scalar_tensor_tensor(out=T_, in0=T_, scalar=self.LAMCOL, in1=RD,
                                                                             op0=ALU.mult, op1=ALU.mult),
                                     [bN, self.bSM], [bN])
                            self.dve(lambda e, T_=T_, L_=L_: e.tensor_tensor(out=L_, in0=T_, in1=hi(T_), op=ALU.subtract),
                                     [bN], [bN])

                    def stage2():
                        self.mm(self.bank(7), self.PX, hi(XS), True, False, [bN, self.bCM], [self.pb[7]])
                        self.mm(self.bank(7), self.PY, hi(YS), False, False, [bN, self.bCM], [self.pb[7]])
                        self.mm(self.bank(7), self.PX, XL, False, False, [bN, self.bCM], [self.pb[7]])
                        self.mm(self.bank(7), self.PY, YL, False, True, [bN, self.bCM], [self.pb[7]])
                        self.act(SQ, self.bank(7), AF.Square, [self.pb[7]], [bN])
                        self.dve(lambda e: e.tensor_copy(out=XS, in_=self.bank(7)), [self.pb[7]], [bN])

                    def stage3(hd=hd):
                        self.mm(self.bank(7), self.ONESDIV, SQ, True, True, [bN, self.bCM], [self.pb[7]])
                        self.act(RD, self.bank(7), AF.Ln, [self.pb[7]], [bN], bias=SUBLN_EPS)
                        self.act(RD, RD, AF.Exp, [bN], [bN], scale=-0.5)
                        odst = self.OT[:, 4 + hd, t * 512:(t + 1) * 512]
                        self.dve(lambda e, odst=odst: e.scalar_tensor_tensor(
                            out=odst, in0=XS, scalar=self.GS, in1=RD,
                            op0=ALU.mult, op1=ALU.mult), [bN, self.bSM], [self.bOT[t]])

                    dd = [min(d, nkb - 1) for d in (1, 5, 8)]
                    deferred.extend([(i + dd[0], stage1), (i + dd[1], stage2), (i + dd[2], stage3)])
                    deferred.sort(key=lambda x: x[0])
                while deferred and deferred[0][0] <= i:
                    deferred.pop(0)[1]()
                if bg and i >= 1:
                    per = -(-len(bg) // max(1, n - 3))
                    for _ in range(per):
                        if bg_pos < len(bg):
                            bg[bg_pos]()
                            bg_pos += 1
            while deferred:
                deferred.pop(0)[1]()
            while bg_pos < len(bg):
                bg[bg_pos]()
                bg_pos += 1
        self.s.barrier()

    def norm_parts(self, x_ap, xbufs, gtile, HTdst, bHTdst, tb, bank_i):
        st = {}

        def sq():
            st["hb"] = self.next_hb()
            st["r"] = self.rstd(x_ap, xbufs, *st["hb"])

        def nrm():
            hb, bhb = st["hb"]
            r, bst = st["r"]
            self.dve(lambda e: e.scalar_tensor_tensor(out=hb, in0=x_ap, scalar=r, in1=gtile, op0=ALU.mult, op1=ALU.mult),
                     list(xbufs) + [bst, self.bG], [bhb])

        def tr():
            hb, bhb = st["hb"]
            tp = self.bank(bank_i).bitcast(BF)
            for c in range(8):
                self.tr(tp[:, c * 128:(c + 1) * 128], hb[:, c * 128:(c + 1) * 128], [bhb], [self.pb[bank_i]])
            self.dve(lambda e: e.tensor_copy(out=HTdst[:, :, tb * 128:(tb + 1) * 128],
                                             in_=tp.rearrange("p (c n) -> p c n", c=8)), [self.pb[bank_i]], [bHTdst])
        return sq, nrm, tr

    def pass_c(self, b):
        A, S, NT = self.A, self.S, self.NT
        A.off = self.persist_mark
        self.alloc_common(4, with_xb=False, extra_g=True)
        GM, GF, GN = self.GMIX, self.GFFN, self.GFIN
        HTf, bHTf = self.HT, self.bHT
        HTm = A.bf(8 * 512).rearrange("p (c s) -> p c s", c=8)
        bHTm = self.buf("HTm")
        XTs = [A.f32(4 * D).rearrange("p (k d) -> p k d", k=4) for _ in range(2)]
        bXTs = [[self.buf(f"XT{i}_{k}") for k in range(4)] for i in range(2)]
        MIX = A.bf(8 * 512).rearrange("p (c s) -> p c s", c=8)
        bMIX = self.buf("MIX")
        ACTT = A.bf(NFC * 512).rearrange("p (f s) -> p f s", f=NFC)
        bACT = self.buf("ACTT")
        S1, S2, M1, M2 = [A.f32(512) for _ in range(4)]
        bSG = self.buf("SG")
        SGL = [A.f32(512) for _ in range(2)]
        bSGL = [self.buf(f"SGL{i}") for i in range(2)]

        def mix_items(t):
            XT, bX = XTs[t % 2], bXTs[t % 2]
            items = []
            for tb in range(4):
                def LD(tb=tb):
                    self.dma("pool", XT[:, tb, :], self.x[b, t * 512 + tb * 128:t * 512 + (tb + 1) * 128, :], [],
                             [bX[tb]], ("ld", f"XT{t % 2}_{tb}"))
                items.append(LD)
            parts = [self.norm_parts(XT[:, tb, :], [bX[tb]], GM, HTm, bHTm, tb, 6 + (tb % 2)) for tb in range(4)]
            sq, nrm, tr = zip(*parts)
            items += [sq[0], sq[1], nrm[0], tr[0], nrm[1], sq[2], tr[1], nrm[2], sq[3], tr[2], nrm[3], tr[3]]
            return items

        for it in mix_items(0):
            it()
        for t in range(NT):
            XT, bXTl = XTs[t % 2], bXTs[t % 2]
            tok = slice(t * 512, (t + 1) * 512)
            for j in range(8):
                ws, bws = self.wslot()
                self.dma("sp", ws[:, 0:3072], self.WG[j], [self.bWC], [bws], ("ld", bws.name))
                b0 = 4 * (j % 2)
                for k in range(2):
                    gb, bbk = b0 + k, b0 + 2 + k
                    for dc in range(8):
                        self.mm(self.bank(gb), ws[:, k * 1024 + dc * 128:k * 1024 + (dc + 1) * 128], HTm[:, dc, :],
                                dc == 0, dc == 7, [bws, bHTm], [self.pb[gb]])
                    for c in range(4):
                        self.mm(self.bank(bbk), ws[:, 2048 + k * 512 + c * 128:2048 + k * 512 + (c + 1) * 128],
                                self.OT[:, 4 * k + c, tok], c == 0, c == 3, [bws, self.bOT[t]], [self.pb[bbk]])
                g1, g2, p1, p2 = [self.bank(b0 + q) for q in range(4)]
                self.act(S1, g1, AF.Sigmoid, [self.pb[b0]], [bSG])
                self.act(S2, g2, AF.Sigmoid, [self.pb[b0 + 1]], [bSG])
                self.dve(lambda e, p1=p1: e.tensor_tensor(out=M1, in0=p1, in1=S1, op=ALU.mult), [self.pb[b0 + 2], bSG], [bSG])
                self.dve(lambda e, p2=p2: e.tensor_tensor(out=M2, in0=p2, in1=S2, op=ALU.mult), [self.pb[b0 + 3], bSG], [bSG])
                self.dve(lambda e, j=j: e.tensor_tensor(out=MIX[:, j, :], in0=M1, in1=M2, op=ALU.add), [bSG], [bMIX])
            wo = []
            for ch in range(2):
                ws, bws = self.wslot()
                self.dma("sp", ws, self.WO[ch], [self.bWC], [bws], ("ld", bws.name))
                wo.append((ws.rearrange("p (c n) -> p c n", c=8), bws))
            fparts = [self.norm_parts(XT[:, tb, :], [bXTl[tb]], GF, HTf, bHTf, tb, 6 + (tb % 2)) for tb in range(4)]
            for tb in range(4):
                for ch in range(2):
                    bi = (2 * tb + ch) % 4
                    wv, bws = wo[ch]
                    for c in range(8):
                        self.mm(self.bank(bi), MIX[:, c, tb * 128:(tb + 1) * 128], wv[:, c, :], c == 0, c == 7,
                                [bws, bMIX], [self.pb[bi]])
                    xs = XT[:, tb, ch * 512:(ch + 1) * 512]
                    bk = self.bank(bi)
                    self.dve(lambda e, xs=xs, bk=bk: e.tensor_tensor(out=xs, in0=bk, in1=xs, op=ALU.add),
                             [self.pb[bi], bXTl[tb]], [bXTl[tb]])
                fparts[tb][0]()
                fparts[tb][1]()
                if tb >= 1:
                    fparts[tb - 1][2]()
            fparts[3][2]()
            bg = mix_items(t + 1) if t + 1 < NT else []
            bg_pos = 0
            for fp in range(NFC // 2):
                ws, bws = self.wslot()
                self.dma("sp", ws.rearrange("p (a m) -> p a m", a=4),
                         self.WF[2 * fp:2 * fp + 2].rearrange("f k p m -> p (f k) m"),
                         [self.bWC], [bws], ("ld", bws.name))
                wv = ws.rearrange("p (a m) -> p a m", a=4)
                for i2 in range(2):
                    f = 2 * fp + i2
                    gb, ub = 2 * (f % 2), 2 * (f % 2) + 1
                    for dc in range(8):
                        self.mm(self.bank(gb), wv[:, 2 * i2, dc * 128:(dc + 1) * 128], HTf[:, dc, :], dc == 0, dc == 7,
                                [bws, bHTf], [self.pb[gb]])
                    for dc in range(8):
                        self.mm(self.bank(ub), wv[:, 2 * i2 + 1, dc * 128:(dc + 1) * 128], HTf[:, dc, :], dc == 0,
                                dc == 7, [bws, bHTf], [self.pb[ub]])
                    sg, bsg = SGL[f % 2], bSGL[f % 2]
                    gbk, ubk = self.bank(gb), self.bank(ub)
                    adst = ACTT[:, f, :]
                    self.act(sg, gbk, AF.Silu, [self.pb[gb]], [bsg])
                    self.dve(lambda e, sg=sg, ubk=ubk, adst=adst: e.tensor_tensor(out=adst, in0=ubk, in1=sg, op=ALU.mult),
                             [self.pb[ub], bsg], [bACT])
                    if f >= 2 and bg_pos < len(bg):
                        bg[bg_pos]()
                        bg_pos += 1
            while bg_pos < len(bg):
                bg[bg_pos]()
                bg_pos += 1
            for ch in range(2):
                for gi, (f0, f1) in enumerate(self.wd_groups):
                    nf = f1 - f0
                    ws, bws = self.wslot()
                    self.dma("sp", ws[:, 0:nf * 512], self.WD[ch][:, f0 * 512:f1 * 512], [self.bWC], [bws],
                             ("ld", bws.name))
                    wv = ws[:, 0:nf * 512].rearrange("p (f n) -> p f n", f=nf)
                    for tb in range(4):
                        bi = 4 + tb
                        for f in range(f0, f1):
                            self.mm(self.bank(bi), ACTT[:, f, tb * 128:(tb + 1) * 128], wv[:, f - f0, :], f == 0,
                                    f == NFC - 1, [bws, bACT], [self.pb[bi]])
                for tb in range(4):
                    bi = 4 + tb
                    xs = XT[:, tb, ch * 512:(ch + 1) * 512]
                    bk = self.bank(bi)
                    self.dve(lambda e, xs=xs, bk=bk: e.tensor_tensor(out=xs, in0=bk, in1=xs, op=ALU.add),
                             [self.pb[bi], bXTl[tb]], [bXTl[tb]])
            for tb in range(4):
                jk, bjk = self.next_hb()
                xr = XT[:, tb, :]
                r, bst = self.rstd(xr, [bXTl[tb]], jk, bjk)
                self.dve(lambda e, xr=xr, r=r: e.scalar_tensor_tensor(out=xr, in0=xr, scalar=r, in1=GN, op0=ALU.mult,
                                                                      op1=ALU.mult),
                         [bXTl[tb], bst, self.bG], [bXTl[tb]])
                st = self.dma("pool", self.out[b, t * 512 + tb * 128:t * 512 + (tb + 1) * 128, :], xr,
                              [bXTl[tb]], [], ("st", f"XT{t % 2}_{tb}"))
                self.s.final_dma.append(st)
        self.s.barrier()


def host_consts(S):
    jj, kk = np.meshgrid(np.arange(128), np.arange(128), indexing="ij")
    ident = np.eye(128, dtype=np.float32)
    negtri8 = np.where(jj >= kk, -8.0, 0.0).astype(np.float32)
    negones8 = np.full((128, 128), -8.0, np.float32)
    ones = np.ones((128, 128), np.float32)
    onesdiv = np.full((128, 128), 1.0 / 128.0, np.float32)
    sbmask = np.where(jj < kk, 0.0, -30000.0).astype(np.float32)
    damask = np.where((jj < 64) | (kk >= 64), 0.0, -30000.0).astype(np.float32)
    p1 = ((jj == kk) & (kk < 64)).astype(np.float32)
    p2 = ((jj == kk + 64) & (kk < 64)).astype(np.float32)
    p3 = ((jj == kk - 64) & (kk >= 64)).astype(np.float32)
    p4 = ((jj == kk) & (kk >= 64)).astype(np.float32)
    cmat = np.concatenate([ident, negtri8, negones8, ones, onesdiv, sbmask, damask, p1 + p2, p3 + p4], axis=1)
    inv_freq = (1.0 / (np.float32(10000.0) ** (np.arange(0, 64, 2, dtype=np.float32) / np.float32(64)))).astype(np.float32)
    ang = np.arange(S, dtype=np.float32)[:, None] * inv_freq[None, :]
    cos = np.cos(ang).astype(np.float32).T
    sin = np.sin(ang).astype(np.float32).T
    cosT = np.ascontiguousarray(np.tile(cos, (4, 1)))
    sinT = np.ascontiguousarray(np.tile(sin, (4, 1)))
    return cmat, cosT, sinT


_NC_CACHE = {}


def make_in_maps(inputs, S, NB, ncores):
    cmat, cosT, sinT = host_consts(S)
    f = lambda a: np.ascontiguousarray(np.asarray(a, dtype=np.float32))
    x = f(inputs["x"])
    shared = {
        "w_in": f(inputs["w_in"][0]),
        "wbs": f(inputs["w_branch_sb"][0]),
        "wbd": f(inputs["w_branch_da"][0]),
        "wo": f(inputs["w_out"][0]),
        "wg": f(inputs["w_ffn_gate"][0]),
        "wu": f(inputs["w_ffn_up"][0]),
        "wd": f(inputs["w_ffn_down"][0]),
        "gmix": f(inputs["g_mix"]).reshape(1, D),
        "gffn": f(inputs["g_ffn"]).reshape(1, D),
        "gfin": f(inputs["g_final"]).reshape(1, D),
        "gsub": f(inputs["g_subln"]).reshape(1, 128),
        "lamv": np.ascontiguousarray(np.stack([f(inputs["lambda_q1"]).reshape(64), f(inputs["lambda_k1"]).reshape(64),
                                               f(inputs["lambda_q2"]).reshape(64), f(inputs["lambda_k2"]).reshape(64)])),
        "cmat": cmat, "cosT": cosT, "sinT": sinT,
    }
    maps = []
    for i in range(ncores):
        m = dict(shared)
        m["x"] = np.ascontiguousarray(x[i * NB:(i + 1) * NB])
        maps.append(m)
    return maps


def kernel(**inputs):
    x = np.asarray(inputs["x"])
    B, S, _ = x.shape
    NB = B // NCORES
    key = (S, NB)
    if key not in _NC_CACHE:
        _NC_CACHE[key] = KB(S, NB).build()
    nc = _NC_CACHE[key]
    in_maps = make_in_maps(inputs, S, NB, NCORES)
    res = run_bass_kernel_spmd(nc, in_maps, core_ids=list(range(NCORES)))
    return np.concatenate([np.asarray(r["out"]) for r in res.results], axis=0).astype(np.float32)
```

```python
import math
from contextlib import ExitStack

import numpy as np
import concourse.bass as bass
import concourse.mybir as mybir
from concourse.bass_utils import run_bass_kernel_spmd

F32 = mybir.dt.float32
BF = mybir.dt.bfloat16
AF = mybir.ActivationFunctionType
ALU = mybir.AluOpType
AX = mybir.AxisListType

D = 1024
DC = 8
DFF = 2816
NFC = 22
INW = 5120
NCORES = 8
SEQ = 4096
BATCH = 16
NORM_EPS = 1e-6
SUBLN_EPS = 1e-5
LAMBDA_INIT = 0.8 - 0.6 * math.exp(-0.3 * 0)

SEM_CAP = 30000
STRICT_SAME_ENGINE = True
ENGS = ("pe", "act", "dve", "pool", "sp")


class Buf:
    __slots__ = ("name", "lw", "rd")

    def __init__(self, name):
        self.name = name
        self.lw = None
        self.rd = []


class Op:
    __slots__ = ("eng", "fn", "pos", "waits", "marked", "sem", "val", "dma_key", "dma_val", "dma_sem")

    def __init__(self, eng, fn):
        self.eng = eng
        self.fn = fn
        self.pos = -1
        self.waits = []
        self.marked = False
        self.sem = None
        self.val = 0
        self.dma_key = None
        self.dma_val = 0
        self.dma_sem = None


class Sched:
    def __init__(self, nc):
        self.nc = nc
        self.streams = {e: [] for e in ENGS}
        self.seen = {e: {p: -1 for p in ENGS} for e in ENGS}
        self.seen_dma = {e: {} for e in ENGS}
        self.dma_cnt = {}
        self.final_dma = []

    def add(self, eng, fn, reads=(), writes=(), dma_key=None):
        op = Op(eng, fn)
        op.pos = len(self.streams[eng])
        is_dma = dma_key is not None
        if is_dma:
            ep, cnt = self.dma_cnt.get(dma_key, (0, 0))
            if cnt + 16 > SEM_CAP:
                ep, cnt = ep + 1, 0
            cnt += 16
            self.dma_cnt[dma_key] = (ep, cnt)
            op.dma_key = (dma_key, ep)
            op.dma_val = cnt
        deps = {}
        for b in reads:
            if b.lw is not None:
                deps[id(b.lw)] = (b.lw, True)
        for b in writes:
            if b.lw is not None and id(b.lw) not in deps:
                deps[id(b.lw)] = (b.lw, False)
            for r in b.rd:
                if id(r) not in deps:
                    deps[id(r)] = (r, False)
        best = {}
        for p, raw in deps.values():
            if p is op:
                continue
            if p.dma_key is not None:
                k = p.dma_key
                if self.seen_dma[eng].get(k, 0) >= p.dma_val:
                    continue
                self.seen_dma[eng][k] = p.dma_val
                op.waits.append(p)
                continue
            if p.eng == eng and not is_dma:
                if eng == "pe" or (not raw and not STRICT_SAME_ENGINE):
                    continue
            if p.eng not in best or best[p.eng].pos < p.pos:
                best[p.eng] = p
        for pe_, p in best.items():
            if self.seen[eng][pe_] >= p.pos:
                continue
            self.seen[eng][pe_] = p.pos
            p.marked = True
            op.waits.append(p)
        for b in reads:
            b.rd.append(op)
        for b in writes:
            b.lw = op
            b.rd = []
        self.streams[eng].append(op)
        return op

    def barrier(self):
        lasts = {}
        for e in ENGS:
            if e == "sp":
                continue
            for o in reversed(self.streams[e]):
                if o.dma_key is None and o.fn is not None:
                    lasts[e] = o
                    break
        dlast = {}
        for e in ENGS:
            for o in self.streams[e]:
                if o.dma_key is not None:
                    dlast[o.dma_key] = o
        for e in ENGS:
            op = Op(e, None)
            op.pos = len(self.streams[e])
            for pe_, p in lasts.items():
                if pe_ == e or self.seen[e][pe_] >= p.pos:
                    continue
                self.seen[e][pe_] = p.pos
                p.marked = True
                op.waits.append(p)
            for k, p in dlast.items():
                if self.seen_dma[e].get(k, 0) >= p.dma_val:
                    continue
                self.seen_dma[e][k] = p.dma_val
                op.waits.append(p)
            if op.waits:
                self.streams[e].append(op)

    def emit(self, es):
        nc = self.nc
        for e in ENGS:
            cnt = 0
            sem = None
            n = 0
            for o in self.streams[e]:
                if o.marked:
                    if sem is None or cnt >= SEM_CAP:
                        sem = es.enter_context(nc.semaphore(f"s_{e}_{n}"))
                        n += 1
                        cnt = 0
                    cnt += 1
                    o.sem = sem
                    o.val = cnt
        dsem = {}
        for e in ENGS:
            for o in self.streams[e]:
                if o.dma_key is not None:
                    if o.dma_key not in dsem:
                        dsem[o.dma_key] = es.enter_context(nc.semaphore(f"d_{len(dsem)}"))
                    o.dma_sem = dsem[o.dma_key]
        streams = self.streams
        final_dma = self.final_dma

        def run(eng_name, eng):
            for o in streams[eng_name]:
                for p in o.waits:
                    if p.dma_key is not None:
                        eng.wait_ge(p.dma_sem, p.dma_val)
                    else:
                        eng.wait_ge(p.sem, p.val)
                if o.fn is None:
                    continue
                ins = o.fn(eng)
                if o.dma_key is not None:
                    ins.then_inc(o.dma_sem, 16)
                elif o.marked:
                    ins.then_inc(o.sem, 1)
            if eng_name == "sp":
                for o in final_dma:
                    eng.wait_ge(o.dma_sem, o.dma_val)

        block = es.enter_context(nc.Block())

        @block.tensor
        def _(eng):
            run("pe", eng)

        @block.scalar
        def _(eng):
            run("act", eng)

        @block.vector
        def _(eng):
            run("dve", eng)

        @block.gpsimd
        def _(eng):
            run("pool", eng)

        @block.sync
        def _(eng):
            run("sp", eng)


class Arena:
    def __init__(self, ap, nbytes):
        self.ap = ap
        self.nbytes = nbytes
        self.off = 0
        self.peak = 0

    def _take(self, nb):
        nb = (nb + 63) // 64 * 64
        o = self.off
        self.off += nb
        self.peak = max(self.peak, self.off)
        assert self.off <= self.nbytes, f"arena overflow {self.off} > {self.nbytes}"
        return o

    def bf(self, n):
        o = self._take(2 * n)
        return self.ap[:, o // 2:o // 2 + n]

    def f32(self, n):
        o = self._take(4 * n)
        return self.ap[:, o // 2:o // 2 + 2 * n].bitcast(F32)


class KB:
    def __init__(self, S, NB, arena_bytes=212480):
        self.S = S
        self.NB = NB
        self.NT = S // 512
        self.arena_bytes = arena_bytes
        self.nbuf = 0

    def buf(self, name):
        self.nbuf += 1
        return Buf(f"{name}#{self.nbuf}")

    def mm(self, out, lhsT, rhs, start, stop, reads, writes, skip=False):
        self.s.add("pe", lambda e: e.matmul(out, lhsT=lhsT, rhs=rhs, start=start, stop=stop,
                                            skip_group_check=skip), reads, writes)

    def tr(self, out, in_, reads, writes):
        ident = self.IDENT
        self.s.add("pe", lambda e: e.transpose(out, in_, ident), list(reads) + [self.bCM], writes)

    def act(self, out, in_, func, reads, writes, bias=None, scale=1.0, accum=None):
        def fn(e):
            kw = {}
            if bias is not None:
                kw["bias"] = bias
            if accum is not None:
                kw["accum_out"] = accum
            return e.activation(out=out, in_=in_, func=func, scale=scale, **kw)
        self.s.add("act", fn, reads, writes)

    def dve(self, fn, reads, writes):
        self.s.add("dve", fn, reads, writes)

    def pool(self, fn, reads, writes):
        self.s.add("pool", fn, reads, writes)

    def dma(self, q, out, in_, reads, writes, key):
        return self.s.add(q, lambda e: e.dma_start(out=out, in_=in_), reads, writes, dma_key=key)

    def bank(self, i):
        return self.PS[i // 2][:, (i % 2) * 512:(i % 2 + 1) * 512]

    def build(self):
        S, NB = self.S, self.NB
        nc = bass.Bass("TRN2", target_bir_lowering=False)
        self.nc = nc
        dt = nc.dram_tensor
        I = "ExternalInput"
        self.x = dt("x", [NB, S, D], F32, kind=I).ap()
        self.w_in = dt("w_in", [D, INW], F32, kind=I).ap()
        self.wbs = dt("wbs", [512, D], F32, kind=I).ap()
        self.wbd = dt("wbd", [512, D], F32, kind=I).ap()
        self.wo = dt("wo", [D, D], F32, kind=I).ap()
        self.wg = dt("wg", [D, DFF], F32, kind=I).ap()
        self.wu = dt("wu", [D, DFF], F32, kind=I).ap()
        self.wd = dt("wd", [DFF, D], F32, kind=I).ap()
        self.gmix = dt("gmix", [1, D], F32, kind=I).ap()
        self.gffn = dt("gffn", [1, D], F32, kind=I).ap()
        self.gfin = dt("gfin", [1, D], F32, kind=I).ap()
        self.gsub = dt("gsub", [1, 128], F32, kind=I).ap()
        self.lamv = dt("lamv", [4, 64], F32, kind=I).ap()
        self.cmat = dt("cmat", [128, 9 * 128], F32, kind=I).ap()
        self.cosT = dt("cosT", [128, S], F32, kind=I).ap()
        self.sinT = dt("sinT", [128, S], F32, kind=I).ap()
        self.out = dt("out", [NB, S, D], F32, kind="ExternalOutput").ap()
        self.WA = dt("WA", [8, 4, 128, 1024], BF, kind="Internal").ap()
        self.WG = dt("WG", [8, 128, 3072], BF, kind="Internal").ap()
        self.WO = dt("WO", [2, 128, 4096], BF, kind="Internal").ap()
        self.WF = dt("WF", [NFC, 2, 128, 1024], BF, kind="Internal").ap()
        self.WD = dt("WD", [2, 128, NFC * 512], BF, kind="Internal").ap()

        with ExitStack() as es:
            AR = es.enter_context(nc.sbuf_tensor("AR", [128, self.arena_bytes // 2], BF))
            self.A = Arena(AR, self.arena_bytes)
            self.PS = [es.enter_context(nc.psum_tensor(f"PS{i}", [128, 1024], F32)) for i in range(4)]
            self.pb = [self.buf(f"pb{i}") for i in range(8)]
            self.s = Sched(nc)
            self.setup()
            self.s.barrier()
            self.phase_w()
            for b in range(NB):
                self.pass_a(b)
                self.pass_b(b)
                self.pass_c(b)
            self.s.emit(es)
        return nc

    def setup(self):
        A, S = self.A, self.S
        self.CM = A.bf(9 * 128)
        self.bCM = self.buf("CM")
        CM = self.CM
        self.IDENT = CM[:, 0:128]
        self.NEGTRI8 = CM[:, 128:256]
        self.NEGONES8 = CM[:, 256:384]
        self.ONES = CM[:, 384:512]
        self.ONESDIV = CM[:, 512:640]
        self.SBMASK = CM[:, 640:768]
        self.DAMASK = CM[:, 768:896]
        self.PX = CM[:, 896:1024]
        self.PY = CM[:, 1024:1152]
        self.dma("pool", CM, self.cmat, [], [self.bCM], "CM")
        self.SM = A.f32(32)
        self.bSM = self.buf("SM")
        SM = self.SM
        self.M05 = SM[:, 0:1]
        self.GSUB = SM[:, 1:2]
        self.GS = SM[:, 2:3]
        self.NEGLAM = SM[:, 3:4]
        self.bLAMV = self.buf("LAMV")
        self.ST = A.f32(32)
        self.bST = [self.buf(f"ST{i}") for i in range(8)]
        self.stn = 0
        self.OT = A.bf(8 * S).rearrange("p (c s) -> p c s", c=8)
        self.bOT = [self.buf(f"OT{t}") for t in range(self.NT)]
        self.persist_mark = A.off
        self.LAMV = A.f32(4 * 64)
        PR = A.f32(128)
        self.pool(lambda e: e.memset(self.M05, -0.5), [], [self.bSM])
        self.dma("sp", self.GSUB, self.gsub.rearrange("o p -> p o"), [], [self.bSM], "SMg")
        self.dma("sp", self.LAMV, self.lamv.rearrange("a b -> (a b)").partition_broadcast(128),
                 [], [self.bLAMV], "LAMV")
        LV = self.LAMV.rearrange("p (a b) -> p a b", a=4)
        bPR = self.buf("PR")
        LS = SM[:, 8:10]
        EL = SM[:, 10:12]
        self.dve(lambda e: e.tensor_tensor(out=PR[:, 0:64], in0=LV[:, 0, :], in1=LV[:, 1, :], op=ALU.mult),
                 [self.bLAMV], [bPR])
        self.dve(lambda e: e.tensor_tensor(out=PR[:, 64:128], in0=LV[:, 2, :], in1=LV[:, 3, :], op=ALU.mult),
                 [self.bLAMV], [bPR])
        self.dve(lambda e: e.reduce_sum(out=LS, in_=PR.rearrange("p (a b) -> p a b", a=2), axis=AX.X),
                 [bPR], [self.bSM])
        self.act(EL, LS, AF.Exp, [self.bSM], [self.bSM])
        self.dve(lambda e: e.tensor_tensor(out=SM[:, 12:13], in0=EL[:, 1:2], in1=EL[:, 0:1], op=ALU.subtract),
                 [self.bSM], [self.bSM])
        self.dve(lambda e: e.tensor_scalar(out=self.NEGLAM, in0=SM[:, 12:13], scalar1=-LAMBDA_INIT, scalar2=None,
                                           op0=ALU.add), [self.bSM], [self.bSM])
        self.dve(lambda e: e.tensor_scalar(out=self.GS, in0=self.GSUB, scalar1=1.0 - LAMBDA_INIT, scalar2=None,
                                           op0=ALU.mult), [self.bSM], [self.bSM])
        self.LAMCOL = SM[:, 4:5]
        self.pool(lambda e: e.memset(SM[0:64, 4:5], 1.0), [], [self.bSM])
        self.dve(lambda e: e.tensor_copy(out=SM[64:128, 4:5], in_=SM[64:128, 3:4]), [self.bSM], [self.bSM])

    def phase_w(self):
        A = self.A
        A.off = self.persist_mark
        Fs = [A.f32(4096) for _ in range(3)]
        bF = [self.buf(f"F{i}") for i in range(3)]
        Hs = [A.bf(4096) for _ in range(3)]
        bH = [self.buf(f"H{i}") for i in range(3)]
        self.wconv_n = 0
        self.wconv_h = 0
        self.bW = {}

        def nextF():
            i = self.wconv_n % 3
            self.wconv_n += 1
            return Fs[i], bF[i]

        def nextH():
            i = self.wconv_h % 3
            self.wconv_h += 1
            return Hs[i], bH[i]

        def wbuf(key):
            b = self.buf(f"W{key}")
            self.bW[key] = b
            return b

        w_in_v = self.w_in.rearrange("(c p) n -> p c n", p=128)
        for g in range(6):
            F, bf_ = nextF()
            self.dma("sp", F.rearrange("p (c n) -> p c n", c=8), w_in_v[:, :, g * 512:(g + 1) * 512], [], [bf_],
                     ("ld", bf_.name))
            H, bh = nextH()
            self.dve(lambda e, H=H, F=F: e.tensor_copy(out=H.rearrange("p (j c n) -> p j c n", j=4, c=8),
                                                      in_=F.rearrange("p (c j n) -> p j c n", c=8, j=4)),
                     [bf_], [bh])
            self.dma("act", self.WA[g].rearrange("j p m -> p j m"), H.rearrange("p (j m) -> p j m", j=4),
                     [bh], [wbuf(("WA", g))], ("wst", "WA", g))
            if g in (3, 4):
                H2, bh2 = nextH()
                for j in range(4):
                    src = F.rearrange("p (c n) -> p c n", c=8)[:, :, j * 128:(j + 1) * 128] \
                        .rearrange("p c (m h r) -> p c m h r", m=2, h=2)
                    dst = H2[:, j * 1024:(j + 1) * 1024].rearrange("p (c m h r) -> p c m h r", c=8, m=2, h=2)
                    self.dve(lambda e, dst=dst, src=src: e.tensor_scalar(out=dst[:, :, :, 0, :], in0=src[:, :, :, 1, :],
                                                                        scalar1=-1.0, scalar2=None, op0=ALU.mult),
                             [bf_], [bh2])
                    self.dve(lambda e, dst=dst, src=src: e.tensor_copy(out=dst[:, :, :, 1, :], in_=src[:, :, :, 0, :]),
                             [bf_], [bh2])
                gg = 6 if g == 3 else 7
                self.dma("act", self.WA[gg].rearrange("j p m -> p j m"), H2.rearrange("p (j m) -> p j m", j=4),
                         [bh2], [wbuf(("WA", gg))], ("wst", "WA", gg))
        self.wd_groups = [(0, 8), (8, 16), (16, 22)]
        self.bWC = self.buf("WC")
        self.s.barrier()

    def alloc_common(self, nslots, with_xb=True, extra_g=False):
        A = self.A
        if with_xb:
            self.XB = [A.f32(D) for _ in range(2)]
            self.bXB = [self.buf(f"XB{i}") for i in range(2)]
        self.HB = [A.bf(D) for _ in range(2)]
        self.bHB = [self.buf(f"HB{i}") for i in range(2)]
        self.HT = A.bf(8 * 512).rearrange("p (c s) -> p c s", c=8)
        self.bHT = self.buf("HT")
        self.WS = [A.bf(4096) for _ in range(nslots)]
        self.bWS = [self.buf(f"WS{i}") for i in range(nslots)]
        self.wsn = 0
        self.nhb = 0
        self.bG = self.buf("G")
        self.GMIX = A.f32(D)
        gl = [(self.GMIX, self.gmix)]
        if extra_g:
            self.GFFN = A.f32(D)
            self.GFIN = A.f32(D)
            gl += [(self.GFFN, self.gffn), (self.GFIN, self.gfin)]
        for gt, gs in gl:
            self.dma("sp", gt, gs.rearrange("o d -> (o d)").partition_broadcast(128), [], [self.bG], "G")

    def wslot(self):
        i = self.wsn % len(self.WS)
        self.wsn += 1
        return self.WS[i], self.bWS[i]

    def rstd(self, x_ap, xbufs, junk, bjunk):
        g = self.stn % 8
        self.stn += 1
        st = self.ST[:, g * 4:g * 4 + 4]
        bst = self.bST[g]
        self.act(junk, x_ap, AF.Square, xbufs, [bjunk, bst], accum=st[:, 0:1])
        self.pool(lambda e: e.tensor_scalar(out=st[:, 1:2], in0=st[:, 0:1], scalar1=1.0 / D, scalar2=NORM_EPS,
                                            op0=ALU.mult, op1=ALU.add), [bst], [bst])
        self.pool(lambda e: e.tensor_tensor(out=st[:, 2:3], in0=st[:, 1:2], in1=self.M05, op=ALU.pow),
                  [bst, self.bSM], [bst])
        return st[:, 2:3], bst

    def next_hb(self):
        i = self.nhb % 2
        self.nhb += 1
        return self.HB[i], self.bHB[i]

    def load_wa(self, g):
        ws, bws = self.wslot()
        self.dma("sp", ws.rearrange("p (j m) -> p j m", j=4), self.WA[g].rearrange("j p m -> p j m"),
                 [self.bW[("WA", g)]], [bws], ("ld", bws.name))
        return ws.rearrange("p (j m) -> p j m", j=4), bws

    def proj_fm(self, wsv, bws, c, bank_i):
        bk = self.bank(bank_i)
        for dc in range(8):
            self.mm(bk, wsv[:, c, dc * 128:(dc + 1) * 128], self.HT[:, dc, :], dc == 0, dc == 7,
                    [bws, self.bHT], [self.pb[bank_i]])
        return bk

    def prep_items(self, b, t, specs, bankfn, vspec=None, pre=()):
        items = []
        st = {}

        def LD(tb):
            i = (t * 4 + tb) % 2
            self.dma("sp", self.XB[i], self.x[b, (t * 4 + tb) * 128:(t * 4 + tb + 1) * 128, :], [], [self.bXB[i]],
                     ("ld", self.bXB[i].name))

        def SQ(tb):
            i = (t * 4 + tb) % 2
            hb, bhb = self.next_hb()
            st[("hb", tb)] = (hb, bhb)
            st[("r", tb)] = self.rstd(self.XB[i], [self.bXB[i]], hb, bhb)

        def NRM(tb):
            i = (t * 4 + tb) % 2
            hb, bhb = st[("hb", tb)]
            r, bst = st[("r", tb)]
            xb = self.XB[i]
            gm = self.GMIX
            self.dve(lambda e: e.scalar_tensor_tensor(out=hb, in0=xb, scalar=r, in1=gm, op0=ALU.mult,
                                                      op1=ALU.mult), [self.bXB[i], bst, self.bG], [bhb])

        def TR(tb):
            hb, bhb = st[("hb", tb)]
            bi = bankfn()
            tp = self.bank(bi).bitcast(BF)
            for c in range(8):
                self.tr(tp[:, c * 128:(c + 1) * 128], hb[:, c * 128:(c + 1) * 128], [bhb], [self.pb[bi]])
            ht = self.HT
            self.dve(lambda e: e.tensor_copy(out=ht[:, :, tb * 128:(tb + 1) * 128],
                                             in_=tp.rearrange("p (c n) -> p c n", c=8)), [self.pb[bi]], [self.bHT])

        items += list(pre)
        items += [lambda: LD(0), lambda: LD(1), lambda: SQ(0), lambda: SQ(1), lambda: NRM(0), lambda: LD(2),
                  lambda: TR(0), lambda: NRM(1), lambda: LD(3), lambda: SQ(2), lambda: TR(1), lambda: NRM(2),
                  lambda: SQ(3), lambda: TR(2), lambda: NRM(3), lambda: TR(3)]
        for (groups, evac_fn) in specs:
            def LW(groups=groups):
                st[("w", tuple(groups))] = [self.load_wa(g) for g in groups]
            items.append(LW)
            for c in range(4):
                for gi in range(len(groups)):
                    def PJ(c=c, gi=gi, groups=groups, evac_fn=evac_fn):
                        wsv, bws = st[("w", tuple(groups))][gi]
                        bi = bankfn()
                        self.proj_fm(wsv, bws, c, bi)
                        evac_fn(c, gi, bi)
                    items.append(PJ)
        if vspec is not None:
            g, Vt, bV = vspec

            def LWV():
                st["wv"] = self.load_wa(g)
            items.append(LWV)
            for tb in range(4):
                def PV(tb=tb):
                    wsv, bws = st["wv"]
                    bi = bankfn()
                    bk = self.bank(bi)
                    for j in range(4):
                        for dc in range(8):
                            self.mm(bk[:, j * 128:(j + 1) * 128], self.HT[:, dc, tb * 128:(tb + 1) * 128],
                                    wsv[:, j, dc * 128:(dc + 1) * 128], dc == 0, dc == 7, [bws, self.bHT],
                                    [self.pb[bi]])
                    vdst = Vt[:, t * 4 + tb, :]
                    self.dve(lambda e: e.tensor_copy(out=vdst, in_=bk), [self.pb[bi]], [bV])
                items.append(PV)
        return items

    def wc_pieces(self):
        A = self.A
        F = A.f32(1024)
        H = A.bf(1024)
        bF, bH = self.buf("WCF"), self.buf("WCH")
        out = []

        def piece(src_ap, fview, conv, dst_ap, hview):
            def ld():
                self.dma("sp", fview(F), src_ap, [], [bF], ("ld", "WCF"))

            def cv():
                self.dve(conv, [bF], [bH])

            def st():
                self.dma("sp", dst_ap, hview(H), [bH], [self.bWC], ("wstc",))
            out.extend([ld, None, cv, st])

        straight = lambda e: e.tensor_copy(out=H, in_=F)
        w_in_v = self.w_in.rearrange("(c p) n -> p c n", p=128)
        v8 = lambda T: T.rearrange("p (c n) -> p c n", c=8)
        ident_h = lambda T: T
        for j in range(8):
            for k in range(2):
                col0 = 3072 + 1024 * k + 128 * j
                piece(w_in_v[:, :, col0:col0 + 128], v8, straight, self.WG[j][:, 1024 * k:1024 * (k + 1)], ident_h)
        for k, wsrc in enumerate((self.wbs, self.wbd)):
            wv = wsrc.rearrange("(c p) n -> p c n", p=128)
            for q in range(4):
                conv = lambda e: e.tensor_copy(out=H.rearrange("p (j c n) -> p j c n", j=2, c=4),
                                               in_=F.rearrange("p (c j n) -> p j c n", c=4, j=2))
                off = 2048 + 512 * k
                piece(wv[:, :, 256 * q:256 * (q + 1)], lambda T: T.rearrange("p (c n) -> p c n", c=4), conv,
                      self.WG[2 * q:2 * q + 2, :, off:off + 512].rearrange("j p m -> p j m"),
                      lambda T: T.rearrange("p (j m) -> p j m", j=2))
        wo_v = self.wo.rearrange("(c p) n -> p c n", p=128)
        v2 = lambda T: T.rearrange("p (c n) -> p c n", c=2)
        for ch in range(2):
            for q in range(4):
                piece(wo_v[:, 2 * q:2 * q + 2, ch * 512:(ch + 1) * 512], v2, straight,
                      self.WO[ch][:, 2 * q * 512:(2 * q + 2) * 512], ident_h)
        for f in range(NFC):
            for k, wsrc in enumerate((self.wg, self.wu)):
                wv = wsrc.rearrange("(c p) n -> p c n", p=128)
                piece(wv[:, :, 128 * f:128 * (f + 1)], v8, straight, self.WF[f, k], ident_h)
        wd_v = self.wd.rearrange("(f p) n -> p f n", p=128)
        for ch in range(2):
            for f in range(0, NFC, 2):
                piece(wd_v[:, f:f + 2, ch * 512:(ch + 1) * 512], v2, straight, self.WD[ch][:, f * 512:(f + 2) * 512],
                      ident_h)
        return out

    def pass_a(self, b):
        A, S, NT = self.A, self.S, self.NT
        A.off = self.persist_mark
        KT = A.bf(4 * S).rearrange("p (c s) -> p c s", c=4)
        Vt = A.bf(S * 4).rearrange("p (k n) -> p k n", n=512)
        bKT = [self.buf(f"KT{t}") for t in range(NT)]
        bV = [self.buf(f"V{t}") for t in range(NT)]
        self.alloc_common(2)
        QTs = [A.bf(4 * 512).rearrange("p (c s) -> p c s", c=4) for _ in range(2)]
        bQTs = [self.buf(f"QT{i}") for i in range(2)]
        Es = [A.f32(1024).rearrange("p (h s) -> p h s", h=2) for _ in range(2)]
        bEs = [self.buf(f"E{i}") for i in range(2)]
        SP = [A.bf(1024).rearrange("p (h s) -> p h s", h=2) for _ in range(2)]
        bSP = [self.buf(f"SP{i}") for i in range(2)]
        WT = [A.bf(1024).rearrange("p (h s) -> p h s", h=2) for _ in range(2)]
        bWT = [self.buf(f"WT{i}") for i in range(2)]
        R = [A.bf(1024).rearrange("p (h s) -> p h s", h=2) for _ in range(2)]
        bR = [self.buf(f"R{i}") for i in range(2)]
        wcp = self.wc_pieces() if b == 0 else []
        wcp_pos = 0
        fg = {"n": 0}

        def fg_bank():
            fg["n"] += 1
            return 4 + fg["n"] % 4

        def mk_items(t, bankfn):
            qt, bqt = QTs[t % 2], bQTs[t % 2]

            def evq(c, gi, bi):
                bk = self.bank(bi)
                self.dve(lambda e: e.tensor_copy(out=qt[:, c, :], in_=bk), [self.pb[bi]], [bqt])

            def evk(c, gi, bi):
                bk = self.bank(bi)
                dst = KT[:, c, t * 512:(t + 1) * 512]
                self.dve(lambda e: e.tensor_copy(out=dst, in_=bk), [self.pb[bi]], [bKT[t]])

            return self.prep_items(b, t, [([0], evq), ([1], evk)], bankfn, (2, Vt, bV[t]))

        for it in mk_items(0, fg_bank):
            it()
        for t in range(NT):
            QT, bQT = QTs[t % 2], bQTs[t % 2]
            bg = mk_items(t + 1, lambda: 7) if t + 1 < NT else []
            bg_pos = 0
            nkb = 4 * t + 4
            steps = [(c, kb) for c in range(4) for kb in range(nkb - 1, -1, -1)]
            n = len(steps)

            def info(i):
                c, kb = steps[i]
                j = kb - 4 * t
                c0 = j * 128 if j >= 0 else 0
                return c, kb, j >= 0, c0, kb == nkb - 1, kb == 0

            def zzv(i):
                return self.PS[i % 3].rearrange("p (h s) -> p h s", h=2)

            def zzb(i):
                return [self.pb[2 * (i % 3)], self.pb[2 * (i % 3) + 1]]

            def S_(i):
                c, kb, dg, c0, first, last = info(i)
                zz = zzv(i)
                for h in range(2):
                    self.mm(zz[:, h, c0:512], KT[64 * h:64 * h + 64, c, kb * 128:(kb + 1) * 128],
                            QT[64 * h:64 * h + 64, c, c0:512], True, True, [bKT[kb // 4], bQT],
                            [self.pb[2 * (i % 3) + h]])
                if dg:
                    for h in range(2):
                        self.mm(zz[:, h, c0:c0 + 128], self.IDENT, self.SBMASK, False, True, [self.bCM],
                                [self.pb[2 * (i % 3) + h]], skip=True)

            def zero_left(buf_ap, bbuf, c0):
                if c0 > 0:
                    self.pool(lambda e: e.memset(buf_ap[:, :, 0:c0], 0.0), [], [bbuf])

            S_(0)
            for i in range(n + 2):
                if i < n:
                    c, kb, dg, c0, first, last = info(i)
                    zz = zzv(i)
                    E, bE = Es[i % 2], bEs[i % 2]
                    self.act(E[:, :, c0:512], zz[:, :, c0:512], AF.Exp, zzb(i), [bE], scale=0.125)
                if 2 <= i:
                    pc, pkb, pdg, pc0, pfirst, plast = info(i - 2)
                    pzz = zzv(i - 2)
                    wt, bwt = WT[(i - 2) % 2], bWT[(i - 2) % 2]
                    if pdg:
                        zero_left(wt, bwt, pc0)
                    self.act(wt[:, :, pc0:512], pzz[:, :, pc0:512], AF.Exp, zzb(i - 2), [bwt], scale=0.125)
                if i < n:
                    sp, bsp = SP[i % 2], bSP[i % 2]
                    if dg:
                        zero_left(sp, bsp, c0)
                    self.act(sp[:, :, c0:512], E[:, :, c0:512], AF.Ln, [bE], [bsp], bias=1.0)
                if i + 1 < n:
                    S_(i + 1)
                if 2 <= i:
                    ob_i = 6
                    ob = self.bank(ob_i)
                    for h in range(2):
                        self.mm(ob[64 * h:64 * h + 64, :], Vt[:, pkb, pc * 128 + 64 * h:pc * 128 + 64 * h + 64],
                                wt[:, h, :], pfirst, plast, [bV[pkb // 4], bwt], [self.pb[ob_i]])
                    if plast:
                        odst = self.OT[:, pc, t * 512:(t + 1) * 512]
                        self.dve(lambda e, ob=ob, odst=odst: e.tensor_copy(out=odst, in_=ob),
                                 [self.pb[ob_i]], [self.bOT[t]])
                if i < n:
                    r, br = R[c % 2], bR[c % 2]
                    for h in range(2):
                        self.mm(zz[:, h, c0:512], self.NEGTRI8, sp[:, h, c0:512], False, True,
                                [bsp, self.bCM], [self.pb[2 * (i % 3) + h]], skip=True)
                        if not first:
                            self.mm(zz[:, h, c0:512], self.NEGONES8, r[:, h, c0:512], False, True,
                                    [br, self.bCM], [self.pb[2 * (i % 3) + h]], skip=True)
                    if not last:
                        if first:
                            self.dve(lambda e, r=r, sp=sp: e.tensor_copy(out=r, in_=sp), [bsp], [br])
                        else:
                            self.dve(lambda e, r=r, sp=sp: e.tensor_tensor(out=r, in0=r, in1=sp, op=ALU.add),
                                     [bsp, br], [br])
                if wcp_pos < len(wcp):
                    if wcp[wcp_pos] is not None:
                        wcp[wcp_pos]()
                    wcp_pos += 1
                if bg and i >= 1:
                    per = -(-len(bg) // max(1, n - 3))
                    for _ in range(per):
                        if bg_pos < len(bg):
                            bg[bg_pos]()
                            bg_pos += 1
            while bg_pos < len(bg):
                bg[bg_pos]()
                bg_pos += 1
        while wcp_pos < len(wcp):
            if wcp[wcp_pos] is not None:
                wcp[wcp_pos]()
            wcp_pos += 1
        self.s.barrier()

    def pass_b(self, b):
        A, S, NT = self.A, self.S, self.NT
        A.off = self.persist_mark
        KT = A.bf(4 * S).rearrange("p (c s) -> p c s", c=4)
        Vt = A.bf(S * 4).rearrange("p (k n) -> p k n", n=512)
        bKT = [self.buf(f"KTd{t}") for t in range(NT)]
        bV = [self.buf(f"Vd{t}") for t in range(NT)]
        self.alloc_common(3)
        QTs = [A.bf(4 * 512).rearrange("p (c s) -> p c s", c=4) for _ in range(2)]
        bQTs = [self.buf(f"QTd{i}") for i in range(2)]
        COS = A.f32(512)
        SIN = A.f32(512)
        bCOS = self.buf("COS")
        bSIN = self.buf("SIN")
        ET = [A.bf(1024).rearrange("p (h s) -> p h s", h=2) for _ in range(2)]
        bET = [self.buf(f"ET{i}") for i in range(2)]
        RD, XS, YS = [A.f32(512) for _ in range(3)]
        bN = self.buf("NRM")
        bRP = self.buf("ROPE")
        RT = A.f32(512)
        SQ = YS.bitcast(BF)[:, 0:512]
        XL, YL = A.bf(512), A.bf(512)

        def hi(ap):
            return ap.bitcast(BF).rearrange("p (n two) -> p n two", two=2)[:, :, 1]
        fg = {"n": 0}

        def fg_bank():
            fg["n"] += 1
            return 4 + fg["n"] % 4

        def mk_items(t, bankfn):
            qt, bqt = QTs[t % 2], bQTs[t % 2]

            def ev(dst, dbuf):
                def f(c, gi, bi):
                    bk = self.bank(bi)
                    if gi == 0:
                        self.dve(lambda e: e.tensor_tensor(out=RT, in0=bk, in1=COS, op=ALU.mult),
                                 [self.pb[bi], bCOS], [bRP])
                    else:
                        d = dst(c)
                        self.dve(lambda e: e.tensor_tensor(out=bk, in0=bk, in1=SIN, op=ALU.mult),
                                 [self.pb[bi], bSIN], [self.pb[bi]])
                        self.dve(lambda e: e.tensor_tensor(out=d, in0=bk, in1=RT, op=ALU.add),
                                 [self.pb[bi], bRP], [dbuf])
                return f

            def ldcs():
                self.dma("sp", COS, self.cosT[:, t * 512:(t + 1) * 512], [], [bCOS], ("ld", "COS"))
                self.dma("sp", SIN, self.sinT[:, t * 512:(t + 1) * 512], [], [bSIN], ("ld", "SIN"))

            return self.prep_items(b, t, [([3, 6], ev(lambda c: qt[:, c, :], bqt)),
                                         ([4, 7], ev(lambda c: KT[:, c, t * 512:(t + 1) * 512], bKT[t]))],
                                   bankfn, (5, Vt, bV[t]), pre=[ldcs])

        for it in mk_items(0, fg_bank):
            it()
        for t in range(NT):
            QT, bQT = QTs[t % 2], bQTs[t % 2]
            bg = mk_items(t + 1, lambda: 7) if t + 1 < NT else []
            bg_pos = 0
            nkb = 4 * t + 4
            steps = [(hd, kb) for hd in range(4) for kb in range(nkb)]
            n = len(steps)

            def info(i):
                hd, kb = steps[i]
                j = kb - 4 * t
                c0 = j * 128 if j >= 0 else 0
                return hd, kb, j >= 0, c0, kb == 0, kb == nkb - 1

            def S_(i):
                hd, kb, dg, c0, first, last = info(i)
                zz = self.PS[i % 2].rearrange("p (h s) -> p h s", h=2)
                for h in range(2):
                    self.mm(zz[:, h, c0:512], KT[64 * h:64 * h + 64, hd, kb * 128:(kb + 1) * 128],
                            QT[64 * h:64 * h + 64, hd, c0:512], True, True, [bKT[kb // 4], bQT],
                            [self.pb[2 * (i % 2) + h]])
                if dg:
                    for h in range(2):
                        self.mm(zz[:, h, c0:c0 + 128], self.IDENT, self.DAMASK, False, True, [self.bCM],
                                [self.pb[2 * (i % 2) + h]], skip=True)

            deferred = []
            S_(0)
            for i in range(n):
                hd, kb, dg, c0, first, last = info(i)
                zz = self.PS[i % 2].rearrange("p (h s) -> p h s", h=2)
                et, bet = ET[i % 2], bET[i % 2]
                self.act(et[:, :, c0:512], zz[:, :, c0:512], AF.Exp, [self.pb[2 * (i % 2)], self.pb[2 * (i % 2) + 1]],
                         [bet], scale=0.125)
                if i + 1 < n:
                    S_(i + 1)
                for half, bi in ((0, 4), (1, 5)):
                    vv = Vt[:, kb, hd * 128 + 64 * half:hd * 128 + 64 * half + 64]
                    for h in range(2):
                        self.mm(self.bank(bi)[64 * h:64 * h + 64, c0:512], vv, et[:, h, c0:512], first, dg,
                                [bV[kb // 4], bet], [self.pb[bi]], skip=(c0 > 0))
                for h in range(2):
                    self.mm(self.bank(6)[64 * h:64 * h + 64, c0:512], self.ONES[:, 0:64], et[:, h, c0:512], first, dg,
                            [self.bCM, bet], [self.pb[6]], skip=(c0 > 0))
                if last:
                    self.act(RD, self.bank(6), AF.Ln, [self.pb[6]], [bN])
                    self.dve(lambda e: e.tensor_copy(out=XS, in_=self.bank(4)), [self.pb[4]], [bN])
                    self.dve(lambda e: e.tensor_copy(out=YS, in_=self.bank(5)), [self.pb[5]], [bN])

                    def stage1():
                        self.act(RD, RD, AF.Exp, [bN], [bN], scale=-1.0)
                        for T_, L_ in ((XS, XL), (YS, YL)):
                            self.dve(lambda e, T_=T_: e.scalar_tensor_tensor(out=T_, in0=T_, scalar=self.LAMCOL, in1=RD,
                                                                             op0=ALU.mult, op1=ALU.mult),
                                     [bN, self.bSM], [bN])
                            self.dve(lambda e, T_=T_, L_=L_: e.tensor_tensor(out=L_, in0=T_, in1=hi(T_), op=ALU.subtract),
                                     [bN], [bN])

                    def stage2():
                        self.mm(self.bank(7), self.PX, hi(XS), True, False, [bN, self.bCM], [self.pb[7]])
                        self.mm(self.bank(7), self.PY, hi(YS), False, False, [bN, self.bCM], [self.pb[7]])
                        self.mm(self.bank(7), self.PX, XL, False, False, [bN, self.bCM], [self.pb[7]])
                        self.mm(self.bank(7), self.PY, YL, False, True, [bN, self.bCM], [self.pb[7]])
                        self.act(SQ, self.bank(7), AF.Square, [self.pb[7]], [bN])
                        self.dve(lambda e: e.tensor_copy(out=XS, in_=self.bank(7)), [self.pb[7]], [bN])

                    def stage3(hd=hd):
                        self.mm(self.bank(7), self.ONESDIV, SQ, True, True, [bN, self.bCM], [self.pb[7]])
                        self.act(RD, self.bank(7), AF.Ln, [self.pb[7]], [bN], bias=SUBLN_EPS)
                        self.act(RD, RD, AF.Exp, [bN], [bN], scale=-0.5)
                        odst = self.OT[:, 4 + hd, t * 512:(t + 1) * 512]
                        self.dve(lambda e, odst=odst: e.scalar_tensor_tensor(
                            out=odst, in0=XS, scalar=self.GS, in1=RD,
                            op0=ALU.mult, op1=ALU.mult), [bN, self.bSM], [self.bOT[t]])

                    dd = [min(d, nkb - 1) for d in (1, 5, 8)]
                    deferred.extend([(i + dd[0], stage1), (i + dd[1], stage2), (i + dd[2], stage3)])
                    deferred.sort(key=lambda x: x[0])
                while deferred and deferred[0][0] <= i:
                    deferred.pop(0)[1]()
                if bg and i >= 1:
                    per = -(-len(bg) // max(1, n - 3))
                    for _ in range(per):
                        if bg_pos < len(bg):
                            bg[bg_pos]()
                            bg_pos += 1
            while deferred:
                deferred.pop(0)[1]()
            while bg_pos < len(bg):
                bg[bg_pos]()
                bg_pos += 1
        self.s.barrier()

    def norm_parts(self, x_ap, xbufs, gtile, HTdst, bHTdst, tb, bank_i):
        st = {}

        def sq():
            st["hb"] = self.next_hb()
            st["r"] = self.rstd(x_ap, xbufs, *st["hb"])

        def nrm():
            hb, bhb = st["hb"]
            r, bst = st["r"]
            self.dve(lambda e: e.scalar_tensor_tensor(out=hb, in0=x_ap, scalar=r, in1=gtile, op0=ALU.mult, op1=ALU.mult),
                     list(xbufs) + [bst, self.bG], [bhb])

        def tr():
            hb, bhb = st["hb"]
            tp = self.bank(bank_i).bitcast(BF)
            for c in range(8):
                self.tr(tp[:, c * 128:(c + 1) * 128], hb[:, c * 128:(c + 1) * 128], [bhb], [self.pb[bank_i]])
            self.dve(lambda e: e.tensor_copy(out=HTdst[:, :, tb * 128:(tb + 1) * 128],
                                             in_=tp.rearrange("p (c n) -> p c n", c=8)), [self.pb[bank_i]], [bHTdst])
        return sq, nrm, tr

    def pass_c(self, b):
        A, S, NT = self.A, self.S, self.NT
        A.off = self.persist_mark
        self.alloc_common(4, with_xb=False, extra_g=True)
        GM, GF, GN = self.GMIX, self.GFFN, self.GFIN
        HTf, bHTf = self.HT, self.bHT
        HTm = A.bf(8 * 512).rearrange("p (c s) -> p c s", c=8)
        bHTm = self.buf("HTm")
        XTs = [A.f32(4 * D).rearrange("p (k d) -> p k d", k=4) for _ in range(2)]
        bXTs = [[self.buf(f"XT{i}_{k}") for k in range(4)] for i in range(2)]
        MIX = A.bf(8 * 512).rearrange("p (c s) -> p c s", c=8)
        bMIX = self.buf("MIX")
        ACTT = A.bf(NFC * 512).rearrange("p (f s) -> p f s", f=NFC)
        bACT = self.buf("ACTT")
        S1, S2, M1, M2 = [A.f32(512) for _ in range(4)]
        bSG = self.buf("SG")
        SGL = [A.f32(512) for _ in range(2)]
        bSGL = [self.buf(f"SGL{i}") for i in range(2)]

        def mix_items(t):
            XT, bX = XTs[t % 2], bXTs[t % 2]
            items = []
            for tb in range(4):
                def LD(tb=tb):
                    self.dma("pool", XT[:, tb, :], self.x[b, t * 512 + tb * 128:t * 512 + (tb + 1) * 128, :], [],
                             [bX[tb]], ("ld", f"XT{t % 2}_{tb}"))
                items.append(LD)
            parts = [self.norm_parts(XT[:, tb, :], [bX[tb]], GM, HTm, bHTm, tb, 6 + (tb % 2)) for tb in range(4)]
            sq, nrm, tr = zip(*parts)
            items += [sq[0], sq[1], nrm[0], tr[0], nrm[1], sq[2], tr[1], nrm[2], sq[3], tr[2], nrm[3], tr[3]]
            return items

        for it in mix_items(0):
            it()
        for t in range(NT):
            XT, bXTl = XTs[t % 2], bXTs[t % 2]
            tok = slice(t * 512, (t + 1) * 512)
            for j in range(8):
                ws, bws = self.wslot()
                self.dma("sp", ws[:, 0:3072], self.WG[j], [self.bWC], [bws], ("ld", bws.name))
                b0 = 4 * (j % 2)
                for k in range(2):
                    gb, bbk = b0 + k, b0 + 2 + k
                    for dc in range(8):
                        self.mm(self.bank(gb), ws[:, k * 1024 + dc * 128:k * 1024 + (dc + 1) * 128], HTm[:, dc, :],
                                dc == 0, dc == 7, [bws, bHTm], [self.pb[gb]])
                    for c in range(4):
                        self.mm(self.bank(bbk), ws[:, 2048 + k * 512 + c * 128:2048 + k * 512 + (c + 1) * 128],
                                self.OT[:, 4 * k + c, tok], c == 0, c == 3, [bws, self.bOT[t]], [self.pb[bbk]])
                g1, g2, p1, p2 = [self.bank(b0 + q) for q in range(4)]
                self.act(S1, g1, AF.Sigmoid, [self.pb[b0]], [bSG])
                self.act(S2, g2, AF.Sigmoid, [self.pb[b0 + 1]], [bSG])
                self.dve(lambda e, p1=p1: e.tensor_tensor(out=M1, in0=p1, in1=S1, op=ALU.mult), [self.pb[b0 + 2], bSG], [bSG])
                self.dve(lambda e, p2=p2: e.tensor_tensor(out=M2, in0=p2, in1=S2, op=ALU.mult), [self.pb[b0 + 3], bSG], [bSG])
                self.dve(lambda e, j=j: e.tensor_tensor(out=MIX[:, j, :], in0=M1, in1=M2, op=ALU.add), [bSG], [bMIX])
            wo = []
            for ch in range(2):
                ws, bws = self.wslot()
                self.dma("sp", ws, self.WO[ch], [self.bWC], [bws], ("ld", bws.name))
                wo.append((ws.rearrange("p (c n) -> p c n", c=8), bws))
            fparts = [self.norm_parts(XT[:, tb, :], [bXTl[tb]], GF, HTf, bHTf, tb, 6 + (tb % 2)) for tb in range(4)]
            for tb in range(4):
                for ch in range(2):
                    bi = (2 * tb + ch) % 4
                    wv, bws = wo[ch]
                    for c in range(8):
                        self.mm(self.bank(bi), MIX[:, c, tb * 128:(tb + 1) * 128], wv[:, c, :], c == 0, c == 7,
                                [bws, bMIX], [self.pb[bi]])
                    xs = XT[:, tb, ch * 512:(ch + 1) * 512]
                    bk = self.bank(bi)
                    self.dve(lambda e, xs=xs, bk=bk: e.tensor_tensor(out=xs, in0=bk, in1=xs, op=ALU.add),
                             [self.pb[bi], bXTl[tb]], [bXTl[tb]])
                fparts[tb][0]()
                fparts[tb][1]()
                if tb >= 1:
                    fparts[tb - 1][2]()
            fparts[3][2]()
            bg = mix_items(t + 1) if t + 1 < NT else []
            bg_pos = 0
            for fp in range(NFC // 2):
                ws, bws = self.wslot()
                self.dma("sp", ws.rearrange("p (a m) -> p a m", a=4),
                         self.WF[2 * fp:2 * fp + 2].rearrange("f k p m -> p (f k) m"),
                         [self.bWC], [bws], ("ld", bws.name))
                wv = ws.rearrange("p (a m) -> p a m", a=4)
                for i2 in range(2):
                    f = 2 * fp + i2
                    gb, ub = 2 * (f % 2), 2 * (f % 2) + 1
                    for dc in range(8):
                        self.mm(self.bank(gb), wv[:, 2 * i2, dc * 128:(dc + 1) * 128], HTf[:, dc, :], dc == 0, dc == 7,
                                [bws, bHTf], [self.pb[gb]])
                    for dc in range(8):
                        self.mm(self.bank(ub), wv[:, 2 * i2 + 1, dc * 128:(dc + 1) * 128], HTf[:, dc, :], dc == 0,
                                dc == 7, [bws, bHTf], [self.pb[ub]])
                    sg, bsg = SGL[f % 2], bSGL[f % 2]
                    gbk, ubk = self.bank(gb), self.bank(ub)
                    adst = ACTT[:, f, :]
                    self.act(sg, gbk, AF.Silu, [self.pb[gb]], [bsg])
                    self.dve(lambda e, sg=sg, ubk=ubk, adst=adst: e.tensor_tensor(out=adst, in0=ubk, in1=sg, op=ALU.mult),
                             [self.pb[ub], bsg], [bACT])
                    if f >= 2 and bg_pos < len(bg):
                        bg[bg_pos]()
                        bg_pos += 1
            while bg_pos < len(bg):
                bg[bg_pos]()
                bg_pos += 1
            for ch in range(2):
                for gi, (f0, f1) in enumerate(self.wd_groups):
                    nf = f1 - f0
                    ws, bws = self.wslot()
                    self.dma("sp", ws[:, 0:nf * 512], self.WD[ch][:, f0 * 512:f1 * 512], [self.bWC], [bws],
                             ("ld", bws.name))
                    wv = ws[:, 0:nf * 512].rearrange("p (f n) -> p f n", f=nf)
                    for tb in range(4):
                        bi = 4 + tb
                        for f in range(f0, f1):
                            self.mm(self.bank(bi), ACTT[:, f, tb * 128:(tb + 1) * 128], wv[:, f - f0, :], f == 0,
                                    f == NFC - 1, [bws, bACT], [self.pb[bi]])
                for tb in range(4):
                    bi = 4 + tb
                    xs = XT[:, tb, ch * 512:(ch + 1) * 512]
                    bk = self.bank(bi)
                    self.dve(lambda e, xs=xs, bk=bk: e.tensor_tensor(out=xs, in0=bk, in1=xs, op=ALU.add),
                             [self.pb[bi], bXTl[tb]], [bXTl[tb]])
            for tb in range(4):
                jk, bjk = self.next_hb()
                xr = XT[:, tb, :]
                r, bst = self.rstd(xr, [bXTl[tb]], jk, bjk)
                self.dve(lambda e, xr=xr, r=r: e.scalar_tensor_tensor(out=xr, in0=xr, scalar=r, in1=GN, op0=ALU.mult,
                                                                      op1=ALU.mult),
                         [bXTl[tb], bst, self.bG], [bXTl[tb]])
                st = self.dma("pool", self.out[b, t * 512 + tb * 128:t * 512 + (tb + 1) * 128, :], xr,
                              [bXTl[tb]], [], ("st", f"XT{t % 2}_{tb}"))
                self.s.final_dma.append(st)
        self.s.barrier()


def host_consts(S):
    jj, kk = np.meshgrid(np.arange(128), np.arange(128), indexing="ij")
    ident = np.eye(128, dtype=np.float32)
    negtri8 = np.where(jj >= kk, -8.0, 0.0).astype(np.float32)
    negones8 = np.full((128, 128), -8.0, np.float32)
    ones = np.ones((128, 128), np.float32)
    onesdiv = np.full((128, 128), 1.0 / 128.0, np.float32)
    sbmask = np.where(jj < kk, 0.0, -30000.0).astype(np.float32)
    damask = np.where((jj < 64) | (kk >= 64), 0.0, -30000.0).astype(np.float32)
    p1 = ((jj == kk) & (kk < 64)).astype(np.float32)
    p2 = ((jj == kk + 64) & (kk < 64)).astype(np.float32)
    p3 = ((jj == kk - 64) & (kk >= 64)).astype(np.float32)
    p4 = ((jj == kk) & (kk >= 64)).astype(np.float32)
    cmat = np.concatenate([ident, negtri8, negones8, ones, onesdiv, sbmask, damask, p1 + p2, p3 + p4], axis=1)
    inv_freq = (1.0 / (np.float32(10000.0) ** (np.arange(0, 64, 2, dtype=np.float32) / np.float32(64)))).astype(np.float32)
    ang = np.arange(S, dtype=np.float32)[:, None] * inv_freq[None, :]
    cos = np.cos(ang).astype(np.float32).T
    sin = np.sin(ang).astype(np.float32).T
    cosT = np.ascontiguousarray(np.tile(cos, (4, 1)))
    sinT = np.ascontiguousarray(np.tile(sin, (4, 1)))
    return cmat, cosT, sinT


_NC_CACHE = {}


def make_in_maps(inputs, S, NB, ncores):
    cmat, cosT, sinT = host_consts(S)
    f = lambda a: np.ascontiguousarray(np.asarray(a, dtype=np.float32))
    x = f(inputs["x"])
    shared = {
        "w_in": f(inputs["w_in"][0]),
        "wbs": f(inputs["w_branch_sb"][0]),
        "wbd": f(inputs["w_branch_da"][0]),
        "wo": f(inputs["w_out"][0]),
        "wg": f(inputs["w_ffn_gate"][0]),
        "wu": f(inputs["w_ffn_up"][0]),
        "wd": f(inputs["w_ffn_down"][0]),
        "gmix": f(inputs["g_mix"]).reshape(1, D),
        "gffn": f(inputs["g_ffn"]).reshape(1, D),
        "gfin": f(inputs["g_final"]).reshape(1, D),
        "gsub": f(inputs["g_subln"]).reshape(1, 128),
        "lamv": np.ascontiguousarray(np.stack([f(inputs["lambda_q1"]).reshape(64), f(inputs["lambda_k1"]).reshape(64),
                                               f(inputs["lambda_q2"]).reshape(64), f(inputs["lambda_k2"]).reshape(64)])),
        "cmat": cmat, "cosT": cosT, "sinT": sinT,
    }
    maps = []
    for i in range(ncores):
        m = dict(shared)
        m["x"] = np.ascontiguousarray(x[i * NB:(i + 1) * NB])
        maps.append(m)
    return maps


def kernel(**inputs):
    x = np.asarray(inputs["x"])
    B, S, _ = x.shape
    NB = B // NCORES
    key = (S, NB)
    if key not in _NC_CACHE:
        _NC_CACHE[key] = KB(S, NB).build()
    nc = _NC_CACHE[key]
    in_maps = make_in_maps(inputs, S, NB, NCORES)
    res = run_bass_kernel_spmd(nc, in_maps, core_ids=list(range(NCORES)))
    return np.concatenate([np.asarray(r["out"]) for r in res.results], axis=0).astype(np.float32)
```

```python
import math
from contextlib import ExitStack

import numpy as np
import concourse.bass as bass
import concourse.mybir as mybir
from concourse.bass_utils import run_bass_kernel_spmd

F32 = mybir.dt.float32
BF = mybir.dt.bfloat16
AF = mybir.ActivationFunctionType
ALU = mybir.AluOpType
AX = mybir.AxisListType

D = 1024
DC = 8
DFF = 2816
NFC = 22
INW = 5120
NCORES = 8
SEQ = 4096
BATCH = 16
NORM_EPS = 1e-6
SUBLN_EPS = 1e-5
LAMBDA_INIT = 0.8 - 0.6 * math.exp(-0.3 * 0)

SEM_CAP = 30000
STRICT_SAME_ENGINE = True
ENGS = ("pe", "act", "dve", "pool", "sp")


class Buf:
    __slots__ = ("name", "lw", "rd")

    def __init__(self, name):
        self.name = name
        self.lw = None
        self.rd = []


class Op:
    __slots__ = ("eng", "fn", "pos", "waits", "marked", "sem", "val", "dma_key", "dma_val", "dma_sem")

    def __init__(self, eng, fn):
        self.eng = eng
        self.fn = fn
        self.pos = -1
        self.waits = []
        self.marked = False
        self.sem = None
        self.val = 0
        self.dma_key = None
        self.dma_val = 0
        self.dma_sem = None


class Sched:
    def __init__(self, nc):
        self.nc = nc
        self.streams = {e: [] for e in ENGS}
        self.seen = {e: {p: -1 for p in ENGS} for e in ENGS}
        self.seen_dma = {e: {} for e in ENGS}
        self.dma_cnt = {}
        self.final_dma = []

    def add(self, eng, fn, reads=(), writes=(), dma_key=None):
        op = Op(eng, fn)
        op.pos = len(self.streams[eng])
        is_dma = dma_key is not None
        if is_dma:
            ep, cnt = self.dma_cnt.get(dma_key, (0, 0))
            if cnt + 16 > SEM_CAP:
                ep, cnt = ep + 1, 0
            cnt += 16
            self.dma_cnt[dma_key] = (ep, cnt)
            op.dma_key = (dma_key, ep)
            op.dma_val = cnt
        deps = {}
        for b in reads:
            if b.lw is not None:
                deps[id(b.lw)] = (b.lw, True)
        for b in writes:
            if b.lw is not None and id(b.lw) not in deps:
                deps[id(b.lw)] = (b.lw, False)
            for r in b.rd:
                if id(r) not in deps:
                    deps[id(r)] = (r, False)
        best = {}
        for p, raw in deps.values():
            if p is op:
                continue
            if p.dma_key is not None:
                k = p.dma_key
                if self.seen_dma[eng].get(k, 0) >= p.dma_val:
                    continue
                self.seen_dma[eng][k] = p.dma_val
                op.waits.append(p)
                continue
            if p.eng == eng and not is_dma:
                if eng == "pe" or (not raw and not STRICT_SAME_ENGINE):
                    continue
            if p.eng not in best or best[p.eng].pos < p.pos:
                best[p.eng] = p
        for pe_, p in best.items():
            if self.seen[eng][pe_] >= p.pos:
                continue
            self.seen[eng][pe_] = p.pos
            p.marked = True
            op.waits.append(p)
        for b in reads:
            b.rd.append(op)
        for b in writes:
            b.lw = op
            b.rd = []
        self.streams[eng].append(op)
        return op

    def barrier(self):
        lasts = {}
        for e in ENGS:
            if e == "sp":
                continue
            for o in reversed(self.streams[e]):
                if o.dma_key is None and o.fn is not None:
                    lasts[e] = o
                    break
        dlast = {}
        for e in ENGS:
            for o in self.streams[e]:
                if o.dma_key is not None:
                    dlast[o.dma_key] = o
        for e in ENGS:
            op = Op(e, None)
            op.pos = len(self.streams[e])
            for pe_, p in lasts.items():
                if pe_ == e or self.seen[e][pe_] >= p.pos:
                    continue
                self.seen[e][pe_] = p.pos
                p.marked = True
                op.waits.append(p)
            for k, p in dlast.items():
                if self.seen_dma[e].get(k, 0) >= p.dma_val:
                    continue
                self.seen_dma[e][k] = p.dma_val
                op.waits.append(p)
            if op.waits:
                self.streams[e].append(op)

    def emit(self, es):
        nc = self.nc
        for e in ENGS:
            cnt = 0
            sem = None
            n = 0
            for o in self.streams[e]:
                if o.marked:
                    if sem is None or cnt >= SEM_CAP:
                        sem = es.enter_context(nc.semaphore(f"s_{e}_{n}"))
                        n += 1
                        cnt = 0
                    cnt += 1
                    o.sem = sem
                    o.val = cnt
        dsem = {}
        for e in ENGS:
            for o in self.streams[e]:
                if o.dma_key is not None:
                    if o.dma_key not in dsem:
                        dsem[o.dma_key] = es.enter_context(nc.semaphore(f"d_{len(dsem)}"))
                    o.dma_sem = dsem[o.dma_key]
        streams = self.streams
        final_dma = self.final_dma

        def run(eng_name, eng):
            for o in streams[eng_name]:
                for p in o.waits:
                    if p.dma_key is not None:
                        eng.wait_ge(p.dma_sem, p.dma_val)
                    else:
                        eng.wait_ge(p.sem, p.val)
                if o.fn is None:
                    continue
                ins = o.fn(eng)
                if o.dma_key is not None:
                    ins.then_inc(o.dma_sem, 16)
                elif o.marked:
                    ins.then_inc(o.sem, 1)
            if eng_name == "sp":
                for o in final_dma:
                    eng.wait_ge(o.dma_sem, o.dma_val)

        block = es.enter_context(nc.Block())

        @block.tensor
        def _(eng):
            run("pe", eng)

        @block.scalar
        def _(eng):
            run("act", eng)

        @block.vector
        def _(eng):
            run("dve", eng)

        @block.gpsimd
        def _(eng):
            run("pool", eng)

        @block.sync
        def _(eng):
            run("sp", eng)


class Arena:
    def __init__(self, ap, nbytes):
        self.ap = ap
        self.nbytes = nbytes
        self.off = 0
        self.peak = 0

    def _take(self, nb):
        nb = (nb + 63) // 64 * 64
        o = self.off
        self.off += nb
        self.peak = max(self.peak, self.off)
        assert self.off <= self.nbytes, f"arena overflow {self.off} > {self.nbytes}"
        return o

    def bf(self, n):
        o = self._take(2 * n)
        return self.ap[:, o // 2:o // 2 + n]

    def f32(self, n):
        o = self._take(4 * n)
        return self.ap[:, o // 2:o // 2 + 2 * n].bitcast(F32)


class KB:
    def __init__(self, S, NB, arena_bytes=212480):
        self.S = S
        self.NB = NB
        self.NT = S // 512
        self.arena_bytes = arena_bytes
        self.nbuf = 0

    def buf(self, name):
        self.nbuf += 1
        return Buf(f"{name}#{self.nbuf}")

    def mm(self, out, lhsT, rhs, start, stop, reads, writes, skip=False):
        self.s.add("pe", lambda e: e.matmul(out, lhsT=lhsT, rhs=rhs, start=start, stop=stop,
                                            skip_group_check=skip), reads, writes)

    def tr(self, out, in_, reads, writes):
        ident = self.IDENT
        self.s.add("pe", lambda e: e.transpose(out, in_, ident), list(reads) + [self.bCM], writes)

    def act(self, out, in_, func, reads, writes, bias=None, scale=1.0, accum=None):
        def fn(e):
            kw = {}
            if bias is not None:
                kw["bias"] = bias
            if accum is not None:
                kw["accum_out"] = accum
            return e.activation(out=out, in_=in_, func=func, scale=scale, **kw)
        self.s.add("act", fn, reads, writes)

    def dve(self, fn, reads, writes):
        self.s.add("dve", fn, reads, writes)

    def pool(self, fn, reads, writes):
        self.s.add("pool", fn, reads, writes)

    def dma(self, q, out, in_, reads, writes, key):
        return self.s.add(q, lambda e: e.dma_start(out=out, in_=in_), reads, writes, dma_key=key)

    def bank(self, i):
        return self.PS[i // 2][:, (i % 2) * 512:(i % 2 + 1) * 512]

    def build(self):
        S, NB = self.S, self.NB
        nc = bass.Bass("TRN2", target_bir_lowering=False)
        self.nc = nc
        dt = nc.dram_tensor
        I = "ExternalInput"
        self.x = dt("x", [NB, S, D], F32, kind=I).ap()
        self.w_in = dt("w_in", [D, INW], F32, kind=I).ap()
        self.wbs = dt("wbs", [512, D], F32, kind=I).ap()
        self.wbd = dt("wbd", [512, D], F32, kind=I).ap()
        self.wo = dt("wo", [D, D], F32, kind=I).ap()
        self.wg = dt("wg", [D, DFF], F32, kind=I).ap()
        self.wu = dt("wu", [D, DFF], F32, kind=I).ap()
        self.wd = dt("wd", [DFF, D], F32, kind=I).ap()
        self.gmix = dt("gmix", [1, D], F32, kind=I).ap()
        self.gffn = dt("gffn", [1, D], F32, kind=I).ap()
        self.gfin = dt("gfin", [1, D], F32, kind=I).ap()
        self.gsub = dt("gsub", [1, 128], F32, kind=I).ap()
        self.lamv = dt("lamv", [4, 64], F32, kind=I).ap()
        self.cmat = dt("cmat", [128, 9 * 128], F32, kind=I).ap()
        self.cosT = dt("cosT", [128, S], F32, kind=I).ap()
        self.sinT = dt("sinT", [128, S], F32, kind=I).ap()
        self.out = dt("out", [NB, S, D], F32, kind="ExternalOutput").ap()
        self.WA = dt("WA", [8, 4, 128, 1024], BF, kind="Internal").ap()
        self.WG = dt("WG", [8, 128, 3072], BF, kind="Internal").ap()
        self.WO = dt("WO", [2, 128, 4096], BF, kind="Internal").ap()
        self.WF = dt("WF", [NFC, 2, 128, 1024], BF, kind="Internal").ap()
        self.WD = dt("WD", [2, 128, NFC * 512], BF, kind="Internal").ap()

        with ExitStack() as es:
            AR = es.enter_context(nc.sbuf_tensor("AR", [128, self.arena_bytes // 2], BF))
            self.A = Arena(AR, self.arena_bytes)
            self.PS = [es.enter_context(nc.psum_tensor(f"PS{i}", [128, 1024], F32)) for i in range(4)]
            self.pb = [self.buf(f"pb{i}") for i in range(8)]
            self.s = Sched(nc)
            self.setup()
            self.s.barrier()
            self.phase_w()
            for b in range(NB):
                self.pass_a(b)
                self.pass_b(b)
                self.pass_c(b)
            self.s.emit(es)
        return nc

    def setup(self):
        A, S = self.A, self.S
        self.CM = A.bf(9 * 128)
        self.bCM = self.buf("CM")
        CM = self.CM
        self.IDENT = CM[:, 0:128]
        self.NEGTRI8 = CM[:, 128:256]
        self.NEGONES8 = CM[:, 256:384]
        self.ONES = CM[:, 384:512]
        self.ONESDIV = CM[:, 512:640]
        self.SBMASK = CM[:, 640:768]
        self.DAMASK = CM[:, 768:896]
        self.PX = CM[:, 896:1024]
        self.PY = CM[:, 1024:1152]
        self.dma("pool", CM, self.cmat, [], [self.bCM], "CM")
        self.SM = A.f32(32)
        self.bSM = self.buf("SM")
        SM = self.SM
        self.M05 = SM[:, 0:1]
        self.GSUB = SM[:, 1:2]
        self.GS = SM[:, 2:3]
        self.NEGLAM = SM[:, 3:4]
        self.bLAMV = self.buf("LAMV")
        self.ST = A.f32(32)
        self.bST = [self.buf(f"ST{i}") for i in range(8)]
        self.stn = 0
        self.OT = A.bf(8 * S).rearrange("p (c s) -> p c s", c=8)
        self.bOT = [self.buf(f"OT{t}") for t in range(self.NT)]
        self.persist_mark = A.off
        self.LAMV = A.f32(4 * 64)
        PR = A.f32(128)
        self.pool(lambda e: e.memset(self.M05, -0.5), [], [self.bSM])
        self.dma("sp", self.GSUB, self.gsub.rearrange("o p -> p o"), [], [self.bSM], "SMg")
        self.dma("sp", self.LAMV, self.lamv.rearrange("a b -> (a b)").partition_broadcast(128),
                 [], [self.bLAMV], "LAMV")
        LV = self.LAMV.rearrange("p (a b) -> p a b", a=4)
        bPR = self.buf("PR")
        LS = SM[:, 8:10]
        EL = SM[:, 10:12]
        self.dve(lambda e: e.tensor_tensor(out=PR[:, 0:64], in0=LV[:, 0, :], in1=LV[:, 1, :], op=ALU.mult),
                 [self.bLAMV], [bPR])
        self.dve(lambda e: e.tensor_tensor(out=PR[:, 64:128], in0=LV[:, 2, :], in1=LV[:, 3, :], op=ALU.mult),
                 [self.bLAMV], [bPR])
        self.dve(lambda e: e.reduce_sum(out=LS, in_=PR.rearrange("p (a b) -> p a b", a=2), axis=AX.X),
                 [bPR], [self.bSM])
        self.act(EL, LS, AF.Exp, [self.bSM], [self.bSM])
        self.dve(lambda e: e.tensor_tensor(out=SM[:, 12:13], in0=EL[:, 1:2], in1=EL[:, 0:1], op=ALU.subtract),
                 [self.bSM], [self.bSM])
        self.dve(lambda e: e.tensor_scalar(out=self.NEGLAM, in0=SM[:, 12:13], scalar1=-LAMBDA_INIT, scalar2=None,
                                           op0=ALU.add), [self.bSM], [self.bSM])
        self.dve(lambda e: e.tensor_scalar(out=self.GS, in0=self.GSUB, scalar1=1.0 - LAMBDA_INIT, scalar2=None,
                                           op0=ALU.mult), [self.bSM], [self.bSM])
        self.LAMCOL = SM[:, 4:5]
        self.pool(lambda e: e.memset(SM[0:64, 4:5], 1.0), [], [self.bSM])
        self.dve(lambda e: e.tensor_copy(out=SM[64:128, 4:5], in_=SM[64:128, 3:4]), [self.bSM], [self.bSM])

    def phase_w(self):
        A = self.A
        A.off = self.persist_mark
        Fs = [A.f32(4096) for _ in range(3)]
        bF = [self.buf(f"F{i}") for i in range(3)]
        Hs = [A.bf(4096) for _ in range(3)]
        bH = [self.buf(f"H{i}") for i in range(3)]
        self.wconv_n = 0
        self.wconv_h = 0
        self.bW = {}

        def nextF():
            i = self.wconv_n % 3
            self.wconv_n += 1
            return Fs[i], bF[i]

        def nextH():
            i = self.wconv_h % 3
            self.wconv_h += 1
            return Hs[i], bH[i]

        def wbuf(key):
            b = self.buf(f"W{key}")
            self.bW[key] = b
            return b

        w_in_v = self.w_in.rearrange("(c p) n -> p c n", p=128)
        for g in range(6):
            F, bf_ = nextF()
            self.dma("sp", F.rearrange("p (c n) -> p c n", c=8), w_in_v[:, :, g * 512:(g + 1) * 512], [], [bf_],
                     ("ld", bf_.name))
            H, bh = nextH()
            self.dve(lambda e, H=H, F=F: e.tensor_copy(out=H.rearrange("p (j c n) -> p j c n", j=4, c=8),
                                                      in_=F.rearrange("p (c j n) -> p j c n", c=8, j=4)),
                     [bf_], [bh])
            self.dma("act", self.WA[g].rearrange("j p m -> p j m"), H.rearrange("p (j m) -> p j m", j=4),
                     [bh], [wbuf(("WA", g))], ("wst", "WA", g))
            if g in (3, 4):
                H2, bh2 = nextH()
                for j in range(4):
                    src = F.rearrange("p (c n) -> p c n", c=8)[:, :, j * 128:(j + 1) * 128] \
                        .rearrange("p c (m h r) -> p c m h r", m=2, h=2)
                    dst = H2[:, j * 1024:(j + 1) * 1024].rearrange("p (c m h r) -> p c m h r", c=8, m=2, h=2)
                    self.dve(lambda e, dst=dst, src=src: e.tensor_scalar(out=dst[:, :, :, 0, :], in0=src[:, :, :, 1, :],
                                                                        scalar1=-1.0, scalar2=None, op0=ALU.mult),
                             [bf_], [bh2])
                    self.dve(lambda e, dst=dst, src=src: e.tensor_copy(out=dst[:, :, :, 1, :], in_=src[:, :, :, 0, :]),
                             [bf_], [bh2])
                gg = 6 if g == 3 else 7
                self.dma("act", self.WA[gg].rearrange("j p m -> p j m"), H2.rearrange("p (j m) -> p j m", j=4),
                         [bh2], [wbuf(("WA", gg))], ("wst", "WA", gg))
        self.wd_groups = [(0, 8), (8, 16), (16, 22)]
        self.bWC = self.buf("WC")
        self.s.barrier()

    def alloc_common(self, nslots, with_xb=True, extra_g=False):
        A = self.A
        if with_xb:
            self.XB = [A.f32(D) for _ in range(2)]
            self.bXB = [self.buf(f"XB{i}") for i in range(2)]
        self.HB = [A.bf(D) for _ in range(2)]
        self.bHB = [self.buf(f"HB{i}") for i in range(2)]
        self.HT = A.bf(8 * 512).rearrange("p (c s) -> p c s", c=8)
        self.bHT = self.buf("HT")
        self.WS = [A.bf(4096) for _ in range(nslots)]
        self.bWS = [self.buf(f"WS{i}") for i in range(nslots)]
        self.wsn = 0
        self.nhb = 0
        self.bG = self.buf("G")
        self.GMIX = A.f32(D)
        gl = [(self.GMIX, self.gmix)]
        if extra_g:
            self.GFFN = A.f32(D)
            self.GFIN = A.f32(D)
            gl += [(self.GFFN, self.gffn), (self.GFIN, self.gfin)]
        for gt, gs in gl:
            self.dma("sp", gt, gs.rearrange("o d -> (o d)").partition_broadcast(128), [], [self.bG], "G")

    def wslot(self):
        i = self.wsn % len(self.WS)
        self.wsn += 1
        return self.WS[i], self.bWS[i]

    def rstd(self, x_ap, xbufs, junk, bjunk):
        g = self.stn % 8
        self.stn += 1
        st = self.ST[:, g * 4:g * 4 + 4]
        bst = self.bST[g]
        if getattr(self, "sq_on_dve", False):
            self.dve(lambda e: e.scalar_tensor_tensor(out=junk, in0=x_ap, scalar=1.0, in1=x_ap, op0=ALU.mult,
                                                      op1=ALU.mult, accum_out=st[:, 0:1]),
                     xbufs, [bjunk, bst])
        else:
            self.act(junk, x_ap, AF.Square, xbufs, [bjunk, bst], accum=st[:, 0:1])
        self.pool(lambda e: e.tensor_scalar(out=st[:, 1:2], in0=st[:, 0:1], scalar1=1.0 / D, scalar2=NORM_EPS,
                                            op0=ALU.mult, op1=ALU.add), [bst], [bst])
        self.pool(lambda e: e.tensor_tensor(out=st[:, 2:3], in0=st[:, 1:2], in1=self.M05, op=ALU.pow),
                  [bst, self.bSM], [bst])
        return st[:, 2:3], bst

    def next_hb(self):
        i = self.nhb % 2
        self.nhb += 1
        return self.HB[i], self.bHB[i]

    def load_wa(self, g):
        ws, bws = self.wslot()
        self.dma("sp", ws.rearrange("p (j m) -> p j m", j=4), self.WA[g].rearrange("j p m -> p j m"),
                 [self.bW[("WA", g)]], [bws], ("ld", bws.name))
        return ws.rearrange("p (j m) -> p j m", j=4), bws

    def proj_fm(self, wsv, bws, c, bank_i):
        bk = self.bank(bank_i)
        for dc in range(8):
            self.mm(bk, wsv[:, c, dc * 128:(dc + 1) * 128], self.HT[:, dc, :], dc == 0, dc == 7,
                    [bws, self.bHT], [self.pb[bank_i]])
        return bk

    def prep_items(self, b, t, specs, bankfn, vspec=None, pre=()):
        items = []
        st = {}

        def LD(tb):
            i = (t * 4 + tb) % 2
            self.dma("sp", self.XB[i], self.x[b, (t * 4 + tb) * 128:(t * 4 + tb + 1) * 128, :], [], [self.bXB[i]],
                     ("ld", self.bXB[i].name))

        def SQ(tb):
            i = (t * 4 + tb) % 2
            hb, bhb = self.next_hb()
            st[("hb", tb)] = (hb, bhb)
            st[("r", tb)] = self.rstd(self.XB[i], [self.bXB[i]], hb, bhb)

        def NRM(tb):
            i = (t * 4 + tb) % 2
            hb, bhb = st[("hb", tb)]
            r, bst = st[("r", tb)]
            xb = self.XB[i]
            gm = self.GMIX
            self.dve(lambda e: e.scalar_tensor_tensor(out=hb, in0=xb, scalar=r, in1=gm, op0=ALU.mult,
                                                      op1=ALU.mult), [self.bXB[i], bst, self.bG], [bhb])

        def TR(tb):
            hb, bhb = st[("hb", tb)]
            bi = bankfn()
            tp = self.bank(bi).bitcast(BF)
            for c in range(8):
                self.tr(tp[:, c * 128:(c + 1) * 128], hb[:, c * 128:(c + 1) * 128], [bhb], [self.pb[bi]])
            ht = self.HT
            self.dve(lambda e: e.tensor_copy(out=ht[:, :, tb * 128:(tb + 1) * 128],
                                             in_=tp.rearrange("p (c n) -> p c n", c=8)), [self.pb[bi]], [self.bHT])

        items += list(pre)
        items += [lambda: LD(0), lambda: LD(1), lambda: SQ(0), lambda: SQ(1), lambda: NRM(0), lambda: LD(2),
                  lambda: TR(0), lambda: NRM(1), lambda: LD(3), lambda: SQ(2), lambda: TR(1), lambda: NRM(2),
                  lambda: SQ(3), lambda: TR(2), lambda: NRM(3), lambda: TR(3)]
        for (groups, evac_fn) in specs:
            def LW(groups=groups):
                st[("w", tuple(groups))] = [self.load_wa(g) for g in groups]
            items.append(LW)
            for c in range(4):
                for gi in range(len(groups)):
                    def PJ(c=c, gi=gi, groups=groups, evac_fn=evac_fn):
                        wsv, bws = st[("w", tuple(groups))][gi]
                        bi = bankfn()
                        self.proj_fm(wsv, bws, c, bi)
                        evac_fn(c, gi, bi)
                    items.append(PJ)
        if vspec is not None:
            g, Vt, bV = vspec

            def LWV():
                st["wv"] = self.load_wa(g)
            items.append(LWV)
            for tb in range(4):
                def PV(tb=tb):
                    wsv, bws = st["wv"]
                    bi = bankfn()
                    bk = self.bank(bi)
                    bk4 = bk.rearrange("p (j n) -> p j n", j=4)
                    for dc in range(8):
                        self.mm(bk4, self.HT[:, dc, tb * 128:(tb + 1) * 128], wsv[:, :, dc * 128:(dc + 1) * 128],
                                dc == 0, dc == 7, [bws, self.bHT], [self.pb[bi]])
                    vdst = Vt[:, t * 4 + tb, :]
                    self.dve(lambda e: e.tensor_copy(out=vdst, in_=bk), [self.pb[bi]], [bV])
                items.append(PV)
        return items

    def wc_pieces(self):
        A = self.A
        F = A.f32(1024)
        H = A.bf(1024)
        bF, bH = self.buf("WCF"), self.buf("WCH")
        out = []

        def piece(src_ap, fview, conv, dst_ap, hview):
            def ld():
                self.dma("sp", fview(F), src_ap, [], [bF], ("ld", "WCF"))

            def cv():
                self.dve(conv, [bF], [bH])

            def st():
                self.dma("sp", dst_ap, hview(H), [bH], [self.bWC], ("wstc",))
            out.extend([ld, None, cv, st])

        straight = lambda e: e.tensor_copy(out=H, in_=F)
        w_in_v = self.w_in.rearrange("(c p) n -> p c n", p=128)
        v8 = lambda T: T.rearrange("p (c n) -> p c n", c=8)
        ident_h = lambda T: T
        for j in range(8):
            for k in range(2):
                col0 = 3072 + 1024 * k + 128 * j
                piece(w_in_v[:, :, col0:col0 + 128], v8, straight, self.WG[j][:, 1024 * k:1024 * (k + 1)], ident_h)
        for k, wsrc in enumerate((self.wbs, self.wbd)):
            wv = wsrc.rearrange("(c p) n -> p c n", p=128)
            for q in range(4):
                conv = lambda e: e.tensor_copy(out=H.rearrange("p (j c n) -> p j c n", j=2, c=4),
                                               in_=F.rearrange("p (c j n) -> p j c n", c=4, j=2))
                off = 2048 + 512 * k
                piece(wv[:, :, 256 * q:256 * (q + 1)], lambda T: T.rearrange("p (c n) -> p c n", c=4), conv,
                      self.WG[2 * q:2 * q + 2, :, off:off + 512].rearrange("j p m -> p j m"),
                      lambda T: T.rearrange("p (j m) -> p j m", j=2))
        wo_v = self.wo.rearrange("(c p) n -> p c n", p=128)
        v2 = lambda T: T.rearrange("p (c n) -> p c n", c=2)
        for ch in range(2):
            for q in range(4):
                piece(wo_v[:, 2 * q:2 * q + 2, ch * 512:(ch + 1) * 512], v2, straight,
                      self.WO[ch][:, 2 * q * 512:(2 * q + 2) * 512], ident_h)
        for f in range(NFC):
            for k, wsrc in enumerate((self.wg, self.wu)):
                wv = wsrc.rearrange("(c p) n -> p c n", p=128)
                piece(wv[:, :, 128 * f:128 * (f + 1)], v8, straight, self.WF[f, k], ident_h)
        wd_v = self.wd.rearrange("(f p) n -> p f n", p=128)
        for ch in range(2):
            for f in range(0, NFC, 2):
                piece(wd_v[:, f:f + 2, ch * 512:(ch + 1) * 512], v2, straight, self.WD[ch][:, f * 512:(f + 2) * 512],
                      ident_h)
        return out

    def pass_a(self, b):
        A, S, NT = self.A, self.S, self.NT
        self.sq_on_dve = True
        A.off = self.persist_mark
        KT = A.bf(4 * S).rearrange("p (c s) -> p c s", c=4)
        Vt = A.bf(S * 4).rearrange("p (k n) -> p k n", n=512)
        bKT = [self.buf(f"KT{t}") for t in range(NT)]
        bV = [self.buf(f"V{t}") for t in range(NT)]
        self.alloc_common(2)
        QTs = [A.bf(4 * 512).rearrange("p (c s) -> p c s", c=4) for _ in range(2)]
        bQTs = [self.buf(f"QT{i}") for i in range(2)]
        Es = [A.f32(1024).rearrange("p (h s) -> p h s", h=2) for _ in range(2)]
        bEs = [self.buf(f"E{i}") for i in range(2)]
        SP = [A.bf(1024).rearrange("p (h s) -> p h s", h=2) for _ in range(2)]
        bSP = [self.buf(f"SP{i}") for i in range(2)]
        WT = [A.bf(1024).rearrange("p (h s) -> p h s", h=2) for _ in range(2)]
        bWT = [self.buf(f"WT{i}") for i in range(2)]
        R = [A.bf(1024).rearrange("p (h s) -> p h s", h=2) for _ in range(2)]
        bR = [self.buf(f"R{i}") for i in range(2)]
        wcp = self.wc_pieces() if b == 0 else []
        wcp_pos = 0
        fg = {"n": 0}

        def fg_bank():
            fg["n"] += 1
            return 4 + fg["n"] % 4

        def mk_items(t, bankfn):
            qt, bqt = QTs[t % 2], bQTs[t % 2]

            def evq(c, gi, bi):
                bk = self.bank(bi)
                self.dve(lambda e: e.tensor_copy(out=qt[:, c, :], in_=bk), [self.pb[bi]], [bqt])

            def evk(c, gi, bi):
                bk = self.bank(bi)
                dst = KT[:, c, t * 512:(t + 1) * 512]
                self.dve(lambda e: e.tensor_copy(out=dst, in_=bk), [self.pb[bi]], [bKT[t]])

            return self.prep_items(b, t, [([0], evq), ([1], evk)], bankfn, (2, Vt, bV[t]))

        for it in mk_items(0, fg_bank):
            it()
        for t in range(NT):
            QT, bQT = QTs[t % 2], bQTs[t % 2]
            bg = mk_items(t + 1, lambda: 7) if t + 1 < NT else []
            bg_pos = 0
            nkb = 4 * t + 4
            steps = [(c, kb) for c in range(4) for kb in range(nkb - 1, -1, -1)]
            n = len(steps)

            def info(i):
                c, kb = steps[i]
                j = kb - 4 * t
                c0 = j * 128 if j >= 0 else 0
                return c, kb, j >= 0, c0, kb == nkb - 1, kb == 0

            def zzv(i):
                return self.PS[i % 3].rearrange("p (h s) -> p h s", h=2)

            def zzb(i):
                return [self.pb[2 * (i % 3)], self.pb[2 * (i % 3) + 1]]

            def S_(i):
                c, kb, dg, c0, first, last = info(i)
                zz = zzv(i)
                for h in range(2):
                    self.mm(zz[:, h, c0:512], KT[64 * h:64 * h + 64, c, kb * 128:(kb + 1) * 128],
                            QT[64 * h:64 * h + 64, c, c0:512], True, True, [bKT[kb // 4], bQT],
                            [self.pb[2 * (i % 3) + h]])
                if dg:
                    for h in range(2):
                        self.mm(zz[:, h, c0:c0 + 128], self.IDENT, self.SBMASK, False, True, [self.bCM],
                                [self.pb[2 * (i % 3) + h]], skip=True)

            def zero_left(buf_ap, bbuf, c0):
                if c0 > 0:
                    self.pool(lambda e: e.memset(buf_ap[:, :, 0:c0], 0.0), [], [bbuf])

            S_(0)
            for i in range(n + 2):
                if i < n:
                    c, kb, dg, c0, first, last = info(i)
                    zz = zzv(i)
                    E, bE = Es[i % 2], bEs[i % 2]
                    self.act(E[:, :, c0:512], zz[:, :, c0:512], AF.Exp, zzb(i), [bE], scale=0.125)
                if 2 <= i:
                    pc, pkb, pdg, pc0, pfirst, plast = info(i - 2)
                    pzz = zzv(i - 2)
                    wt, bwt = WT[(i - 2) % 2], bWT[(i - 2) % 2]
                    if pdg:
                        zero_left(wt, bwt, pc0)
                    self.act(wt[:, :, pc0:512], pzz[:, :, pc0:512], AF.Exp, zzb(i - 2), [bwt], scale=0.125)
                if i < n:
                    sp, bsp = SP[i % 2], bSP[i % 2]
                    if dg:
                        zero_left(sp, bsp, c0)
                    self.act(sp[:, :, c0:512], E[:, :, c0:512], AF.Ln, [bE], [bsp], bias=1.0)
                if i + 1 < n:
                    S_(i + 1)
                if 2 <= i:
                    ob_i = 6
                    ob = self.bank(ob_i)
                    for h in range(2):
                        self.mm(ob[64 * h:64 * h + 64, :], Vt[:, pkb, pc * 128 + 64 * h:pc * 128 + 64 * h + 64],
                                wt[:, h, :], pfirst, plast, [bV[pkb // 4], bwt], [self.pb[ob_i]])
                    if plast:
                        odst = self.OT[:, pc, t * 512:(t + 1) * 512]
                        self.dve(lambda e, ob=ob, odst=odst: e.tensor_copy(out=odst, in_=ob),
                                 [self.pb[ob_i]], [self.bOT[t]])
                if i < n:
                    r, br = R[c % 2], bR[c % 2]
                    for h in range(2):
                        self.mm(zz[:, h, c0:512], self.NEGTRI8, sp[:, h, c0:512], False, True,
                                [bsp, self.bCM], [self.pb[2 * (i % 3) + h]], skip=True)
                        if not first:
                            self.mm(zz[:, h, c0:512], self.NEGONES8, r[:, h, c0:512], False, True,
                                    [br, self.bCM], [self.pb[2 * (i % 3) + h]], skip=True)
                    if not last:
                        if first:
                            self.dve(lambda e, r=r, sp=sp: e.tensor_copy(out=r, in_=sp), [bsp], [br])
                        else:
                            self.dve(lambda e, r=r, sp=sp: e.tensor_tensor(out=r, in0=r, in1=sp, op=ALU.add),
                                     [bsp, br], [br])
                if wcp_pos < len(wcp):
                    if wcp[wcp_pos] is not None:
                        wcp[wcp_pos]()
                    wcp_pos += 1
                if bg and i >= 1:
                    per = -(-len(bg) // max(1, n - 3))
                    for _ in range(per):
                        if bg_pos < len(bg):
                            bg[bg_pos]()
                            bg_pos += 1
            while bg_pos < len(bg):
                bg[bg_pos]()
                bg_pos += 1
        while wcp_pos < len(wcp):
            if wcp[wcp_pos] is not None:
                wcp[wcp_pos]()
            wcp_pos += 1
        self.s.barrier()

    def pass_b(self, b):
        A, S, NT = self.A, self.S, self.NT
        self.sq_on_dve = False
        A.off = self.persist_mark
        KT = A.bf(4 * S).rearrange("p (c s) -> p c s", c=4)
        Vt = A.bf(S * 4).rearrange("p (k n) -> p k n", n=512)
        bKT = [self.buf(f"KTd{t}") for t in range(NT)]
        bV = [self.buf(f"Vd{t}") for t in range(NT)]
        self.alloc_common(3)
        QTs = [A.bf(4 * 512).rearrange("p (c s) -> p c s", c=4) for _ in range(2)]
        bQTs = [self.buf(f"QTd{i}") for i in range(2)]
        COS = A.f32(512)
        SIN = A.f32(512)
        bCOS = self.buf("COS")
        bSIN = self.buf("SIN")
        ET = [A.bf(1024).rearrange("p (h s) -> p h s", h=2) for _ in range(2)]
        bET = [self.buf(f"ET{i}") for i in range(2)]
        RD, XS, YS = [A.f32(512) for _ in range(3)]
        bN = self.buf("NRM")
        bRP = self.buf("ROPE")
        RT = A.f32(512)
        SQ = YS.bitcast(BF)[:, 0:512]
        XL, YL = A.bf(512), A.bf(512)

        def hi(ap):
            return ap.bitcast(BF).rearrange("p (n two) -> p n two", two=2)[:, :, 1]
        fg = {"n": 0}

        def fg_bank():
            fg["n"] += 1
            return 4 + fg["n"] % 4

        def mk_items(t, bankfn):
            qt, bqt = QTs[t % 2], bQTs[t % 2]

            def ev(dst, dbuf):
                def f(c, gi, bi):
                    bk = self.bank(bi)
                    if gi == 0:
                        self.dve(lambda e: e.tensor_tensor(out=RT, in0=bk, in1=COS, op=ALU.mult),
                                 [self.pb[bi], bCOS], [bRP])
                    else:
                        d = dst(c)
                        self.dve(lambda e: e.tensor_tensor(out=bk, in0=bk, in1=SIN, op=ALU.mult),
                                 [self.pb[bi], bSIN], [self.pb[bi]])
                        self.dve(lambda e: e.tensor_tensor(out=d, in0=bk, in1=RT, op=ALU.add),
                                 [self.pb[bi], bRP], [dbuf])
                return f

            def ldcs():
                self.dma("sp", COS, self.cosT[:, t * 512:(t + 1) * 512], [], [bCOS], ("ld", "COS"))
                self.dma("sp", SIN, self.sinT[:, t * 512:(t + 1) * 512], [], [bSIN], ("ld", "SIN"))

            return self.prep_items(b, t, [([3, 6], ev(lambda c: qt[:, c, :], bqt)),
                                         ([4, 7], ev(lambda c: KT[:, c, t * 512:(t + 1) * 512], bKT[t]))],
                                   bankfn, (5, Vt, bV[t]), pre=[ldcs])

        deferred = []
        gstep = [0]
        for it in mk_items(0, fg_bank):
            it()
        for t in range(NT):
            QT, bQT = QTs[t % 2], bQTs[t % 2]
            bg = mk_items(t + 1, lambda: 7) if t + 1 < NT else []
            bg_pos = 0
            nkb = 4 * t + 4
            steps = [(hd, kb) for hd in range(4) for kb in range(nkb)]
            n = len(steps)

            def info(i):
                hd, kb = steps[i]
                j = kb - 4 * t
                c0 = j * 128 if j >= 0 else 0
                return hd, kb, j >= 0, c0, kb == 0, kb == nkb - 1

            def S_(i):
                hd, kb, dg, c0, first, last = info(i)
                zz = self.PS[i % 2].rearrange("p (h s) -> p h s", h=2)
                for h in range(2):
                    self.mm(zz[:, h, c0:512], KT[64 * h:64 * h + 64, hd, kb * 128:(kb + 1) * 128],
                            QT[64 * h:64 * h + 64, hd, c0:512], True, True, [bKT[kb // 4], bQT],
                            [self.pb[2 * (i % 2) + h]])
                if dg:
                    for h in range(2):
                        self.mm(zz[:, h, c0:c0 + 128], self.IDENT, self.DAMASK, False, True, [self.bCM],
                                [self.pb[2 * (i % 2) + h]], skip=True)

            S_(0)
            for i in range(n):
                gstep[0] += 1
                hd, kb, dg, c0, first, last = info(i)
                zz = self.PS[i % 2].rearrange("p (h s) -> p h s", h=2)
                et, bet = ET[i % 2], bET[i % 2]
                self.act(et[:, :, c0:512], zz[:, :, c0:512], AF.Exp, [self.pb[2 * (i % 2)], self.pb[2 * (i % 2) + 1]],
                         [bet], scale=0.125)
                if i + 1 < n:
                    S_(i + 1)
                for half, bi in ((0, 4), (1, 5)):
                    vv = Vt[:, kb, hd * 128 + 64 * half:hd * 128 + 64 * half + 64]
                    for h in range(2):
                        self.mm(self.bank(bi)[64 * h:64 * h + 64, c0:512], vv, et[:, h, c0:512], first, dg,
                                [bV[kb // 4], bet], [self.pb[bi]], skip=(c0 > 0))
                for h in range(2):
                    self.mm(self.bank(6)[64 * h:64 * h + 64, c0:512], self.ONES[:, 0:64], et[:, h, c0:512], first, dg,
                            [self.bCM, bet], [self.pb[6]], skip=(c0 > 0))
                if last:
                    self.act(RD, self.bank(6), AF.Ln, [self.pb[6]], [bN])
                    self.dve(lambda e: e.tensor_copy(out=XS, in_=self.bank(4)), [self.pb[4]], [bN])
                    self.dve(lambda e: e.tensor_copy(out=YS, in_=self.bank(5)), [self.pb[5]], [bN])

                    def stage1():
                        self.act(RD, RD, AF.Exp, [bN], [bN], scale=-1.0)
                        for T_, L_ in ((XS, XL), (YS, YL)):
                            self.dve(lambda e, T_=T_: e.scalar_tensor_tensor(out=T_, in0=T_, scalar=self.LAMCOL, in1=RD,
                                                                             op0=ALU.mult, op1=ALU.mult),
                                     [bN, self.bSM], [bN])
                            self.dve(lambda e, T_=T_, L_=L_: e.tensor_tensor(out=L_, in0=T_, in1=hi(T_), op=ALU.subtract),
                                     [bN], [bN])

                    def stage2():
                        self.mm(self.bank(7), self.PX, hi(XS), True, False, [bN, self.bCM], [self.pb[7]])
                        self.mm(self.bank(7), self.PY, hi(YS), False, False, [bN, self.bCM], [self.pb[7]])
                        self.mm(self.bank(7), self.PX, XL, False, False, [bN, self.bCM], [self.pb[7]])
                        self.mm(self.bank(7), self.PY, YL, False, True, [bN, self.bCM], [self.pb[7]])
                        self.act(SQ, self.bank(7), AF.Square, [self.pb[7]], [bN])
                        self.dve(lambda e: e.tensor_copy(out=XS, in_=self.bank(7)), [self.pb[7]], [bN])

                    def stage3(hd=hd, t=t):
                        self.mm(self.bank(7), self.ONESDIV, SQ, True, True, [bN, self.bCM], [self.pb[7]])
                        self.act(RD, self.bank(7), AF.Ln, [self.pb[7]], [bN], bias=SUBLN_EPS)
                        self.act(RD, RD, AF.Exp, [bN], [bN], scale=-0.5)
                        odst = self.OT[:, 4 + hd, t * 512:(t + 1) * 512]
                        self.dve(lambda e, odst=odst: e.scalar_tensor_tensor(
                            out=odst, in0=XS, scalar=self.GS, in1=RD,
                            op0=ALU.mult, op1=ALU.mult), [bN, self.bSM], [self.bOT[t]])

                    dd = [min(d, nkb - 1) for d in (1, 5, 8)]
                    g0 = gstep[0]
                    deferred.extend([(g0 + dd[0], stage1), (g0 + dd[1], stage2), (g0 + dd[2], stage3)])
                    deferred.sort(key=lambda x: x[0])
                while deferred and deferred[0][0] <= gstep[0]:
                    deferred.pop(0)[1]()
                if bg and i >= 1:
                    per = -(-len(bg) // max(1, n - 3))
                    for _ in range(per):
                        if bg_pos < len(bg):
                            bg[bg_pos]()
                            bg_pos += 1
            while bg_pos < len(bg):
                bg[bg_pos]()
                bg_pos += 1
        while deferred:
            deferred.pop(0)[1]()
        self.s.barrier()

    def norm_parts(self, x_ap, xbufs, gtile, HTdst, bHTdst, tb, bank_i):
        st = {}

        def sq():
            st["hb"] = self.next_hb()
            st["r"] = self.rstd(x_ap, xbufs, *st["hb"])

        def nrm():
            hb, bhb = st["hb"]
            r, bst = st["r"]
            self.dve(lambda e: e.scalar_tensor_tensor(out=hb, in0=x_ap, scalar=r, in1=gtile, op0=ALU.mult, op1=ALU.mult),
                     list(xbufs) + [bst, self.bG], [bhb])

        def tr():
            hb, bhb = st["hb"]
            tp = self.bank(bank_i).bitcast(BF)
            for c in range(8):
                self.tr(tp[:, c * 128:(c + 1) * 128], hb[:, c * 128:(c + 1) * 128], [bhb], [self.pb[bank_i]])
            self.dve(lambda e: e.tensor_copy(out=HTdst[:, :, tb * 128:(tb + 1) * 128],
                                             in_=tp.rearrange("p (c n) -> p c n", c=8)), [self.pb[bank_i]], [bHTdst])
        return sq, nrm, tr

    def pass_c(self, b):
        A, S, NT = self.A, self.S, self.NT
        self.sq_on_dve = False
        A.off = self.persist_mark
        self.alloc_common(4, with_xb=False, extra_g=True)
        GM, GF, GN = self.GMIX, self.GFFN, self.GFIN
        HTf, bHTf = self.HT, self.bHT
        HTm = A.bf(8 * 512).rearrange("p (c s) -> p c s", c=8)
        bHTm = self.buf("HTm")
        XTs = [A.f32(4 * D).rearrange("p (k d) -> p k d", k=4) for _ in range(2)]
        bXTs = [[self.buf(f"XT{i}_{k}") for k in range(4)] for i in range(2)]
        MIX = A.bf(8 * 512).rearrange("p (c s) -> p c s", c=8)
        bMIX = self.buf("MIX")
        ACTT = A.bf(NFC * 512).rearrange("p (f s) -> p f s", f=NFC)
        bACT = self.buf("ACTT")
        S1, S2, M1, M2 = [A.f32(512) for _ in range(4)]
        bSG = self.buf("SG")
        SGL = [A.f32(512) for _ in range(2)]
        bSGL = [self.buf(f"SGL{i}") for i in range(2)]

        def mix_items(t):
            XT, bX = XTs[t % 2], bXTs[t % 2]
            items = []
            for tb in range(4):
                def LD(tb=tb):
                    self.dma("pool", XT[:, tb, :], self.x[b, t * 512 + tb * 128:t * 512 + (tb + 1) * 128, :], [],
                             [bX[tb]], ("ld", f"XT{t % 2}_{tb}"))
                items.append(LD)
            parts = [self.norm_parts(XT[:, tb, :], [bX[tb]], GM, HTm, bHTm, tb, 6 + (tb % 2)) for tb in range(4)]
            sq, nrm, tr = zip(*parts)
            items += [sq[0], sq[1], nrm[0], tr[0], nrm[1], sq[2], tr[1], nrm[2], sq[3], tr[2], nrm[3], tr[3]]
            return items

        for it in mix_items(0):
            it()
        for t in range(NT):
            XT, bXTl = XTs[t % 2], bXTs[t % 2]
            tok = slice(t * 512, (t + 1) * 512)
            for j in range(8):
                ws, bws = self.wslot()
                self.dma("sp", ws[:, 0:3072], self.WG[j], [self.bWC], [bws], ("ld", bws.name))
                b0 = 4 * (j % 2)
                for k in range(2):
                    gb, bbk = b0 + k, b0 + 2 + k
                    for dc in range(8):
                        self.mm(self.bank(gb), ws[:, k * 1024 + dc * 128:k * 1024 + (dc + 1) * 128], HTm[:, dc, :],
                                dc == 0, dc == 7, [bws, bHTm], [self.pb[gb]])
                    for c in range(4):
                        self.mm(self.bank(bbk), ws[:, 2048 + k * 512 + c * 128:2048 + k * 512 + (c + 1) * 128],
                                self.OT[:, 4 * k + c, tok], c == 0, c == 3, [bws, self.bOT[t]], [self.pb[bbk]])
                g1, g2, p1, p2 = [self.bank(b0 + q) for q in range(4)]
                self.act(S1, g1, AF.Sigmoid, [self.pb[b0]], [bSG])
                self.act(S2, g2, AF.Sigmoid, [self.pb[b0 + 1]], [bSG])
                self.dve(lambda e, p1=p1: e.tensor_tensor(out=M1, in0=p1, in1=S1, op=ALU.mult), [self.pb[b0 + 2], bSG], [bSG])
                self.dve(lambda e, p2=p2: e.tensor_tensor(out=M2, in0=p2, in1=S2, op=ALU.mult), [self.pb[b0 + 3], bSG], [bSG])
                self.dve(lambda e, j=j: e.tensor_tensor(out=MIX[:, j, :], in0=M1, in1=M2, op=ALU.add), [bSG], [bMIX])
            wo = []
            for ch in range(2):
                ws, bws = self.wslot()
                self.dma("sp", ws, self.WO[ch], [self.bWC], [bws], ("ld", bws.name))
                wo.append((ws.rearrange("p (c n) -> p c n", c=8), bws))
            fparts = [self.norm_parts(XT[:, tb, :], [bXTl[tb]], GF, HTf, bHTf, tb, 6 + (tb % 2)) for tb in range(4)]
            for tb in range(4):
                for ch in range(2):
                    bi = (2 * tb + ch) % 4
                    wv, bws = wo[ch]
                    for c in range(8):
                        self.mm(self.bank(bi), MIX[:, c, tb * 128:(tb + 1) * 128], wv[:, c, :], c == 0, c == 7,
                                [bws, bMIX], [self.pb[bi]])
                    xs = XT[:, tb, ch * 512:(ch + 1) * 512]
                    bk = self.bank(bi)
                    self.dve(lambda e, xs=xs, bk=bk: e.tensor_tensor(out=xs, in0=bk, in1=xs, op=ALU.add),
                             [self.pb[bi], bXTl[tb]], [bXTl[tb]])
                fparts[tb][0]()
                fparts[tb][1]()
                if tb >= 1:
                    fparts[tb - 1][2]()
            fparts[3][2]()
            bg = mix_items(t + 1) if t + 1 < NT else []
            bg_pos = 0
            for fp in range(NFC // 2):
                ws, bws = self.wslot()
                self.dma("sp", ws.rearrange("p (a m) -> p a m", a=4),
                         self.WF[2 * fp:2 * fp + 2].rearrange("f k p m -> p (f k) m"),
                         [self.bWC], [bws], ("ld", bws.name))
                wv = ws.rearrange("p (a m) -> p a m", a=4)
                for i2 in range(2):
                    f = 2 * fp + i2
                    gb, ub = 2 * (f % 2), 2 * (f % 2) + 1
                    for dc in range(8):
                        self.mm(self.bank(gb), wv[:, 2 * i2, dc * 128:(dc + 1) * 128], HTf[:, dc, :], dc == 0, dc == 7,
                                [bws, bHTf], [self.pb[gb]])
                    for dc in range(8):
                        self.mm(self.bank(ub), wv[:, 2 * i2 + 1, dc * 128:(dc + 1) * 128], HTf[:, dc, :], dc == 0,
                                dc == 7, [bws, bHTf], [self.pb[ub]])
                    sg, bsg = SGL[f % 2], bSGL[f % 2]
                    gbk, ubk = self.bank(gb), self.bank(ub)
                    adst = ACTT[:, f, :]
                    self.act(sg, gbk, AF.Silu, [self.pb[gb]], [bsg])
                    self.dve(lambda e, sg=sg, ubk=ubk, adst=adst: e.tensor_tensor(out=adst, in0=ubk, in1=sg, op=ALU.mult),
                             [self.pb[ub], bsg], [bACT])
                    if f >= 2 and bg_pos < len(bg):
                        bg[bg_pos]()
                        bg_pos += 1
            while bg_pos < len(bg):
                bg[bg_pos]()
                bg_pos += 1
            for ch in range(2):
                for gi, (f0, f1) in enumerate(self.wd_groups):
                    nf = f1 - f0
                    ws, bws = self.wslot()
                    self.dma("sp", ws[:, 0:nf * 512], self.WD[ch][:, f0 * 512:f1 * 512], [self.bWC], [bws],
                             ("ld", bws.name))
                    wv = ws[:, 0:nf * 512].rearrange("p (f n) -> p f n", f=nf)
                    for tb in range(4):
                        bi = 4 + tb
                        for f in range(f0, f1):
                            self.mm(self.bank(bi), ACTT[:, f, tb * 128:(tb + 1) * 128], wv[:, f - f0, :], f == 0,
                                    f == NFC - 1, [bws, bACT], [self.pb[bi]])
                for tb in range(4):
                    bi = 4 + tb
                    xs = XT[:, tb, ch * 512:(ch + 1) * 512]
                    bk = self.bank(bi)
                    self.dve(lambda e, xs=xs, bk=bk: e.tensor_tensor(out=xs, in0=bk, in1=xs, op=ALU.add),
                             [self.pb[bi], bXTl[tb]], [bXTl[tb]])
            for tb in range(4):
                jk, bjk = self.next_hb()
                xr = XT[:, tb, :]
                r, bst = self.rstd(xr, [bXTl[tb]], jk, bjk)
                self.dve(lambda e, xr=xr, r=r: e.scalar_tensor_tensor(out=xr, in0=xr, scalar=r, in1=GN, op0=ALU.mult,
                                                                      op1=ALU.mult),
                         [bXTl[tb], bst, self.bG], [bXTl[tb]])
                st = self.dma("pool", self.out[b, t * 512 + tb * 128:t * 512 + (tb + 1) * 128, :], xr,
                              [bXTl[tb]], [], ("st", f"XT{t % 2}_{tb}"))
                self.s.final_dma.append(st)
        self.s.barrier()


def host_consts(S):
    jj, kk = np.meshgrid(np.arange(128), np.arange(128), indexing="ij")
    ident = np.eye(128, dtype=np.float32)
    negtri8 = np.where(jj >= kk, -8.0, 0.0).astype(np.float32)
    negones8 = np.full((128, 128), -8.0, np.float32)
    ones = np.ones((128, 128), np.float32)
    onesdiv = np.full((128, 128), 1.0 / 128.0, np.float32)
    sbmask = np.where(jj < kk, 0.0, -30000.0).astype(np.float32)
    damask = np.where((jj < 64) | (kk >= 64), 0.0, -30000.0).astype(np.float32)
    p1 = ((jj == kk) & (kk < 64)).astype(np.float32)
    p2 = ((jj == kk + 64) & (kk < 64)).astype(np.float32)
    p3 = ((jj == kk - 64) & (kk >= 64)).astype(np.float32)
    p4 = ((jj == kk) & (kk >= 64)).astype(np.float32)
    cmat = np.concatenate([ident, negtri8, negones8, ones, onesdiv, sbmask, damask, p1 + p2, p3 + p4], axis=1)
    inv_freq = (1.0 / (np.float32(10000.0) ** (np.arange(0, 64, 2, dtype=np.float32) / np.float32(64)))).astype(np.float32)
    ang = np.arange(S, dtype=np.float32)[:, None] * inv_freq[None, :]
    cos = np.cos(ang).astype(np.float32).T
    sin = np.sin(ang).astype(np.float32).T
    cosT = np.ascontiguousarray(np.tile(cos, (4, 1)))
    sinT = np.ascontiguousarray(np.tile(sin, (4, 1)))
    return cmat, cosT, sinT


_NC_CACHE = {}


def make_in_maps(inputs, S, NB, ncores):
    cmat, cosT, sinT = host_consts(S)
    f = lambda a: np.ascontiguousarray(np.asarray(a, dtype=np.float32))
    x = f(inputs["x"])
    shared = {
        "w_in": f(inputs["w_in"][0]),
        "wbs": f(inputs["w_branch_sb"][0]),
        "wbd": f(inputs["w_branch_da"][0]),
        "wo": f(inputs["w_out"][0]),
        "wg": f(inputs["w_ffn_gate"][0]),
        "wu": f(inputs["w_ffn_up"][0]),
        "wd": f(inputs["w_ffn_down"][0]),
        "gmix": f(inputs["g_mix"]).reshape(1, D),
        "gffn": f(inputs["g_ffn"]).reshape(1, D),
        "gfin": f(inputs["g_final"]).reshape(1, D),
        "gsub": f(inputs["g_subln"]).reshape(1, 128),
        "lamv": np.ascontiguousarray(np.stack([f(inputs["lambda_q1"]).reshape(64), f(inputs["lambda_k1"]).reshape(64),
                                               f(inputs["lambda_q2"]).reshape(64), f(inputs["lambda_k2"]).reshape(64)])),
        "cmat": cmat, "cosT": cosT, "sinT": sinT,
    }
    maps = []
    for i in range(ncores):
        m = dict(shared)
        m["x"] = np.ascontiguousarray(x[i * NB:(i + 1) * NB])
        maps.append(m)
    return maps


def kernel(**inputs):
    x = np.asarray(inputs["x"])
    B, S, _ = x.shape
    NB = B // NCORES
    key = (S, NB)
    if key not in _NC_CACHE:
        _NC_CACHE[key] = KB(S, NB).build()
    nc = _NC_CACHE[key]
    in_maps = make_in_maps(inputs, S, NB, NCORES)
    res = run_bass_kernel_spmd(nc, in_maps, core_ids=list(range(NCORES)))
    return np.concatenate([np.asarray(r["out"]) for r in res.results], axis=0).astype(np.float32)
```

```python
import math
from contextlib import ExitStack

import numpy as np
import concourse.bass as bass
import concourse.mybir as mybir
from concourse.bass_utils import run_bass_kernel_spmd

F32 = mybir.dt.float32
BF = mybir.dt.bfloat16
AF = mybir.ActivationFunctionType
ALU = mybir.AluOpType
AX = mybir.AxisListType

D = 1024
DC = 8
DFF = 2816
NFC = 22
INW = 5120
NCORES = 8
SEQ = 4096
BATCH = 16
NORM_EPS = 1e-6
SUBLN_EPS = 1e-5
LAMBDA_INIT = 0.8 - 0.6 * math.exp(-0.3 * 0)

SEM_CAP = 30000
STRICT_SAME_ENGINE = True
ENGS = ("pe", "act", "dve", "pool", "sp")


class Buf:
    __slots__ = ("name", "lw", "rd")

    def __init__(self, name):
        self.name = name
        self.lw = None
        self.rd = []


class Op:
    __slots__ = ("eng", "fn", "pos", "waits", "marked", "sem", "val", "dma_key", "dma_val", "dma_sem")

    def __init__(self, eng, fn):
        self.eng = eng
        self.fn = fn
        self.pos = -1
        self.waits = []
        self.marked = False
        self.sem = None
        self.val = 0
        self.dma_key = None
        self.dma_val = 0
        self.dma_sem = None


class Sched:
    def __init__(self, nc):
        self.nc = nc
        self.streams = {e: [] for e in ENGS}
        self.seen = {e: {p: -1 for p in ENGS} for e in ENGS}
        self.seen_dma = {e: {} for e in ENGS}
        self.dma_cnt = {}
        self.final_dma = []

    def add(self, eng, fn, reads=(), writes=(), dma_key=None):
        op = Op(eng, fn)
        op.pos = len(self.streams[eng])
        is_dma = dma_key is not None
        if is_dma:
            ep, cnt = self.dma_cnt.get(dma_key, (0, 0))
            if cnt + 16 > SEM_CAP:
                ep, cnt = ep + 1, 0
            cnt += 16
            self.dma_cnt[dma_key] = (ep, cnt)
            op.dma_key = (dma_key, ep)
            op.dma_val = cnt
        deps = {}
        for b in reads:
            if b.lw is not None:
                deps[id(b.lw)] = (b.lw, True)
        for b in writes:
            if b.lw is not None and id(b.lw) not in deps:
                deps[id(b.lw)] = (b.lw, False)
            for r in b.rd:
                if id(r) not in deps:
                    deps[id(r)] = (r, False)
        best = {}
        for p, raw in deps.values():
            if p is op:
                continue
            if p.dma_key is not None:
                k = p.dma_key
                if self.seen_dma[eng].get(k, 0) >= p.dma_val:
                    continue
                self.seen_dma[eng][k] = p.dma_val
                op.waits.append(p)
                continue
            if p.eng == eng and not is_dma:
                if eng == "pe" or (not raw and not STRICT_SAME_ENGINE):
                    continue
            if p.eng not in best or best[p.eng].pos < p.pos:
                best[p.eng] = p
        for pe_, p in best.items():
            if self.seen[eng][pe_] >= p.pos:
                continue
            self.seen[eng][pe_] = p.pos
            p.marked = True
            op.waits.append(p)
        for b in reads:
            b.rd.append(op)
        for b in writes:
            b.lw = op
            b.rd = []
        self.streams[eng].append(op)
        return op

    def barrier(self):
        lasts = {}
        for e in ENGS:
            if e == "sp":
                continue
            for o in reversed(self.streams[e]):
                if o.dma_key is None and o.fn is not None:
                    lasts[e] = o
                    break
        dlast = {}
        for e in ENGS:
            for o in self.streams[e]:
                if o.dma_key is not None:
                    dlast[o.dma_key] = o
        for e in ENGS:
            op = Op(e, None)
            op.pos = len(self.streams[e])
            for pe_, p in lasts.items():
                if pe_ == e or self.seen[e][pe_] >= p.pos:
                    continue
                self.seen[e][pe_] = p.pos
                p.marked = True
                op.waits.append(p)
            for k, p in dlast.items():
                if self.seen_dma[e].get(k, 0) >= p.dma_val:
                    continue
                self.seen_dma[e][k] = p.dma_val
                op.waits.append(p)
            if op.waits:
                self.streams[e].append(op)

    def emit(self, es):
        nc = self.nc
        for e in ENGS:
            cnt = 0
            sem = None
            n = 0
            for o in self.streams[e]:
                if o.marked:
                    if sem is None or cnt >= SEM_CAP:
                        sem = es.enter_context(nc.semaphore(f"s_{e}_{n}"))
                        n += 1
                        cnt = 0
                    cnt += 1
                    o.sem = sem
                    o.val = cnt
        dsem = {}
        for e in ENGS:
            for o in self.streams[e]:
                if o.dma_key is not None:
                    if o.dma_key not in dsem:
                        dsem[o.dma_key] = es.enter_context(nc.semaphore(f"d_{len(dsem)}"))
                    o.dma_sem = dsem[o.dma_key]
        streams = self.streams
        final_dma = self.final_dma

        def run(eng_name, eng):
            for o in streams[eng_name]:
                for p in o.waits:
                    if p.dma_key is not None:
                        eng.wait_ge(p.dma_sem, p.dma_val)
                    else:
                        eng.wait_ge(p.sem, p.val)
                if o.fn is None:
                    continue
                ins = o.fn(eng)
                if o.dma_key is not None:
                    ins.then_inc(o.dma_sem, 16)
                elif o.marked:
                    ins.then_inc(o.sem, 1)
            if eng_name == "sp":
                for o in final_dma:
                    eng.wait_ge(o.dma_sem, o.dma_val)

        block = es.enter_context(nc.Block())

        @block.tensor
        def _(eng):
            run("pe", eng)

        @block.scalar
        def _(eng):
            run("act", eng)

        @block.vector
        def _(eng):
            run("dve", eng)

        @block.gpsimd
        def _(eng):
            run("pool", eng)

        @block.sync
        def _(eng):
            run("sp", eng)


class Arena:
    def __init__(self, ap, nbytes):
        self.ap = ap
        self.nbytes = nbytes
        self.off = 0
        self.peak = 0

    def _take(self, nb):
        nb = (nb + 63) // 64 * 64
        o = self.off
        self.off += nb
        self.peak = max(self.peak, self.off)
        assert self.off <= self.nbytes, f"arena overflow {self.off} > {self.nbytes}"
        return o

    def bf(self, n):
        o = self._take(2 * n)
        return self.ap[:, o // 2:o // 2 + n]

    def f32(self, n):
        o = self._take(4 * n)
        return self.ap[:, o // 2:o // 2 + 2 * n].bitcast(F32)


class KB:
    def __init__(self, S, NB, arena_bytes=212480):
        self.S = S
        self.NB = NB
        self.NT = S // 512
        self.arena_bytes = arena_bytes
        self.nbuf = 0

    def buf(self, name):
        self.nbuf += 1
        return Buf(f"{name}#{self.nbuf}")

    def mm(self, out, lhsT, rhs, start, stop, reads, writes, skip=False):
        self.s.add("pe", lambda e: e.matmul(out, lhsT=lhsT, rhs=rhs, start=start, stop=stop,
                                            skip_group_check=skip), reads, writes)

    def tr(self, out, in_, reads, writes):
        ident = self.IDENT
        self.s.add("pe", lambda e: e.transpose(out, in_, ident), list(reads) + [self.bCM], writes)

    def act(self, out, in_, func, reads, writes, bias=None, scale=1.0, accum=None):
        def fn(e):
            kw = {}
            if bias is not None:
                kw["bias"] = bias
            if accum is not None:
                kw["accum_out"] = accum
            return e.activation(out=out, in_=in_, func=func, scale=scale, **kw)
        self.s.add("act", fn, reads, writes)

    def dve(self, fn, reads, writes):
        self.s.add("dve", fn, reads, writes)

    def pool(self, fn, reads, writes):
        self.s.add("pool", fn, reads, writes)

    def dma(self, q, out, in_, reads, writes, key):
        return self.s.add(q, lambda e: e.dma_start(out=out, in_=in_), reads, writes, dma_key=key)

    def bank(self, i):
        return self.PS[i // 2][:, (i % 2) * 512:(i % 2 + 1) * 512]

    def build(self):
        S, NB = self.S, self.NB
        nc = bass.Bass("TRN2", target_bir_lowering=False)
        self.nc = nc
        dt = nc.dram_tensor
        I = "ExternalInput"
        self.x = dt("x", [NB, S, D], F32, kind=I).ap()
        self.w_in = dt("w_in", [D, INW], F32, kind=I).ap()
        self.wbs = dt("wbs", [512, D], F32, kind=I).ap()
        self.wbd = dt("wbd", [512, D], F32, kind=I).ap()
        self.wo = dt("wo", [D, D], F32, kind=I).ap()
        self.wg = dt("wg", [D, DFF], F32, kind=I).ap()
        self.wu = dt("wu", [D, DFF], F32, kind=I).ap()
        self.wd = dt("wd", [DFF, D], F32, kind=I).ap()
        self.gmix = dt("gmix", [1, D], F32, kind=I).ap()
        self.gffn = dt("gffn", [1, D], F32, kind=I).ap()
        self.gfin = dt("gfin", [1, D], F32, kind=I).ap()
        self.gsub = dt("gsub", [1, 128], F32, kind=I).ap()
        self.lamv = dt("lamv", [4, 64], F32, kind=I).ap()
        self.cmat = dt("cmat", [128, 9 * 128], F32, kind=I).ap()
        self.cosT = dt("cosT", [128, S], F32, kind=I).ap()
        self.sinT = dt("sinT", [128, S], F32, kind=I).ap()
        self.out = dt("out", [NB, S, D], F32, kind="ExternalOutput").ap()
        self.WA = dt("WA", [8, 4, 128, 1024], BF, kind="Internal").ap()
        self.WG = dt("WG", [8, 128, 3072], BF, kind="Internal").ap()
        self.WO = dt("WO", [2, 128, 4096], BF, kind="Internal").ap()
        self.WF = dt("WF", [NFC, 2, 128, 1024], BF, kind="Internal").ap()
        self.WD = dt("WD", [2, 128, NFC * 512], BF, kind="Internal").ap()

        with ExitStack() as es:
            AR = es.enter_context(nc.sbuf_tensor("AR", [128, self.arena_bytes // 2], BF))
            self.A = Arena(AR, self.arena_bytes)
            self.PS = [es.enter_context(nc.psum_tensor(f"PS{i}", [128, 1024], F32)) for i in range(4)]
            self.pb = [self.buf(f"pb{i}") for i in range(8)]
            self.s = Sched(nc)
            self.setup()
            self.s.barrier()
            self.phase_w()
            for b in range(NB):
                self.pass_a(b)
                self.pass_b(b)
                self.pass_c(b)
            self.s.emit(es)
        return nc

    def setup(self):
        A, S = self.A, self.S
        self.CM = A.bf(9 * 128)
        self.bCM = self.buf("CM")
        CM = self.CM
        self.IDENT = CM[:, 0:128]
        self.NEGTRI8 = CM[:, 128:256]
        self.NEGONES8 = CM[:, 256:384]
        self.ONES = CM[:, 384:512]
        self.ONESDIV = CM[:, 512:640]
        self.SBMASK = CM[:, 640:768]
        self.DAMASK = CM[:, 768:896]
        self.PX = CM[:, 896:1024]
        self.PY = CM[:, 1024:1152]
        self.dma("pool", CM, self.cmat, [], [self.bCM], "CM")
        self.SM = A.f32(32)
        self.bSM = self.buf("SM")
        SM = self.SM
        self.M05 = SM[:, 0:1]
        self.GSUB = SM[:, 1:2]
        self.GS = SM[:, 2:3]
        self.NEGLAM = SM[:, 3:4]
        self.bLAMV = self.buf("LAMV")
        self.ST = A.f32(32)
        self.bST = [self.buf(f"ST{i}") for i in range(8)]
        self.stn = 0
        self.OT = A.bf(8 * S).rearrange("p (c s) -> p c s", c=8)
        self.bOT = [self.buf(f"OT{t}") for t in range(self.NT)]
        self.persist_mark = A.off
        self.LAMV = A.f32(4 * 64)
        PR = A.f32(128)
        self.pool(lambda e: e.memset(self.M05, -0.5), [], [self.bSM])
        self.dma("sp", self.GSUB, self.gsub.rearrange("o p -> p o"), [], [self.bSM], "SMg")
        self.dma("sp", self.LAMV, self.lamv.rearrange("a b -> (a b)").partition_broadcast(128),
                 [], [self.bLAMV], "LAMV")
        LV = self.LAMV.rearrange("p (a b) -> p a b", a=4)
        bPR = self.buf("PR")
        LS = SM[:, 8:10]
        EL = SM[:, 10:12]
        self.dve(lambda e: e.tensor_tensor(out=PR[:, 0:64], in0=LV[:, 0, :], in1=LV[:, 1, :], op=ALU.mult),
                 [self.bLAMV], [bPR])
        self.dve(lambda e: e.tensor_tensor(out=PR[:, 64:128], in0=LV[:, 2, :], in1=LV[:, 3, :], op=ALU.mult),
                 [self.bLAMV], [bPR])
        self.dve(lambda e: e.reduce_sum(out=LS, in_=PR.rearrange("p (a b) -> p a b", a=2), axis=AX.X),
                 [bPR], [self.bSM])
        self.act(EL, LS, AF.Exp, [self.bSM], [self.bSM])
        self.dve(lambda e: e.tensor_tensor(out=SM[:, 12:13], in0=EL[:, 1:2], in1=EL[:, 0:1], op=ALU.subtract),
                 [self.bSM], [self.bSM])
        self.dve(lambda e: e.tensor_scalar(out=self.NEGLAM, in0=SM[:, 12:13], scalar1=-LAMBDA_INIT, scalar2=None,
                                           op0=ALU.add), [self.bSM], [self.bSM])
        self.dve(lambda e: e.tensor_scalar(out=self.GS, in0=self.GSUB, scalar1=1.0 - LAMBDA_INIT, scalar2=None,
                                           op0=ALU.mult), [self.bSM], [self.bSM])
        self.LAMCOL = SM[:, 4:5]
        self.pool(lambda e: e.memset(SM[0:64, 4:5], 1.0), [], [self.bSM])
        self.dve(lambda e: e.tensor_copy(out=SM[64:128, 4:5], in_=SM[64:128, 3:4]), [self.bSM], [self.bSM])

    def phase_w(self):
        A = self.A
        A.off = self.persist_mark
        Fs = [A.f32(4096) for _ in range(3)]
        bF = [self.buf(f"F{i}") for i in range(3)]
        Hs = [A.bf(4096) for _ in range(3)]
        bH = [self.buf(f"H{i}") for i in range(3)]
        self.wconv_n = 0
        self.wconv_h = 0
        self.bW = {}

        def nextF():
            i = self.wconv_n % 3
            self.wconv_n += 1
            return Fs[i], bF[i]

        def nextH():
            i = self.wconv_h % 3
            self.wconv_h += 1
            return Hs[i], bH[i]

        def wbuf(key):
            b = self.buf(f"W{key}")
            self.bW[key] = b
            return b

        w_in_v = self.w_in.rearrange("(c p) n -> p c n", p=128)
        for g in range(6):
            F, bf_ = nextF()
            self.dma("sp", F.rearrange("p (c n) -> p c n", c=8), w_in_v[:, :, g * 512:(g + 1) * 512], [], [bf_],
                     ("ld", bf_.name))
            H, bh = nextH()
            self.dve(lambda e, H=H, F=F: e.tensor_copy(out=H.rearrange("p (j c n) -> p j c n", j=4, c=8),
                                                      in_=F.rearrange("p (c j n) -> p j c n", c=8, j=4)),
                     [bf_], [bh])
            self.dma("act", self.WA[g].rearrange("j p m -> p j m"), H.rearrange("p (j m) -> p j m", j=4),
                     [bh], [wbuf(("WA", g))], ("wst", "WA", g))
            if g in (3, 4):
                H2, bh2 = nextH()
                for j in range(4):
                    src = F.rearrange("p (c n) -> p c n", c=8)[:, :, j * 128:(j + 1) * 128] \
                        .rearrange("p c (m h r) -> p c m h r", m=2, h=2)
                    dst = H2[:, j * 1024:(j + 1) * 1024].rearrange("p (c m h r) -> p c m h r", c=8, m=2, h=2)
                    self.dve(lambda e, dst=dst, src=src: e.tensor_scalar(out=dst[:, :, :, 0, :], in0=src[:, :, :, 1, :],
                                                                        scalar1=-1.0, scalar2=None, op0=ALU.mult),
                             [bf_], [bh2])
                    self.dve(lambda e, dst=dst, src=src: e.tensor_copy(out=dst[:, :, :, 1, :], in_=src[:, :, :, 0, :]),
                             [bf_], [bh2])
                gg = 6 if g == 3 else 7
                self.dma("act", self.WA[gg].rearrange("j p m -> p j m"), H2.rearrange("p (j m) -> p j m", j=4),
                         [bh2], [wbuf(("WA", gg))], ("wst", "WA", gg))
        self.wd_groups = [(0, 8), (8, 16), (16, 22)]
        self.bWC = self.buf("WC")
        self.s.barrier()

    def alloc_common(self, nslots, with_xb=True, extra_g=False):
        A = self.A
        if with_xb:
            self.XB = [A.f32(D) for _ in range(2)]
            self.bXB = [self.buf(f"XB{i}") for i in range(2)]
        self.HB = [A.bf(D) for _ in range(2)]
        self.bHB = [self.buf(f"HB{i}") for i in range(2)]
        self.HT = A.bf(8 * 512).rearrange("p (c s) -> p c s", c=8)
        self.bHT = self.buf("HT")
        self.WS = [A.bf(4096) for _ in range(nslots)]
        self.bWS = [self.buf(f"WS{i}") for i in range(nslots)]
        self.wsn = 0
        self.nhb = 0
        self.bG = self.buf("G")
        self.GMIX = A.f32(D)
        gl = [(self.GMIX, self.gmix)]
        if extra_g:
            self.GFFN = A.f32(D)
            self.GFIN = A.f32(D)
            gl += [(self.GFFN, self.gffn), (self.GFIN, self.gfin)]
        for gt, gs in gl:
            self.dma("sp", gt, gs.rearrange("o d -> (o d)").partition_broadcast(128), [], [self.bG], "G")

    def wslot(self):
        i = self.wsn % len(self.WS)
        self.wsn += 1
        return self.WS[i], self.bWS[i]

    def rstd(self, x_ap, xbufs, junk, bjunk):
        g = self.stn % 8
        self.stn += 1
        st = self.ST[:, g * 4:g * 4 + 4]
        bst = self.bST[g]
        if getattr(self, "sq_on_dve", False):
            self.dve(lambda e: e.scalar_tensor_tensor(out=junk, in0=x_ap, scalar=1.0, in1=x_ap, op0=ALU.mult,
                                                      op1=ALU.mult, accum_out=st[:, 0:1]),
                     xbufs, [bjunk, bst])
        else:
            self.act(junk, x_ap, AF.Square, xbufs, [bjunk, bst], accum=st[:, 0:1])
        self.pool(lambda e: e.tensor_scalar(out=st[:, 1:2], in0=st[:, 0:1], scalar1=1.0 / D, scalar2=NORM_EPS,
                                            op0=ALU.mult, op1=ALU.add), [bst], [bst])
        self.pool(lambda e: e.tensor_tensor(out=st[:, 2:3], in0=st[:, 1:2], in1=self.M05, op=ALU.pow),
                  [bst, self.bSM], [bst])
        return st[:, 2:3], bst

    def next_hb(self):
        i = self.nhb % 2
        self.nhb += 1
        return self.HB[i], self.bHB[i]

    def load_wa(self, g):
        ws, bws = self.wslot()
        self.dma("sp", ws.rearrange("p (j m) -> p j m", j=4), self.WA[g].rearrange("j p m -> p j m"),
                 [self.bW[("WA", g)]], [bws], ("ld", bws.name))
        return ws.rearrange("p (j m) -> p j m", j=4), bws

    def proj_fm(self, wsv, bws, c, bank_i):
        bk = self.bank(bank_i)
        for dc in range(8):
            self.mm(bk, wsv[:, c, dc * 128:(dc + 1) * 128], self.HT[:, dc, :], dc == 0, dc == 7,
                    [bws, self.bHT], [self.pb[bank_i]])
        return bk

    def prep_items(self, b, t, specs, bankfn, vspec=None, pre=()):
        items = []
        st = {}

        def LD(tb):
            i = (t * 4 + tb) % 2
            self.dma("sp", self.XB[i], self.x[b, (t * 4 + tb) * 128:(t * 4 + tb + 1) * 128, :], [], [self.bXB[i]],
                     ("ld", self.bXB[i].name))

        def SQ(tb):
            i = (t * 4 + tb) % 2
            hb, bhb = self.next_hb()
            st[("hb", tb)] = (hb, bhb)
            st[("r", tb)] = self.rstd(self.XB[i], [self.bXB[i]], hb, bhb)

        def NRM(tb):
            i = (t * 4 + tb) % 2
            hb, bhb = st[("hb", tb)]
            r, bst = st[("r", tb)]
            xb = self.XB[i]
            gm = self.GMIX
            self.dve(lambda e: e.scalar_tensor_tensor(out=hb, in0=xb, scalar=r, in1=gm, op0=ALU.mult,
                                                      op1=ALU.mult), [self.bXB[i], bst, self.bG], [bhb])

        def TR(tb):
            hb, bhb = st[("hb", tb)]
            bi = bankfn()
            tp = self.bank(bi).bitcast(BF)
            for c in range(8):
                self.tr(tp[:, c * 128:(c + 1) * 128], hb[:, c * 128:(c + 1) * 128], [bhb], [self.pb[bi]])
            ht = self.HT
            self.dve(lambda e: e.tensor_copy(out=ht[:, :, tb * 128:(tb + 1) * 128],
                                             in_=tp.rearrange("p (c n) -> p c n", c=8)), [self.pb[bi]], [self.bHT])

        items += list(pre)
        items += [lambda: LD(0), lambda: LD(1), lambda: SQ(0), lambda: SQ(1), lambda: NRM(0), lambda: LD(2),
                  lambda: TR(0), lambda: NRM(1), lambda: LD(3), lambda: SQ(2), lambda: TR(1), lambda: NRM(2),
                  lambda: SQ(3), lambda: TR(2), lambda: NRM(3), lambda: TR(3)]
        for (groups, evac_fn) in specs:
            def LW(groups=groups):
                st[("w", tuple(groups))] = [self.load_wa(g) for g in groups]
            items.append(LW)
            for c in range(4):
                for gi in range(len(groups)):
                    def PJ(c=c, gi=gi, groups=groups, evac_fn=evac_fn):
                        wsv, bws = st[("w", tuple(groups))][gi]
                        bi = bankfn()
                        self.proj_fm(wsv, bws, c, bi)
                        evac_fn(c, gi, bi)
                    items.append(PJ)
        if vspec is not None:
            g, Vt, bV = vspec

            def LWV():
                st["wv"] = self.load_wa(g)
            items.append(LWV)
            for tb in range(4):
                def PV(tb=tb):
                    wsv, bws = st["wv"]
                    bi = bankfn()
                    bk = self.bank(bi)
                    bk4 = bk.rearrange("p (j n) -> p j n", j=4)
                    for dc in range(8):
                        self.mm(bk4, self.HT[:, dc, tb * 128:(tb + 1) * 128], wsv[:, :, dc * 128:(dc + 1) * 128],
                                dc == 0, dc == 7, [bws, self.bHT], [self.pb[bi]])
                    vdst = Vt[:, t * 4 + tb, :]
                    self.dve(lambda e: e.tensor_copy(out=vdst, in_=bk), [self.pb[bi]], [bV])
                items.append(PV)
        return items

    def wc_pieces(self):
        A = self.A
        F = A.f32(1024)
        H = A.bf(1024)
        bF, bH = self.buf("WCF"), self.buf("WCH")
        out = []

        def piece(src_ap, fview, conv, dst_ap, hview):
            def ld():
                self.dma("sp", fview(F), src_ap, [], [bF], ("ld", "WCF"))

            def cv():
                self.dve(conv, [bF], [bH])

            def st():
                self.dma("sp", dst_ap, hview(H), [bH], [self.bWC], ("wstc",))
            out.extend([ld, None, cv, st])

        straight = lambda e: e.tensor_copy(out=H, in_=F)
        w_in_v = self.w_in.rearrange("(c p) n -> p c n", p=128)
        v8 = lambda T: T.rearrange("p (c n) -> p c n", c=8)
        ident_h = lambda T: T
        for j in range(8):
            for k in range(2):
                col0 = 3072 + 1024 * k + 128 * j
                piece(w_in_v[:, :, col0:col0 + 128], v8, straight, self.WG[j][:, 1024 * k:1024 * (k + 1)], ident_h)
        for k, wsrc in enumerate((self.wbs, self.wbd)):
            wv = wsrc.rearrange("(c p) n -> p c n", p=128)
            for q in range(4):
                conv = lambda e: e.tensor_copy(out=H.rearrange("p (j c n) -> p j c n", j=2, c=4),
                                               in_=F.rearrange("p (c j n) -> p j c n", c=4, j=2))
                off = 2048 + 512 * k
                piece(wv[:, :, 256 * q:256 * (q + 1)], lambda T: T.rearrange("p (c n) -> p c n", c=4), conv,
                      self.WG[2 * q:2 * q + 2, :, off:off + 512].rearrange("j p m -> p j m"),
                      lambda T: T.rearrange("p (j m) -> p j m", j=2))
        wo_v = self.wo.rearrange("(c p) n -> p c n", p=128)
        v2 = lambda T: T.rearrange("p (c n) -> p c n", c=2)
        for ch in range(2):
            for q in range(4):
                piece(wo_v[:, 2 * q:2 * q + 2, ch * 512:(ch + 1) * 512], v2, straight,
                      self.WO[ch][:, 2 * q * 512:(2 * q + 2) * 512], ident_h)
        for f in range(NFC):
            for k, wsrc in enumerate((self.wg, self.wu)):
                wv = wsrc.rearrange("(c p) n -> p c n", p=128)
                piece(wv[:, :, 128 * f:128 * (f + 1)], v8, straight, self.WF[f, k], ident_h)
        wd_v = self.wd.rearrange("(f p) n -> p f n", p=128)
        for ch in range(2):
            for f in range(0, NFC, 2):
                piece(wd_v[:, f:f + 2, ch * 512:(ch + 1) * 512], v2, straight, self.WD[ch][:, f * 512:(f + 2) * 512],
                      ident_h)
        return out

    def pass_a(self, b):
        A, S, NT = self.A, self.S, self.NT
        self.sq_on_dve = True
        A.off = self.persist_mark
        KT = A.bf(4 * S).rearrange("p (c s) -> p c s", c=4)
        Vt = A.bf(S * 4).rearrange("p (k n) -> p k n", n=512)
        bKT = [self.buf(f"KT{t}") for t in range(NT)]
        bV = [self.buf(f"V{t}") for t in range(NT)]
        self.alloc_common(2)
        QTs = [A.bf(4 * 512).rearrange("p (c s) -> p c s", c=4) for _ in range(2)]
        bQTs = [self.buf(f"QT{i}") for i in range(2)]
        Es = [A.f32(1024).rearrange("p (h s) -> p h s", h=2) for _ in range(2)]
        bEs = [self.buf(f"E{i}") for i in range(2)]
        SP = [A.bf(1024).rearrange("p (h s) -> p h s", h=2) for _ in range(2)]
        bSP = [self.buf(f"SP{i}") for i in range(2)]
        WT = [A.bf(1024).rearrange("p (h s) -> p h s", h=2) for _ in range(2)]
        bWT = [self.buf(f"WT{i}") for i in range(2)]
        R = [A.bf(1024).rearrange("p (h s) -> p h s", h=2) for _ in range(2)]
        bR = [self.buf(f"R{i}") for i in range(2)]
        wcp = self.wc_pieces() if b == 0 else []
        wcp_pos = 0
        fg = {"n": 0}

        def fg_bank():
            fg["n"] += 1
            return 4 + fg["n"] % 4

        def mk_items(t, bankfn):
            qt, bqt = QTs[t % 2], bQTs[t % 2]

            def evq(c, gi, bi):
                bk = self.bank(bi)
                self.dve(lambda e: e.tensor_copy(out=qt[:, c, :], in_=bk), [self.pb[bi]], [bqt])

            def evk(c, gi, bi):
                bk = self.bank(bi)
                dst = KT[:, c, t * 512:(t + 1) * 512]
                self.dve(lambda e: e.tensor_copy(out=dst, in_=bk), [self.pb[bi]], [bKT[t]])

            return self.prep_items(b, t, [([0], evq), ([1], evk)], bankfn, (2, Vt, bV[t]))

        for it in mk_items(0, fg_bank):
            it()
        for t in range(NT):
            QT, bQT = QTs[t % 2], bQTs[t % 2]
            bg = mk_items(t + 1, lambda: 7) if t + 1 < NT else []
            bg_pos = 0
            nkb = 4 * t + 4
            steps = [(c, kb) for c in range(4) for kb in range(nkb - 1, -1, -1)]
            n = len(steps)

            def info(i):
                c, kb = steps[i]
                j = kb - 4 * t
                c0 = j * 128 if j >= 0 else 0
                return c, kb, j >= 0, c0, kb == nkb - 1, kb == 0

            def zzv(i):
                return self.PS[i % 3].rearrange("p (h s) -> p h s", h=2)

            def zzb(i):
                return [self.pb[2 * (i % 3)], self.pb[2 * (i % 3) + 1]]

            def S_(i):
                c, kb, dg, c0, first, last = info(i)
                zz = zzv(i)
                for h in range(2):
                    self.mm(zz[:, h, c0:512], KT[64 * h:64 * h + 64, c, kb * 128:(kb + 1) * 128],
                            QT[64 * h:64 * h + 64, c, c0:512], True, True, [bKT[kb // 4], bQT],
                            [self.pb[2 * (i % 3) + h]])
                if dg:
                    for h in range(2):
                        self.mm(zz[:, h, c0:c0 + 128], self.IDENT, self.SBMASK, False, True, [self.bCM],
                                [self.pb[2 * (i % 3) + h]], skip=True)

            def zero_left(buf_ap, bbuf, c0):
                if c0 > 0:
                    self.pool(lambda e: e.memset(buf_ap[:, :, 0:c0], 0.0), [], [bbuf])

            S_(0)
            for i in range(n + 2):
                if i < n:
                    c, kb, dg, c0, first, last = info(i)
                    zz = zzv(i)
                    E, bE = Es[i % 2], bEs[i % 2]
                    self.act(E[:, :, c0:512], zz[:, :, c0:512], AF.Exp, zzb(i), [bE], scale=0.125)
                if 2 <= i:
                    pc, pkb, pdg, pc0, pfirst, plast = info(i - 2)
                    pzz = zzv(i - 2)
                    wt, bwt = WT[(i - 2) % 2], bWT[(i - 2) % 2]
                    if pdg:
                        zero_left(wt, bwt, pc0)
                    self.act(wt[:, :, pc0:512], pzz[:, :, pc0:512], AF.Exp, zzb(i - 2), [bwt], scale=0.125)
                if i < n:
                    sp, bsp = SP[i % 2], bSP[i % 2]
                    if dg:
                        zero_left(sp, bsp, c0)
                    self.act(sp[:, :, c0:512], E[:, :, c0:512], AF.Ln, [bE], [bsp], bias=1.0)
                if i + 1 < n:
                    S_(i + 1)
                if 2 <= i:
                    ob_i = 6
                    ob = self.bank(ob_i)
                    for h in range(2):
                        self.mm(ob[64 * h:64 * h + 64, :], Vt[:, pkb, pc * 128 + 64 * h:pc * 128 + 64 * h + 64],
                                wt[:, h, :], pfirst, plast, [bV[pkb // 4], bwt], [self.pb[ob_i]])
                    if plast:
                        odst = self.OT[:, pc, t * 512:(t + 1) * 512]
                        self.dve(lambda e, ob=ob, odst=odst: e.tensor_copy(out=odst, in_=ob),
                                 [self.pb[ob_i]], [self.bOT[t]])
                if i < n:
                    r, br = R[c % 2], bR[c % 2]
                    for h in range(2):
                        self.mm(zz[:, h, c0:512], self.NEGTRI8, sp[:, h, c0:512], False, True,
                                [bsp, self.bCM], [self.pb[2 * (i % 3) + h]], skip=True)
                        if not first:
                            self.mm(zz[:, h, c0:512], self.NEGONES8, r[:, h, c0:512], False, True,
                                    [br, self.bCM], [self.pb[2 * (i % 3) + h]], skip=True)
                    if not last:
                        if first:
                            self.dve(lambda e, r=r, sp=sp: e.tensor_copy(out=r, in_=sp), [bsp], [br])
                        else:
                            self.dve(lambda e, r=r, sp=sp: e.tensor_tensor(out=r, in0=r, in1=sp, op=ALU.add),
                                     [bsp, br], [br])
                if wcp_pos < len(wcp):
                    if wcp[wcp_pos] is not None:
                        wcp[wcp_pos]()
                    wcp_pos += 1
                if bg and i >= 1:
                    per = -(-len(bg) // max(1, n - 3))
                    for _ in range(per):
                        if bg_pos < len(bg):
                            bg[bg_pos]()
                            bg_pos += 1
            while bg_pos < len(bg):
                bg[bg_pos]()
                bg_pos += 1
        while wcp_pos < len(wcp):
            if wcp[wcp_pos] is not None:
                wcp[wcp_pos]()
            wcp_pos += 1
        self.s.barrier()

    def pass_b(self, b):
        A, S, NT = self.A, self.S, self.NT
        self.sq_on_dve = True
        A.off = self.persist_mark
        KT = A.bf(4 * S).rearrange("p (c s) -> p c s", c=4)
        Vt = A.bf(S * 4).rearrange("p (k n) -> p k n", n=512)
        bKT = [self.buf(f"KTd{t}") for t in range(NT)]
        bV = [self.buf(f"Vd{t}") for t in range(NT)]
        self.alloc_common(3)
        QTs = [A.bf(4 * 512).rearrange("p (c s) -> p c s", c=4) for _ in range(2)]
        bQTs = [self.buf(f"QTd{i}") for i in range(2)]
        COS = A.f32(512)
        SIN = A.f32(512)
        bCOS = self.buf("COS")
        bSIN = self.buf("SIN")
        ET = [A.bf(1024).rearrange("p (h s) -> p h s", h=2) for _ in range(2)]
        bET = [self.buf(f"ET{i}") for i in range(2)]
        RD, XS, YS = [A.f32(512) for _ in range(3)]
        bN = self.buf("NRM")
        bRP = self.buf("ROPE")
        RT = A.f32(512)
        SQ = YS.bitcast(BF)[:, 0:512]
        XL, YL = A.bf(512), A.bf(512)

        def hi(ap):
            return ap.bitcast(BF).rearrange("p (n two) -> p n two", two=2)[:, :, 1]
        fg = {"n": 0}

        def fg_bank():
            fg["n"] += 1
            return 4 + fg["n"] % 4

        def mk_items(t, bankfn):
            qt, bqt = QTs[t % 2], bQTs[t % 2]

            def ev(dst, dbuf):
                def f(c, gi, bi):
                    bk = self.bank(bi)
                    if gi == 0:
                        self.dve(lambda e: e.tensor_tensor(out=RT, in0=bk, in1=COS, op=ALU.mult),
                                 [self.pb[bi], bCOS], [bRP])
                    else:
                        d = dst(c)
                        self.dve(lambda e: e.tensor_tensor(out=bk, in0=bk, in1=SIN, op=ALU.mult),
                                 [self.pb[bi], bSIN], [self.pb[bi]])
                        self.dve(lambda e: e.tensor_tensor(out=d, in0=bk, in1=RT, op=ALU.add),
                                 [self.pb[bi], bRP], [dbuf])
                return f

            def ldcs():
                self.dma("sp", COS, self.cosT[:, t * 512:(t + 1) * 512], [], [bCOS], ("ld", "COS"))
                self.dma("sp", SIN, self.sinT[:, t * 512:(t + 1) * 512], [], [bSIN], ("ld", "SIN"))

            return self.prep_items(b, t, [([3, 6], ev(lambda c: qt[:, c, :], bqt)),
                                         ([4, 7], ev(lambda c: KT[:, c, t * 512:(t + 1) * 512], bKT[t]))],
                                   bankfn, (5, Vt, bV[t]), pre=[ldcs])

        deferred = []
        gstep = [0]
        for it in mk_items(0, fg_bank):
            it()
        for t in range(NT):
            QT, bQT = QTs[t % 2], bQTs[t % 2]
            bg = mk_items(t + 1, lambda: 7) if t + 1 < NT else []
            bg_pos = 0
            nkb = 4 * t + 4
            steps = [(hd, kb) for hd in range(4) for kb in range(nkb)]
            n = len(steps)

            def info(i):
                hd, kb = steps[i]
                j = kb - 4 * t
                c0 = j * 128 if j >= 0 else 0
                return hd, kb, j >= 0, c0, kb == 0, kb == nkb - 1

            def S_(i):
                hd, kb, dg, c0, first, last = info(i)
                zz = self.PS[i % 2].rearrange("p (h s) -> p h s", h=2)
                for h in range(2):
                    self.mm(zz[:, h, c0:512], KT[64 * h:64 * h + 64, hd, kb * 128:(kb + 1) * 128],
                            QT[64 * h:64 * h + 64, hd, c0:512], True, True, [bKT[kb // 4], bQT],
                            [self.pb[2 * (i % 2) + h]])
                if dg:
                    for h in range(2):
                        self.mm(zz[:, h, c0:c0 + 128], self.IDENT, self.DAMASK, False, True, [self.bCM],
                                [self.pb[2 * (i % 2) + h]], skip=True)

            S_(0)
            for i in range(n):
                gstep[0] += 1
                hd, kb, dg, c0, first, last = info(i)
                zz = self.PS[i % 2].rearrange("p (h s) -> p h s", h=2)
                et, bet = ET[i % 2], bET[i % 2]
                self.act(et[:, :, c0:512], zz[:, :, c0:512], AF.Exp, [self.pb[2 * (i % 2)], self.pb[2 * (i % 2) + 1]],
                         [bet], scale=0.125)
                if i + 1 < n:
                    S_(i + 1)
                for half, bi in ((0, 4), (1, 5)):
                    vv = Vt[:, kb, hd * 128 + 64 * half:hd * 128 + 64 * half + 64]
                    for h in range(2):
                        self.mm(self.bank(bi)[64 * h:64 * h + 64, c0:512], vv, et[:, h, c0:512], first, dg,
                                [bV[kb // 4], bet], [self.pb[bi]], skip=(c0 > 0))
                for h in range(2):
                    self.mm(self.bank(6)[64 * h:64 * h + 64, c0:512], self.ONES[:, 0:64], et[:, h, c0:512], first, dg,
                            [self.bCM, bet], [self.pb[6]], skip=(c0 > 0))
                if last:
                    self.act(RD, self.bank(6), AF.Ln, [self.pb[6]], [bN])
                    self.dve(lambda e: e.tensor_copy(out=XS, in_=self.bank(4)), [self.pb[4]], [bN])
                    self.act(YS, self.bank(5), AF.Copy, [self.pb[5]], [bN])

                    def stage1():
                        self.act(RD, RD, AF.Exp, [bN], [bN], scale=-1.0)
                        for T_, L_ in ((XS, XL), (YS, YL)):
                            self.dve(lambda e, T_=T_: e.scalar_tensor_tensor(out=T_, in0=T_, scalar=self.LAMCOL, in1=RD,
                                                                             op0=ALU.mult, op1=ALU.mult),
                                     [bN, self.bSM], [bN])
                            self.dve(lambda e, T_=T_, L_=L_: e.tensor_tensor(out=L_, in0=T_, in1=hi(T_), op=ALU.subtract),
                                     [bN], [bN])

                    def stage2():
                        self.mm(self.bank(7), self.PX, hi(XS), True, False, [bN, self.bCM], [self.pb[7]])
                        self.mm(self.bank(7), self.PY, hi(YS), False, False, [bN, self.bCM], [self.pb[7]])
                        self.mm(self.bank(7), self.PX, XL, False, False, [bN, self.bCM], [self.pb[7]])
                        self.mm(self.bank(7), self.PY, YL, False, True, [bN, self.bCM], [self.pb[7]])
                        self.act(SQ, self.bank(7), AF.Square, [self.pb[7]], [bN])
                        self.dve(lambda e: e.tensor_copy(out=XS, in_=self.bank(7)), [self.pb[7]], [bN])

                    def stage3(hd=hd, t=t):
                        self.mm(self.bank(7), self.ONESDIV, SQ, True, True, [bN, self.bCM], [self.pb[7]])
                        self.act(RD, self.bank(7), AF.Ln, [self.pb[7]], [bN], bias=SUBLN_EPS)
                        self.act(RD, RD, AF.Exp, [bN], [bN], scale=-0.5)
                        odst = self.OT[:, 4 + hd, t * 512:(t + 1) * 512]
                        self.dve(lambda e, odst=odst: e.scalar_tensor_tensor(
                            out=odst, in0=XS, scalar=self.GS, in1=RD,
                            op0=ALU.mult, op1=ALU.mult), [bN, self.bSM], [self.bOT[t]])

                    dd = [min(d, nkb - 1) for d in (1, 5, 8)]
                    g0 = gstep[0]
                    deferred.extend([(g0 + dd[0], stage1), (g0 + dd[1], stage2), (g0 + dd[2], stage3)])
                    deferred.sort(key=lambda x: x[0])
                while deferred and deferred[0][0] <= gstep[0]:
                    deferred.pop(0)[1]()
                if bg and i >= 1:
                    per = -(-len(bg) // max(1, n - 3))
                    for _ in range(per):
                        if bg_pos < len(bg):
                            bg[bg_pos]()
                            bg_pos += 1
            while bg_pos < len(bg):
                bg[bg_pos]()
                bg_pos += 1
        while deferred:
            deferred.pop(0)[1]()
        self.s.barrier()

    def norm_parts(self, x_ap, xbufs, gtile, HTdst, bHTdst, tb, bank_i):
        st = {}

        def sq():
            st["hb"] = self.next_hb()
            st["r"] = self.rstd(x_ap, xbufs, *st["hb"])

        def nrm():
            hb, bhb = st["hb"]
            r, bst = st["r"]
            self.dve(lambda e: e.scalar_tensor_tensor(out=hb, in0=x_ap, scalar=r, in1=gtile, op0=ALU.mult, op1=ALU.mult),
                     list(xbufs) + [bst, self.bG], [bhb])

        def tr():
            hb, bhb = st["hb"]
            tp = self.bank(bank_i).bitcast(BF)
            for c in range(8):
                self.tr(tp[:, c * 128:(c + 1) * 128], hb[:, c * 128:(c + 1) * 128], [bhb], [self.pb[bank_i]])
            self.dve(lambda e: e.tensor_copy(out=HTdst[:, :, tb * 128:(tb + 1) * 128],
                                             in_=tp.rearrange("p (c n) -> p c n", c=8)), [self.pb[bank_i]], [bHTdst])
        return sq, nrm, tr

    def pass_c(self, b):
        A, S, NT = self.A, self.S, self.NT
        self.sq_on_dve = False
        A.off = self.persist_mark
        self.alloc_common(4, with_xb=False, extra_g=True)
        GM, GF, GN = self.GMIX, self.GFFN, self.GFIN
        HTf, bHTf = self.HT, self.bHT
        HTm = A.bf(8 * 512).rearrange("p (c s) -> p c s", c=8)
        bHTm = self.buf("HTm")
        XTs = [A.f32(4 * D).rearrange("p (k d) -> p k d", k=4) for _ in range(2)]
        bXTs = [[self.buf(f"XT{i}_{k}") for k in range(4)] for i in range(2)]
        MIX = A.bf(8 * 512).rearrange("p (c s) -> p c s", c=8)
        bMIX = self.buf("MIX")
        ACTT = A.bf(NFC * 512).rearrange("p (f s) -> p f s", f=NFC)
        bACT = self.buf("ACTT")
        S1, S2, M1, M2 = [A.f32(512) for _ in range(4)]
        bSG = self.buf("SG")
        SGL = [A.f32(512) for _ in range(2)]
        bSGL = [self.buf(f"SGL{i}") for i in range(2)]

        def mix_items(t):
            XT, bX = XTs[t % 2], bXTs[t % 2]
            items = []
            for tb in range(4):
                def LD(tb=tb):
                    self.dma("pool", XT[:, tb, :], self.x[b, t * 512 + tb * 128:t * 512 + (tb + 1) * 128, :], [],
                             [bX[tb]], ("ld", f"XT{t % 2}_{tb}"))
                items.append(LD)
            parts = [self.norm_parts(XT[:, tb, :], [bX[tb]], GM, HTm, bHTm, tb, 6 + (tb % 2)) for tb in range(4)]
            sq, nrm, tr = zip(*parts)
            items += [sq[0], sq[1], nrm[0], tr[0], nrm[1], sq[2], tr[1], nrm[2], sq[3], tr[2], nrm[3], tr[3]]
            return items

        for it in mix_items(0):
            it()
        for t in range(NT):
            XT, bXTl = XTs[t % 2], bXTs[t % 2]
            tok = slice(t * 512, (t + 1) * 512)
            for j in range(8):
                ws, bws = self.wslot()
                self.dma("sp", ws[:, 0:3072], self.WG[j], [self.bWC], [bws], ("ld", bws.name))
                b0 = 4 * (j % 2)
                for k in range(2):
                    gb, bbk = b0 + k, b0 + 2 + k
                    for dc in range(8):
                        self.mm(self.bank(gb), ws[:, k * 1024 + dc * 128:k * 1024 + (dc + 1) * 128], HTm[:, dc, :],
                                dc == 0, dc == 7, [bws, bHTm], [self.pb[gb]])
                    for c in range(4):
                        self.mm(self.bank(bbk), ws[:, 2048 + k * 512 + c * 128:2048 + k * 512 + (c + 1) * 128],
                                self.OT[:, 4 * k + c, tok], c == 0, c == 3, [bws, self.bOT[t]], [self.pb[bbk]])
                g1, g2, p1, p2 = [self.bank(b0 + q) for q in range(4)]
                self.act(S1, g1, AF.Sigmoid, [self.pb[b0]], [bSG])
                self.act(S2, g2, AF.Sigmoid, [self.pb[b0 + 1]], [bSG])
                self.dve(lambda e, p1=p1: e.tensor_tensor(out=M1, in0=p1, in1=S1, op=ALU.mult), [self.pb[b0 + 2], bSG], [bSG])
                self.dve(lambda e, p2=p2: e.tensor_tensor(out=M2, in0=p2, in1=S2, op=ALU.mult), [self.pb[b0 + 3], bSG], [bSG])
                self.dve(lambda e, j=j: e.tensor_tensor(out=MIX[:, j, :], in0=M1, in1=M2, op=ALU.add), [bSG], [bMIX])
            wo = []
            for ch in range(2):
                ws, bws = self.wslot()
                self.dma("sp", ws, self.WO[ch], [self.bWC], [bws], ("ld", bws.name))
                wo.append((ws.rearrange("p (c n) -> p c n", c=8), bws))
            fparts = [self.norm_parts(XT[:, tb, :], [bXTl[tb]], GF, HTf, bHTf, tb, 6 + (tb % 2)) for tb in range(4)]
            for tb in range(4):
                for ch in range(2):
                    bi = (2 * tb + ch) % 4
                    wv, bws = wo[ch]
                    for c in range(8):
                        self.mm(self.bank(bi), MIX[:, c, tb * 128:(tb + 1) * 128], wv[:, c, :], c == 0, c == 7,
                                [bws, bMIX], [self.pb[bi]])
                    xs = XT[:, tb, ch * 512:(ch + 1) * 512]
                    bk = self.bank(bi)
                    self.dve(lambda e, xs=xs, bk=bk: e.tensor_tensor(out=xs, in0=bk, in1=xs, op=ALU.add),
                             [self.pb[bi], bXTl[tb]], [bXTl[tb]])
                fparts[tb][0]()
                fparts[tb][1]()
                if tb >= 1:
                    fparts[tb - 1][2]()
            fparts[3][2]()
            bg = mix_items(t + 1) if t + 1 < NT else []
            bg_pos = 0
            for fp in range(NFC // 2):
                ws, bws = self.wslot()
                self.dma("sp", ws.rearrange("p (a m) -> p a m", a=4),
                         self.WF[2 * fp:2 * fp + 2].rearrange("f k p m -> p (f k) m"),
                         [self.bWC], [bws], ("ld", bws.name))
                wv = ws.rearrange("p (a m) -> p a m", a=4)
                for i2 in range(2):
                    f = 2 * fp + i2
                    gb, ub = 2 * (f % 2), 2 * (f % 2) + 1
                    for dc in range(8):
                        self.mm(self.bank(gb), wv[:, 2 * i2, dc * 128:(dc + 1) * 128], HTf[:, dc, :], dc == 0, dc == 7,
                                [bws, bHTf], [self.pb[gb]])
                    for dc in range(8):
                        self.mm(self.bank(ub), wv[:, 2 * i2 + 1, dc * 128:(dc + 1) * 128], HTf[:, dc, :], dc == 0,
                                dc == 7, [bws, bHTf], [self.pb[ub]])
                    sg, bsg = SGL[f % 2], bSGL[f % 2]
                    gbk, ubk = self.bank(gb), self.bank(ub)
                    adst = ACTT[:, f, :]
                    self.act(sg, gbk, AF.Silu, [self.pb[gb]], [bsg])
                    self.dve(lambda e, sg=sg, ubk=ubk, adst=adst: e.tensor_tensor(out=adst, in0=ubk, in1=sg, op=ALU.mult),
                             [self.pb[ub], bsg], [bACT])
                    if f >= 2 and bg_pos < len(bg):
                        bg[bg_pos]()
                        bg_pos += 1
            while bg_pos < len(bg):
                bg[bg_pos]()
                bg_pos += 1
            for ch in range(2):
                for gi, (f0, f1) in enumerate(self.wd_groups):
                    nf = f1 - f0
                    ws, bws = self.wslot()
                    self.dma("sp", ws[:, 0:nf * 512], self.WD[ch][:, f0 * 512:f1 * 512], [self.bWC], [bws],
                             ("ld", bws.name))
                    wv = ws[:, 0:nf * 512].rearrange("p (f n) -> p f n", f=nf)
                    for tb in range(4):
                        bi = 4 + tb
                        for f in range(f0, f1):
                            self.mm(self.bank(bi), ACTT[:, f, tb * 128:(tb + 1) * 128], wv[:, f - f0, :], f == 0,
                                    f == NFC - 1, [bws, bACT], [self.pb[bi]])
                for tb in range(4):
                    bi = 4 + tb
                    xs = XT[:, tb, ch * 512:(ch + 1) * 512]
                    bk = self.bank(bi)
                    self.dve(lambda e, xs=xs, bk=bk: e.tensor_tensor(out=xs, in0=bk, in1=xs, op=ALU.add),
                             [self.pb[bi], bXTl[tb]], [bXTl[tb]])
            for tb in range(4):
                jk, bjk = self.next_hb()
                xr = XT[:, tb, :]
                r, bst = self.rstd(xr, [bXTl[tb]], jk, bjk)
                self.dve(lambda e, xr=xr, r=r: e.scalar_tensor_tensor(out=xr, in0=xr, scalar=r, in1=GN, op0=ALU.mult,
                                                                      op1=ALU.mult),
                         [bXTl[tb], bst, self.bG], [bXTl[tb]])
                st = self.dma("pool", self.out[b, t * 512 + tb * 128:t * 512 + (tb + 1) * 128, :], xr,
                              [bXTl[tb]], [], ("st", f"XT{t % 2}_{tb}"))
                self.s.final_dma.append(st)
        self.s.barrier()


def host_consts(S):
    jj, kk = np.meshgrid(np.arange(128), np.arange(128), indexing="ij")
    ident = np.eye(128, dtype=np.float32)
    negtri8 = np.where(jj >= kk, -8.0, 0.0).astype(np.float32)
    negones8 = np.full((128, 128), -8.0, np.float32)
    ones = np.ones((128, 128), np.float32)
    onesdiv = np.full((128, 128), 1.0 / 128.0, np.float32)
    sbmask = np.where(jj < kk, 0.0, -30000.0).astype(np.float32)
    damask = np.where((jj < 64) | (kk >= 64), 0.0, -30000.0).astype(np.float32)
    p1 = ((jj == kk) & (kk < 64)).astype(np.float32)
    p2 = ((jj == kk + 64) & (kk < 64)).astype(np.float32)
    p3 = ((jj == kk - 64) & (kk >= 64)).astype(np.float32)
    p4 = ((jj == kk) & (kk >= 64)).astype(np.float32)
    cmat = np.concatenate([ident, negtri8, negones8, ones, onesdiv, sbmask, damask, p1 + p2, p3 + p4], axis=1)
    inv_freq = (1.0 / (np.float32(10000.0) ** (np.arange(0, 64, 2, dtype=np.float32) / np.float32(64)))).astype(np.float32)
    ang = np.arange(S, dtype=np.float32)[:, None] * inv_freq[None, :]
    cos = np.cos(ang).astype(np.float32).T
    sin = np.sin(ang).astype(np.float32).T
    cosT = np.ascontiguousarray(np.tile(cos, (4, 1)))
    sinT = np.ascontiguousarray(np.tile(sin, (4, 1)))
    return cmat, cosT, sinT


_NC_CACHE = {}


def make_in_maps(inputs, S, NB, ncores):
    cmat, cosT, sinT = host_consts(S)
    f = lambda a: np.ascontiguousarray(np.asarray(a, dtype=np.float32))
    x = f(inputs["x"])
    shared = {
        "w_in": f(inputs["w_in"][0]),
        "wbs": f(inputs["w_branch_sb"][0]),
        "wbd": f(inputs["w_branch_da"][0]),
        "wo": f(inputs["w_out"][0]),
        "wg": f(inputs["w_ffn_gate"][0]),
        "wu": f(inputs["w_ffn_up"][0]),
        "wd": f(inputs["w_ffn_down"][0]),
        "gmix": f(inputs["g_mix"]).reshape(1, D),
        "gffn": f(inputs["g_ffn"]).reshape(1, D),
        "gfin": f(inputs["g_final"]).reshape(1, D),
        "gsub": f(inputs["g_subln"]).reshape(1, 128),
        "lamv": np.ascontiguousarray(np.stack([f(inputs["lambda_q1"]).reshape(64), f(inputs["lambda_k1"]).reshape(64),
                                               f(inputs["lambda_q2"]).reshape(64), f(inputs["lambda_k2"]).reshape(64)])),
        "cmat": cmat, "cosT": cosT, "sinT": sinT,
    }
    maps = []
    for i in range(ncores):
        m = dict(shared)
        m["x"] = np.ascontiguousarray(x[i * NB:(i + 1) * NB])
        maps.append(m)
    return maps


def kernel(**inputs):
    x = np.asarray(inputs["x"])
    B, S, _ = x.shape
    NB = B // NCORES
    key = (S, NB)
    if key not in _NC_CACHE:
        _NC_CACHE[key] = KB(S, NB).build()
    nc = _NC_CACHE[key]
    in_maps = make_in_maps(inputs, S, NB, NCORES)
    res = run_bass_kernel_spmd(nc, in_maps, core_ids=list(range(NCORES)))
    return np.concatenate([np.asarray(r["out"]) for r in res.results], axis=0).astype(np.float32)
```
